# Optimizing a Trainium2 kernel written in Bass

```python
import math
import jax, jax.numpy as jnp
from jax import lax
import numpy as np

D_MODEL = 1024
BATCH = 16
SEQ = 2048
DEPTH = 2

N_MIXERS = 2
HEAD_DIM = 64
MEM_LEN = 256
MEM_HEADS = 4
MEM_W = MEM_HEADS * HEAD_DIM
TOK_W = D_MODEL - MEM_W
CHUNK = 128
GMLP_GROUP_DIM = 128
GMLP_GROUPS = TOK_W // GMLP_GROUP_DIM
DIFF_HEADS = TOK_W // (2 * HEAD_DIM)
Q_BLOCK = 128
D_FF = ((8 * D_MODEL // 3) + 127) // 128 * 128
EPS = 1e-6

kernel_name = "hybrid_gmlp_diffattn_macaron_memxattn"


def rms_norm(x, g):
    xf = x.astype(jnp.float32)
    y = xf * lax.rsqrt(jnp.mean(xf * xf, axis=-1, keepdims=True) + EPS)
    return (y * g.astype(jnp.float32)).astype(x.dtype)


def swiglu(h, w_in, w_out):
    gate, up = jnp.split(h @ w_in, 2, axis=-1)
    return (jax.nn.silu(gate) * up) @ w_out


def chunked_gmlp(z, v_gain, w_s, b_s):
    B, S, _ = z.shape
    u, v = jnp.split(jax.nn.gelu(z, approximate=False), 2, axis=-1)
    v = rms_norm(v.reshape(B, S, GMLP_GROUPS, GMLP_GROUP_DIM), v_gain.reshape(GMLP_GROUPS, GMLP_GROUP_DIM))
    v = v.reshape(B, S // CHUNK, CHUNK, GMLP_GROUPS, GMLP_GROUP_DIM)
    ws = w_s * jnp.tril(jnp.ones((CHUNK, CHUNK), dtype=w_s.dtype))
    mixed = jnp.einsum('gts,bnsgc->bntgc', ws, v) + b_s.T[:, :, None]
    return u * mixed.reshape(B, S, TOK_W)


def diff_attention(z, gq, gk, lam_p, subln_g, lambda_init):
    B, S, _ = z.shape
    q, k, v = jnp.split(z, 3, axis=-1)
    q = rms_norm(q.reshape(B, S, DIFF_HEADS, 2, HEAD_DIM), gq)
    k = rms_norm(k.reshape(B, S, DIFF_HEADS, 2, HEAD_DIM), gk)
    vf = v.reshape(B, S, DIFF_HEADS, 2 * HEAD_DIM).astype(jnp.float32)
    lp = lam_p.astype(jnp.float32)
    lam = jnp.exp(jnp.sum(lp[0] * lp[1])) - jnp.exp(jnp.sum(lp[2] * lp[3])) + lambda_init
    scale = HEAD_DIM ** -0.5
    n_blk = S // Q_BLOCK
    qb = q.reshape(B, n_blk, Q_BLOCK, DIFF_HEADS, 2, HEAD_DIM).transpose(1, 0, 2, 3, 4, 5)
    kpos = jnp.arange(S)

    def block(args):
        q_blk, i = args
        s = jnp.einsum('bqhcd,bkhcd->bhcqk', q_blk, k).astype(jnp.float32) * scale
        qpos = i * Q_BLOCK + jnp.arange(Q_BLOCK)
        mask = kpos[None, :] <= qpos[:, None]
        p = jax.nn.softmax(jnp.where(mask, s, -jnp.inf), axis=-1)
        a = p[:, :, 0] - lam * p[:, :, 1]
        return jnp.einsum('bhqk,bkhe->bqhe', a, vf)

    o = lax.map(block, (qb, jnp.arange(n_blk)))
    o = o.transpose(1, 0, 2, 3, 4).reshape(B, S, DIFF_HEADS, 2 * HEAD_DIM)
    o = rms_norm(o, subln_g) * (1.0 - lambda_init)
    return o.reshape(B, S, TOK_W).astype(z.dtype)


def mem_cross_attention(qm, mem_h, w_kv, gq, gk):
    B, S, _ = qm.shape
    L = mem_h.shape[1]
    q = rms_norm(qm.reshape(B, S, MEM_HEADS, HEAD_DIM), gq)
    k, v = jnp.split(mem_h @ w_kv, 2, axis=-1)
    k = rms_norm(k.reshape(B, L, MEM_HEADS, HEAD_DIM), gk)
    v = v.reshape(B, L, MEM_HEADS, HEAD_DIM).astype(jnp.float32)
    s = jnp.einsum('bshd,blhd->bhsl', q, k).astype(jnp.float32) * (HEAD_DIM ** -0.5)
    p = jax.nn.softmax(s, axis=-1)
    o = jnp.einsum('bhsl,blhd->bshd', p, v)
    return o.reshape(B, S, MEM_W).astype(qm.dtype)


def setup_inputs(seed: int = 0) -> dict:
    key = jax.random.key(seed)
    ks = jax.random.split(key, 24)
    n_a = (DEPTH + 1) // 2
    n_b = DEPTH // 2
    f32 = jnp.float32

    def nrm(k, shape, s):
        return jax.random.normal(k, shape, f32) * s

    def gain(k, shape):
        return 1.0 + 0.02 * jax.random.normal(k, shape, f32)

    return {
        "x": jax.random.normal(ks[0], (BATCH, SEQ, D_MODEL), f32),
        "mem": jax.random.normal(ks[1], (BATCH, MEM_LEN, D_MODEL), f32),
        "ffn_norm": gain(ks[2], (DEPTH, 2, D_MODEL)),
        "ffn_w_in": nrm(ks[3], (DEPTH, 2, D_MODEL, 2 * D_FF), D_MODEL ** -0.5),
        "ffn_w_out": nrm(ks[4], (DEPTH, 2, D_FF, D_MODEL), D_FF ** -0.5),
        "mix_norm": gain(ks[5], (DEPTH, D_MODEL)),
        "mem_norm": gain(ks[6], (DEPTH, D_MODEL)),
        "w_mem_kv": nrm(ks[7], (DEPTH, D_MODEL, 2 * MEM_W), D_MODEL ** -0.5),
        "memq_norm": gain(ks[8], (DEPTH, HEAD_DIM)),
        "memk_norm": gain(ks[9], (DEPTH, HEAD_DIM)),
        "w_out": nrm(ks[10], (DEPTH, TOK_W + MEM_W, D_MODEL), (TOK_W + MEM_W) ** -0.5),
        "a_w_in": nrm(ks[11], (n_a, D_MODEL, 2 * TOK_W + MEM_W), D_MODEL ** -0.5),
        "a_v_norm": gain(ks[12], (n_a, TOK_W)),
        "a_w_s": nrm(ks[13], (n_a, GMLP_GROUPS, CHUNK, CHUNK), CHUNK ** -0.5),
        "a_b_s": nrm(ks[14], (n_a, GMLP_GROUPS, CHUNK), 0.02),
        "b_w_in": nrm(ks[15], (n_b, D_MODEL, 3 * TOK_W + MEM_W), D_MODEL ** -0.5),
        "b_q_norm": gain(ks[16], (n_b, HEAD_DIM)),
        "b_k_norm": gain(ks[17], (n_b, HEAD_DIM)),
        "b_lambda": nrm(ks[18], (n_b, 4, HEAD_DIM), 0.1),
        "b_subln": gain(ks[19], (n_b, 2 * HEAD_DIM)),
    }


def reference(x, mem, ffn_norm, ffn_w_in, ffn_w_out, mix_norm, mem_norm, w_mem_kv,
              memq_norm, memk_norm, w_out, a_w_in, a_v_norm, a_w_s, a_b_s,
              b_w_in, b_q_norm, b_k_norm, b_lambda, b_subln):
    for i in range(DEPTH):
        j = i // N_MIXERS
        x = x + 0.5 * swiglu(rms_norm(x, ffn_norm[i, 0]), ffn_w_in[i, 0], ffn_w_out[i, 0])
        h = rms_norm(x, mix_norm[i])
        mem_h = rms_norm(mem, mem_norm[i])
        if i % N_MIXERS == 0:
            z = h @ a_w_in[j]
            tok = chunked_gmlp(z[..., :2 * TOK_W], a_v_norm[j], a_w_s[j], a_b_s[j])
            qm = z[..., 2 * TOK_W:]
        else:
            z = h @ b_w_in[j]
            lambda_init = 0.8 - 0.6 * math.exp(-0.3 * i)
            tok = diff_attention(z[..., :3 * TOK_W], b_q_norm[j], b_k_norm[j], b_lambda[j], b_subln[j], lambda_init)
            qm = z[..., 3 * TOK_W:]
        mo = mem_cross_attention(qm, mem_h, w_mem_kv[i], memq_norm[i], memk_norm[i])
        x = x + jnp.concatenate([tok, mo], axis=-1) @ w_out[i]
        x = x + 0.5 * swiglu(rms_norm(x, ffn_norm[i, 1]), ffn_w_in[i, 1], ffn_w_out[i, 1])
    return x
```

```python
import math
from contextlib import ExitStack

import numpy as np
import concourse.bass as bass
import concourse.mybir as mybir
from concourse.bass_utils import run_bass_kernel_spmd

F32 = mybir.dt.float32
BF16 = mybir.dt.bfloat16
AF = mybir.ActivationFunctionType
ALU = mybir.AluOpType
AX = mybir.AxisListType

N_CORES = 8
D = 1024
S = 2048
MEM_L = 256
KC = 8
TT = 4
TW = 512
DFF = 2816
FC = 22
TOKW = 768
EPS = 1e-6
NB_CORE = 2
GROUP = 4
NDSEM = 12


class Trk:
    __slots__ = ("w", "r")

    def __init__(self):
        self.w = None
        self.r = {}


class Eng:
    def __init__(self, name, sem, is_pe=False):
        self.name = name
        self.sem = sem
        self.cnt = 0
        self.seen = {}
        self.prog = []
        self.is_pe = is_pe
        self.dsems = []
        self.dvals = []
        self.dnext = 0


class Sched:
    def __init__(self, nc, es):
        self.nc = nc
        self.engs = {}
        for name in ("pe", "act", "dve", "pool", "sp"):
            sem = es.enter_context(nc.semaphore("tl_" + name))
            self.engs[name] = Eng(name, sem, is_pe=(name == "pe"))
        for name in ("pool", "sp"):
            e = self.engs[name]
            for i in range(NDSEM):
                e.dsems.append(es.enter_context(nc.semaphore(f"d_{name}{i}")))
                e.dvals.append(0)
        self.pe, self.act, self.dve, self.pool, self.sp = (self.engs[n] for n in ("pe", "act", "dve", "pool", "sp"))

    def _wait(self, eng, tk):
        sem, val, src = tk
        key = sem.num
        if eng.seen.get(key, 0) >= val:
            return
        eng.seen[key] = val
        eng.prog.append(lambda h, sem=sem, val=val: h.wait_ge(sem, val))

    def _deps(self, eng, r, w):
        for t in r:
            tk = t.w
            if tk is not None:
                if tk[2] is eng:
                    if not eng.is_pe:
                        self._wait(eng, tk)
                else:
                    self._wait(eng, tk)
        for t in w:
            tk = t.w
            if tk is not None and tk[2] is not eng:
                self._wait(eng, tk)
            for tk in t.r.values():
                if tk[2] is not eng:
                    self._wait(eng, tk)

    def _mark(self, tk, r, w, key):
        for t in w:
            t.w = tk
            t.r = {}
        for t in r:
            t.r[key] = tk

    def op(self, eng, fn, r=(), w=()):
        self._deps(eng, r, w)
        eng.cnt += 1
        sem = eng.sem
        eng.prog.append(lambda h, fn=fn, sem=sem: fn(h).then_inc(sem, 1))
        tk = (sem, eng.cnt, eng)
        self._mark(tk, r, w, eng.name)
        return tk

    def dma(self, eng, out, in_, r=(), w=(), allow=False):
        self._deps(eng, r, w)
        i = eng.dnext
        eng.dnext = (i + 1) % len(eng.dsems)
        sem = eng.dsems[i]
        if eng.dvals[i] > 0:
            self._wait(eng, (sem, eng.dvals[i], None))
        eng.dvals[i] += 16
        val = eng.dvals[i]
        if allow:
            eng.prog.append(lambda h, out=out, in_=in_, sem=sem: h.dma_start(out=out, in_=in_, allow_slow_non_contiguous=True).then_inc(sem, 16))
        else:
            eng.prog.append(lambda h, out=out, in_=in_, sem=sem: h.dma_start(out=out, in_=in_).then_inc(sem, 16))
        tk = (sem, val, None)
        self._mark(tk, r, w, "dma_%s_%d" % (eng.name, i))
        return tk

    def wait_all_dma(self, eng):
        for e in (self.pool, self.sp):
            for i, sem in enumerate(e.dsems):
                if e.dvals[i] > 0:
                    self._wait(eng, (sem, e.dvals[i], None))

    def mm(self, out, lhsT, rhs, start, stop, r, w):
        return self.op(self.pe, lambda h: h.matmul(out, lhsT, rhs, start=start, stop=stop), r=r, w=w)

    def transpose(self, out, in_, ident, r, w):
        return self.op(self.pe, lambda h: h.transpose(out, in_, ident), r=r, w=w)

    def actf(self, out, in_, func, r, w, scale=None, bias=None):
        kw = {}
        if scale is not None:
            kw["scale"] = scale
        if bias is not None:
            kw["bias"] = bias
        return self.op(self.act, lambda h: h.activation(out=out, in_=in_, func=func, **kw), r=r, w=w)

    def tt(self, eng, out, in0, in1, op, r, w):
        return self.op(eng, lambda h: h.tensor_tensor(out=out, in0=in0, in1=in1, op=op), r=r, w=w)

    def ts(self, eng, out, in0, s1, s2, op0, op1, r, w):
        if op1 is None:
            return self.op(eng, lambda h: h.tensor_scalar(out=out, in0=in0, scalar1=s1, scalar2=None, op0=op0), r=r, w=w)
        return self.op(eng, lambda h: h.tensor_scalar(out=out, in0=in0, scalar1=s1, scalar2=s2, op0=op0, op1=op1), r=r, w=w)

    def stt(self, out, in0, scalar, in1, op0, op1, r, w):
        return self.op(self.dve, lambda h: h.scalar_tensor_tensor(out=out, in0=in0, scalar=scalar, in1=in1, op0=op0, op1=op1), r=r, w=w)

    def recip(self, out, in_, r, w):
        return self.op(self.dve, lambda h: h.reciprocal(out=out, in_=in_), r=r, w=w)

    def copy(self, eng, out, in_, r, w):
        if eng is self.act:
            return self.op(eng, lambda h: h.copy(out=out, in_=in_), r=r, w=w)
        return self.op(eng, lambda h: h.tensor_copy(out=out, in_=in_), r=r, w=w)


class Rot:
    def __init__(self, items):
        self.items = items
        self.i = 0

    def next(self):
        it = self.items[self.i]
        self.i = (self.i + 1) % len(self.items)
        return it


def build_nc(nb=NB_CORE, stages=6, nslot=None):
    nc = bass.Bass("TRN2", target_bir_lowering=False)
    dr = {}

    def din(name, shape):
        dr[name] = nc.dram_tensor(name, list(shape), F32, kind="ExternalInput").ap()
        return dr[name]

    x_d = din("x", (nb, S, D))
    mem_d = din("mem", (nb, MEM_L, D))
    ffn_norm = din("ffn_norm", (2, 2, D))
    ffn_w_in = din("ffn_w_in", (2, 2, D, 2 * DFF))
    ffn_w_out = din("ffn_w_out", (2, 2, DFF, D))
    mix_norm = din("mix_norm", (2, D))
    mem_norm = din("mem_norm", (2, D))
    w_mem_kv = din("w_mem_kv", (2, D, 512))
    memq_norm = din("memq_norm", (2, 64))
    memk_norm = din("memk_norm", (2, 64))
    w_out = din("w_out", (2, D, D))
    a_w_in = din("a_w_in", (1, D, 1792))
    a_v_norm = din("a_v_norm", (1, TOKW))
    a_w_s = din("a_w_s", (1, 6, 128, 128))
    a_b_s = din("a_b_s", (1, 6, 128))
    b_w_in = din("b_w_in", (1, D, 2560))
    b_q_norm = din("b_q_norm", (1, 64))
    b_k_norm = din("b_k_norm", (1, 64))
    b_lambda = din("b_lambda", (1, 4, 64))
    b_subln = din("b_subln", (1, 128))
    y_d = nc.dram_tensor("y", [nb, S, D], F32, kind="ExternalOutput").ap()

    es = ExitStack()
    with es:
        def sb(name, shape, dt):
            return es.enter_context(nc.sbuf_tensor(name, list(shape), dt))

        xT = sb("xT", (128, KC, S), F32)
        hT = sb("hT", (128, KC, S), BF16)
        Abuf = [sb(f"A{i}", (128, GROUP, TW), BF16) for i in range(2)]
        tokT = sb("tokT", (128, 8, TW), BF16)
        qT = sb("qT", (128, 8, TW), BF16)
        ptl = [sb(f"pt{i}", (128, TW), BF16) for i in range(3)]
        ftl = [sb(f"ft{i}", (128, TW), F32) for i in range(4)]
        ident = sb("ident", (128, 128), F32)
        ones_bf = sb("ones_bf", (128, 128), BF16)
        blk_bf = sb("blk_bf", (128, 128), BF16)
        mask_bf = sb("mask_bf", (128, 128), BF16)
        wsT = sb("wsT", (128, 6, 128), BF16)
        gcols = sb("gcols", (128, 64), F32)
        avn_bc = sb("avn_bc", (128, TOKW), F32)
        abs_bc = sb("abs_bc", (128, TOKW), F32)
        small = sb("small", (128, 32), F32)
        kmT = sb("kmT", (128, 2, MEM_L), BF16)
        vm = sb("vm", (128, 2, 256), BF16)
        if nslot is None:
            nslot = 16
        slots = [sb(f"slot{i}", (128, 2048), BF16) for i in range(nslot)]
        ps = [es.enter_context(nc.psum_tensor(f"ps{i}", [128, TW], F32)) for i in range(8)]

        Sx = Sched(nc, es)
        PE, ACT, DVE, POOL, SP = Sx.pe, Sx.act, Sx.dve, Sx.pool, Sx.sp

        x_trk = [[Trk() for _ in range(TT)] for _ in range(KC)]
        h_trk = [[Trk() for _ in range(TT)] for _ in range(KC)]
        A_trk = [Trk(), Trk()]
        tok_trk = [Trk() for _ in range(8)]
        q_trk = [Trk() for _ in range(8)]
        ps_trk = [Trk() for _ in range(8)]
        slot_trk = [Trk() for _ in range(nslot)]
        const_trk = Trk()
        km_trk = Trk()
        vm_trk = Trk()
        small_trk = Trk()
        pt_rot = Rot([(ptl[i], Trk()) for i in range(3)])
        ft_rot = Rot([(ftl[i], Trk()) for i in range(4)])

        free_slots = list(range(nslot))

        def alloc(n):
            assert len(free_slots) >= n, "slot pool exhausted"
            got = free_slots[:n]
            del free_slots[:n]
            return got

        def release(ids):
            free_slots.extend(ids)

        wq = []
        wq_pos = [0]
        wq_map = {}

        def wreg(key, n, emit):
            ent = {"key": key, "n": n, "emit": emit, "slots": None}
            wq.append(ent)
            wq_map[key] = ent

        def pump(reserve=4):
            while wq_pos[0] < len(wq):
                ent = wq[wq_pos[0]]
                if len(free_slots) - reserve < ent["n"]:
                    break
                ent["slots"] = alloc(ent["n"])
                ent["emit"](ent["slots"])
                wq_pos[0] += 1

        def want(key):
            ent = wq_map[key]
            while ent["slots"] is None:
                nxt = wq[wq_pos[0]]
                nxt["slots"] = alloc(nxt["n"])
                nxt["emit"](nxt["slots"])
                wq_pos[0] += 1
            return ent["slots"]

        def wdma(slot_id, out_ap, in_ap):
            Sx.dma(POOL, out_ap, in_ap, r=(), w=(slot_trk[slot_id],))

        def slot3(sid, a, b):
            return slots[sid][:].rearrange("p (a b) -> p a b", a=a, b=b)

        def gload(col, vec_ap):
            Sx.dma(SP, gcols[:, col:col + KC], vec_ap.rearrange("(kc p) -> p kc", p=128), r=(), w=(const_trk,), allow=True)

        for i in range(2):
            for j in range(2):
                gload((i * 2 + j) * 8, ffn_norm[i, j])
            gload(32 + 8 * i, mix_norm[i])
            gload(48 + 8 * i, mem_norm[i])

        def hload(col, vec_ap):
            v = vec_ap.rearrange("(p o) -> p o", o=1)
            Sx.dma(SP, small[0:64, col:col + 1], v, r=(), w=(small_trk,), allow=True)
            Sx.dma(SP, small[64:128, col:col + 1], v, r=(), w=(small_trk,), allow=True)

        hload(1, memq_norm[0]); hload(2, memq_norm[1]); hload(3, memk_norm[0]); hload(4, memk_norm[1])
        hload(5, b_q_norm[0]); hload(6, b_k_norm[0])
        Sx.dma(SP, small[:, 7:8], b_subln[0].rearrange("(p o) -> p o", o=1), r=(), w=(small_trk,), allow=True)
        Sx.dma(SP, avn_bc[:], a_v_norm[0:1, :].broadcast_to([128, TOKW]), r=(), w=(const_trk,))
        Sx.dma(SP, abs_bc[:], a_b_s[0].rearrange("g t -> (g t)").rearrange("(o n) -> o n", o=1).broadcast_to([128, TOKW]), r=(), w=(const_trk,))

        Sx.op(POOL, lambda h: h.memset(small[:, 0:1], EPS), r=(), w=(small_trk,))
        Sx.op(POOL, lambda h: h.memset(ones_bf[:], 1.0), r=(), w=(const_trk,))
        Sx.op(POOL, lambda h: h.memset(ident[:], 1.0), r=(), w=(const_trk,))
        Sx.op(POOL, lambda h: h.affine_select(out=ident[:], in_=ident[:], pattern=[[-1, 128]], compare_op=ALU.is_equal, fill=0.0, base=0, channel_multiplier=1), r=(), w=(const_trk,))
        Sx.op(POOL, lambda h: h.memset(mask_bf[:], 1.0), r=(), w=(const_trk,))
        Sx.op(POOL, lambda h: h.affine_select(out=mask_bf[:], in_=mask_bf[:], pattern=[[1, 128]], compare_op=ALU.is_ge, fill=0.0, base=0, channel_multiplier=-1), r=(), w=(const_trk,))
        Sx.op(POOL, lambda h: h.memset(blk_bf[:], 0.0), r=(), w=(const_trk,))
        Sx.op(POOL, lambda h: h.memset(blk_bf[0:64, 0:64], 1.0), r=(), w=(const_trk,))
        Sx.op(POOL, lambda h: h.memset(blk_bf[64:128, 64:128], 1.0), r=(), w=(const_trk,))

        lambda_init = 0.8 - 0.6 * math.exp(-0.3 * 1)
        lam_bc, lamt = ft_rot.next()
        Sx.dma(SP, lam_bc[:, 0:256], b_lambda[0].rearrange("a b -> (a b)").rearrange("(o n) -> o n", o=1).broadcast_to([128, 256]), r=(), w=(lamt,))
        f0, f0t = ft_rot.next()
        Sx.tt(DVE, f0[:, 0:64], lam_bc[:, 0:64], lam_bc[:, 64:128], ALU.mult, r=(lamt,), w=(f0t,))
        Sx.tt(DVE, f0[:, 64:128], lam_bc[:, 128:192], lam_bc[:, 192:256], ALU.mult, r=(lamt,), w=(f0t,))
        Sx.op(DVE, lambda h: h.tensor_reduce(out=small[:, 9:11], in_=f0[:, 0:128].rearrange("p (a b) -> p a b", a=2), op=ALU.add, axis=AX.X), r=(f0t,), w=(small_trk,))
        Sx.actf(small[:, 11:13], small[:, 9:11], AF.Exp, r=(small_trk,), w=(small_trk,))
        Sx.tt(DVE, small[:, 8:9], small[:, 12:13], small[:, 11:12], ALU.subtract, r=(small_trk,), w=(small_trk,))
        Sx.ts(DVE, small[:, 8:9], small[:, 8:9], -lambda_init, None, ALU.add, None, r=(small_trk,), w=(small_trk,))
        Sx.ts(DVE, small[:, 7:8], small[:, 7:8], 1.0 - lambda_init, None, ALU.mult, None, r=(small_trk,), w=(small_trk,))

        (sid,) = alloc(1)
        wsf = slots[sid][:].bitcast(F32).rearrange("p (g s) -> p g s", g=8)
        Sx.dma(SP, wsf[:, 0:6, :], a_w_s[0].rearrange("g t s -> t g s"), r=(), w=(slot_trk[sid],))
        for g in range(6):
            pb = g % 2
            Sx.transpose(ps[pb][:, 0:128], wsf[:, g, :], ident[:], r=(slot_trk[sid], const_trk), w=(ps_trk[pb],))
            Sx.tt(DVE, wsT[:, g, :], ps[pb][:, 0:128], mask_bf[:], ALU.mult, r=(const_trk,), w=(ps_trk[pb], const_trk))
        release([sid])

        eps_col = small[:, 0:1]

        def rstd_from_psum(pbank, ncols, scale, nparts=128):
            ft, ftt = ft_rot.next()
            Sx.actf(ft[0:nparts, 0:ncols], ps[pbank][0:nparts, 0:ncols], AF.Sqrt, r=(small_trk,), w=(ps_trk[pbank], ftt), scale=scale, bias=eps_col[0:nparts, :])
            Sx.recip(ft[0:nparts, 0:ncols], ft[0:nparts, 0:ncols], r=(ftt,), w=(ftt,))
            return ft, ftt

        def norm_tile(tt, gcol0):
            t0 = tt * TW
            pb = 7
            for kc in range(KC):
                pt, ptt = pt_rot.next()
                Sx.actf(pt[:], xT[:, kc, t0:t0 + TW], AF.Square, r=(x_trk[kc][tt],), w=(ptt,))
                Sx.mm(ps[pb][:], ones_bf[:], pt[:], kc == 0, kc == KC - 1, r=(ptt, const_trk), w=(ps_trk[pb],))
            ft, ftt = rstd_from_psum(pb, TW, 1.0 / D)
            for kc in range(KC):
                Sx.stt(hT[:, kc, t0:t0 + TW], xT[:, kc, t0:t0 + TW], gcols[:, gcol0 + kc:gcol0 + kc + 1], ft[:], ALU.mult, ALU.mult,
                       r=(x_trk[kc][tt], ftt, const_trk), w=(h_trk[kc][tt],))

        def head_norm(pbank_raw, pbank_stat, ncols, gcol, out_ap, out_trk):
            pt, ptt = pt_rot.next()
            Sx.actf(pt[:, 0:ncols], ps[pbank_raw][:, 0:ncols], AF.Square, r=(), w=(ps_trk[pbank_raw], ptt))
            Sx.mm(ps[pbank_stat][:, 0:ncols], blk_bf[:], pt[:, 0:ncols], True, True, r=(ptt, const_trk), w=(ps_trk[pbank_stat],))
            ft, ftt = rstd_from_psum(pbank_stat, ncols, 1.0 / 64)
            Sx.stt(out_ap, ps[pbank_raw][:, 0:ncols], gcol, ft[:, 0:ncols], ALU.mult, ALU.mult,
                   r=(ftt, small_trk), w=(ps_trk[pbank_raw], out_trk))

        def load_x(b):
            for tt in range(TT):
                sids = alloc(4)
                for n in range(4):
                    tc_ = tt * 4 + n
                    Sx.dma(SP, slots[sids[n]][:].bitcast(F32), x_d[b, tc_ * 128:(tc_ + 1) * 128, :], r=(), w=(slot_trk[sids[n]],))
                for kc in range(KC):
                    pb = kc % 4
                    for n in range(4):
                        src = slots[sids[n]][:].bitcast(F32)
                        Sx.transpose(ps[pb][:, n * 128:(n + 1) * 128], src[:, kc * 128:(kc + 1) * 128], ident[:],
                                     r=(slot_trk[sids[n]], const_trk), w=(ps_trk[pb],))
                    eng = ACT if kc % 2 == 0 else DVE
                    Sx.copy(eng, xT[:, kc, tt * TW:(tt + 1) * TW], ps[pb][:], r=(), w=(ps_trk[pb], x_trk[kc][tt]))
                release(sids)

        def store_x(b):
            for tc_ in range(16):
                tt = tc_ // 4
                (sid,) = alloc(1)
                dst = slots[sid][:].bitcast(F32)
                for half in range(2):
                    pb = 4 + (tc_ * 2 + half) % 4
                    for q in range(4):
                        kc = half * 4 + q
                        Sx.transpose(ps[pb][:, q * 128:(q + 1) * 128], xT[:, kc, tc_ * 128:(tc_ + 1) * 128], ident[:],
                                     r=(x_trk[kc][tt], const_trk), w=(ps_trk[pb],))
                    eng = ACT if half == 0 else DVE
                    Sx.copy(eng, dst[:, half * 512:(half + 1) * 512], ps[pb][:], r=(), w=(ps_trk[pb], slot_trk[sid]))
                Sx.dma(SP, y_d[b, tc_ * 128:(tc_ + 1) * 128, :], dst, r=(slot_trk[sid],), w=())
                release([sid])

        def reg_ffn(b, i, j):
            win = ffn_w_in[i, j].rearrange("(kc p) f -> p kc f", p=128)
            wout = ffn_w_out[i, j]
            ngrp = (FC + GROUP - 1) // GROUP
            for g in range(ngrp):
                f0 = g * GROUP
                nf = min(GROUP, FC - f0)
                nsl = (nf + 1) // 2

                def emit(sids, f0=f0, nf=nf, nsl=nsl):
                    for s_ in range(nsl):
                        c0 = (f0 + 2 * s_) * 128
                        w_ = min(256, (f0 + nf) * 128 - c0)
                        wdma(sids[s_], slot3(sids[s_], 8, 256)[:, :, 0:w_], win[:, :, c0:c0 + w_])
                    for s_ in range(nsl):
                        c0 = DFF + (f0 + 2 * s_) * 128
                        w_ = min(256, DFF + (f0 + nf) * 128 - c0)
                        wdma(sids[nsl + s_], slot3(sids[nsl + s_], 8, 256)[:, :, 0:w_], win[:, :, c0:c0 + w_])
                    for s_ in range(nsl):
                        r0 = (f0 + 2 * s_) * 128
                        nr = min(2, f0 + nf - (f0 + 2 * s_))
                        wdma(sids[2 * nsl + s_], slot3(sids[2 * nsl + s_], 2, 1024)[:, 0:nr, :],
                             wout[r0:r0 + nr * 128, :].rearrange("(fc p) d -> p fc d", p=128))
                wreg(("ffn", b, i, j, g), 3 * nsl, emit)

        def reg_slabs(key, w2d, col0, ncol):
            wv = w2d.rearrange("(kc p) f -> p kc f", p=128)
            for s_ in range(ncol // 256):
                def emit(sids, s_=s_):
                    c0 = col0 + s_ * 256
                    wdma(sids[0], slot3(sids[0], 8, 256), wv[:, :, c0:c0 + 256])
                wreg(key + (s_,), 1, emit)

        def reg_wout(key, w2d):
            for s_ in range(4):
                def emit(sids, s_=s_):
                    wdma(sids[0], slot3(sids[0], 2, 1024), w2d[s_ * 256:(s_ + 1) * 256, :].rearrange("(c p) d -> p c d", p=128))
                wreg(key + (s_,), 1, emit)

        ycnt = [0]
        gucnt = [0]

        def ffn(b, i, j):
            for tt in range(TT):
                norm_tile(tt, (i * 2 + j) * 8)
            ngrp = (FC + GROUP - 1) // GROUP
            pending = [None]
            acnt = [0]
            for g in range(ngrp):
                f0 = g * GROUP
                nf = min(GROUP, FC - f0)
                nsl = (nf + 1) // 2
                sids = want(("ffn", b, i, j, g))
                pump(reserve=4)
                gate_s, up_s, out_s = sids[0:nsl], sids[nsl:2 * nsl], sids[2 * nsl:3 * nsl]
                for tt in range(TT):
                    t0 = tt * TW
                    ab = acnt[0] % 2
                    acnt[0] += 1
                    A = Abuf[ab]
                    for q in range(nf):
                        pg = (gucnt[0] % 2) * 2
                        pu = pg + 1
                        gucnt[0] += 1
                        gs = slot3(gate_s[q // 2], 8, 256)
                        us = slot3(up_s[q // 2], 8, 256)
                        co = (q % 2) * 128
                        for kc in range(KC):
                            Sx.mm(ps[pg][:], gs[:, kc, co:co + 128], hT[:, kc, t0:t0 + TW], kc == 0, kc == KC - 1,
                                  r=(slot_trk[gate_s[q // 2]], h_trk[kc][tt]), w=(ps_trk[pg],))
                        for kc in range(KC):
                            Sx.mm(ps[pu][:], us[:, kc, co:co + 128], hT[:, kc, t0:t0 + TW], kc == 0, kc == KC - 1,
                                  r=(slot_trk[up_s[q // 2]], h_trk[kc][tt]), w=(ps_trk[pu],))
                        ft, ftt = ft_rot.next()
                        Sx.actf(ft[:], ps[pg][:], AF.Silu, r=(), w=(ps_trk[pg], ftt))
                        Sx.tt(DVE, A[:, q, :], ft[:], ps[pu][:], ALU.mult, r=(ftt,), w=(ps_trk[pu], A_trk[ab]))
                    if pending[0] is not None:
                        pending[0]()

                    def ywork(tt=tt, t0=t0, A=A, ab=ab, nf=nf, out_s=out_s):
                        for dc in range(KC):
                            py = 4 + ycnt[0] % 3
                            ycnt[0] += 1
                            for q in range(nf):
                                ws_ = slot3(out_s[q // 2], 2, 1024)
                                Sx.mm(ps[py][:], ws_[:, q % 2, dc * 128:(dc + 1) * 128], A[:, q, :], q == 0, q == nf - 1,
                                      r=(slot_trk[out_s[q // 2]], A_trk[ab]), w=(ps_trk[py],))
                            Sx.stt(xT[:, dc, t0:t0 + TW], ps[py][:], 0.5, xT[:, dc, t0:t0 + TW], ALU.mult, ALU.add,
                                   r=(), w=(ps_trk[py], x_trk[dc][tt]))
                    pending[0] = ywork
                if g == ngrp - 1:
                    pending[0]()
                    pending[0] = None
                    release(sids)
                else:
                    pending[0]()
                    pending[0] = None
                    release(sids)

        def mem_kv(b, i):
            sm = alloc(2)
            (sh,) = alloc(1)
            mh = slot3(sh, 8, 256)
            for lc in range(2):
                mf = slots[sm[lc]][:].bitcast(F32)
                Sx.dma(SP, mf, mem_d[b, lc * 128:(lc + 1) * 128, :], r=(), w=(slot_trk[sm[lc]],))
                ft, ftt = ft_rot.next()
                for hf in range(2):
                    Sx.tt(DVE, ft[:], mf[:, hf * 512:(hf + 1) * 512], mf[:, hf * 512:(hf + 1) * 512], ALU.mult, r=(slot_trk[sm[lc]],), w=(ftt,))
                    Sx.op(DVE, lambda h, ft=ft, hf=hf: h.tensor_reduce(out=small[:, 16 + hf:17 + hf], in_=ft[:], op=ALU.add, axis=AX.X), r=(ftt,), w=(small_trk,))
                Sx.tt(DVE, small[:, 18:19], small[:, 16:17], small[:, 17:18], ALU.add, r=(small_trk,), w=(small_trk,))
                Sx.actf(small[:, 19:20], small[:, 18:19], AF.Sqrt, r=(small_trk,), w=(small_trk,), scale=1.0 / D, bias=eps_col)
                Sx.recip(small[:, 20:21], small[:, 19:20], r=(small_trk,), w=(small_trk,))
                Sx.ts(DVE, mf, mf, small[:, 20:21], None, ALU.mult, None, r=(small_trk, slot_trk[sm[lc]]), w=(slot_trk[sm[lc]],))
            for kc in range(KC):
                pb = kc % 2
                for lc in range(2):
                    mf = slots[sm[lc]][:].bitcast(F32)
                    Sx.transpose(ps[pb][:, lc * 128:(lc + 1) * 128], mf[:, kc * 128:(kc + 1) * 128], ident[:],
                                 r=(slot_trk[sm[lc]], const_trk), w=(ps_trk[pb],))
                Sx.ts(DVE, mh[:, kc, :], ps[pb][:, 0:256], gcols[:, 48 + 8 * i + kc:48 + 8 * i + kc + 1], None, ALU.mult, None,
                      r=(const_trk,), w=(ps_trk[pb], slot_trk[sh]))
            release(sm)
            ks = want(("memkv", b, i, 0))[0]
            vs = want(("memkv", b, i, 1))[0]
            kw = slot3(ks, 8, 256)
            vw = slot3(vs, 8, 256)
            for hc in range(2):
                pr = 2 + hc
                for kc in range(KC):
                    Sx.mm(ps[pr][:, 0:256], kw[:, kc, hc * 128:(hc + 1) * 128], mh[:, kc, :], kc == 0, kc == KC - 1,
                          r=(slot_trk[ks], slot_trk[sh]), w=(ps_trk[pr],))
                head_norm(pr, 7, 256, small[:, 3 + i:4 + i], kmT[:, hc, :], km_trk)
            for lc in range(2):
                pr = 4 + lc
                for kc in range(KC):
                    Sx.mm(ps[pr][:, 0:256], mh[:, kc, lc * 128:(lc + 1) * 128], vw[:, kc, :], kc == 0, kc == KC - 1,
                          r=(slot_trk[vs], slot_trk[sh]), w=(ps_trk[pr],))
                Sx.copy(ACT, vm[:, lc, :], ps[pr][:, 0:256], r=(), w=(ps_trk[pr], vm_trk))
            release([ks, vs, sh])

        def mem_attn(tt):
            for hc in range(2):
                pnum, pden = 0, 1
                for hh in range(2):
                    hm = hc * 2 + hh
                    r0 = hh * 64
                    for lc in range(2):
                        pss = 4 + (hm * 2 + lc) % 3
                        Sx.mm(ps[pss][:], kmT[r0:r0 + 64, hc, lc * 128:(lc + 1) * 128], qT[r0:r0 + 64, 6 + hc, :], True, True,
                              r=(km_trk, q_trk[6 + hc]), w=(ps_trk[pss],))
                        pt, ptt = pt_rot.next()
                        Sx.actf(pt[:], ps[pss][:], AF.Exp, r=(), w=(ps_trk[pss], ptt), scale=0.125)
                        Sx.mm(ps[pnum][r0:r0 + 64, :], vm[:, lc, hm * 64:(hm + 1) * 64], pt[:], lc == 0, lc == 1,
                              r=(vm_trk, ptt), w=(ps_trk[pnum],))
                        Sx.mm(ps[pden][r0:r0 + 64, :], ones_bf[:, 0:64], pt[:], lc == 0, lc == 1,
                              r=(const_trk, ptt), w=(ps_trk[pden],))
                ft, ftt = ft_rot.next()
                Sx.recip(ft[:], ps[pden][:], r=(), w=(ps_trk[pden], ftt))
                Sx.tt(DVE, tokT[:, 6 + hc, :], ps[pnum][:], ft[:], ALU.mult, r=(ftt,), w=(ps_trk[pnum], tok_trk[6 + hc]))

        def out_proj(b, i, tt):
            t0 = tt * TW
            ws_ = [want(("wout", b, i, tt, s_))[0] for s_ in range(4)]
            for dc in range(KC):
                py = 4 + ycnt[0] % 3
                ycnt[0] += 1
                for c in range(8):
                    wv = slot3(ws_[c // 2], 2, 1024)
                    Sx.mm(ps[py][:], wv[:, c % 2, dc * 128:(dc + 1) * 128], tokT[:, c, :], c == 0, c == 7,
                          r=(slot_trk[ws_[c // 2]], tok_trk[c]), w=(ps_trk[py],))
                Sx.tt(DVE, xT[:, dc, t0:t0 + TW], ps[py][:], xT[:, dc, t0:t0 + TW], ALU.add, r=(), w=(ps_trk[py], x_trk[dc][tt]))
            release(ws_)

        def qm_proj(tt, slab_sid, memq_col):
            t0 = tt * TW
            wv = slot3(slab_sid, 8, 256)
            for hc in range(2):
                pr = 2 + hc
                for kc in range(KC):
                    Sx.mm(ps[pr][:], wv[:, kc, hc * 128:(hc + 1) * 128], hT[:, kc, t0:t0 + TW], kc == 0, kc == KC - 1,
                          r=(slot_trk[slab_sid], h_trk[kc][tt]), w=(ps_trk[pr],))
                head_norm(pr, 7, TW, memq_col, qT[:, 6 + hc, :], q_trk[6 + hc])

        def mixer_a(b):
            i = 0
            for tt in range(TT):
                norm_tile(tt, 32)
            mem_kv(b, i)
            slabs = [want(("a_in", b, s_))[0] for s_ in range(7)]
            pump(reserve=4)
            for tt in range(TT):
                t0 = tt * TW
                qm_proj(tt, slabs[6], small[:, 1:2])
                for g in range(6):
                    us = slot3(slabs[g // 2], 8, 256)
                    vs = slot3(slabs[3 + g // 2], 8, 256)
                    co = (g % 2) * 128
                    pu, pv, pm = 0, 1, 2 + g % 2
                    for kc in range(KC):
                        Sx.mm(ps[pu][:], us[:, kc, co:co + 128], hT[:, kc, t0:t0 + TW], kc == 0, kc == KC - 1,
                              r=(slot_trk[slabs[g // 2]], h_trk[kc][tt]), w=(ps_trk[pu],))
                    for n in range(4):
                        for kc in range(KC):
                            Sx.mm(ps[pv][:, n * 128:(n + 1) * 128], hT[:, kc, t0 + n * 128:t0 + (n + 1) * 128], vs[:, kc, co:co + 128],
                                  kc == 0, kc == KC - 1, r=(slot_trk[slabs[3 + g // 2]], h_trk[kc][tt]), w=(ps_trk[pv],))
                    fu, fut = ft_rot.next()
                    Sx.actf(fu[:], ps[pu][:], AF.Gelu, r=(), w=(ps_trk[pu], fut))
                    fv, fvt = ft_rot.next()
                    Sx.actf(fv[:], ps[pv][:], AF.Gelu, r=(), w=(ps_trk[pv], fvt))
                    fs, fst = ft_rot.next()
                    Sx.tt(DVE, fs[:], fv[:], fv[:], ALU.mult, r=(fvt,), w=(fst,))
                    Sx.op(DVE, lambda h, fs=fs: h.tensor_reduce(out=small[:, 24:28], in_=fs[:].rearrange("p (a b) -> p a b", a=4), op=ALU.add, axis=AX.X),
                          r=(fst,), w=(small_trk,))
                    Sx.actf(small[:, 28:32], small[:, 24:28], AF.Sqrt, r=(small_trk,), w=(small_trk,), scale=1.0 / 128, bias=eps_col)
                    Sx.recip(small[:, 28:32], small[:, 28:32], r=(small_trk,), w=(small_trk,))
                    pt, ptt = pt_rot.next()
                    for n in range(4):
                        Sx.stt(pt[:, n * 128:(n + 1) * 128], fv[:, n * 128:(n + 1) * 128], small[:, 28 + n:29 + n], avn_bc[:, g * 128:(g + 1) * 128],
                               ALU.mult, ALU.mult, r=(fvt, small_trk, const_trk), w=(ptt,))
                    for n in range(4):
                        Sx.mm(ps[pm][:, n * 128:(n + 1) * 128], pt[:, n * 128:(n + 1) * 128], wsT[:, g, :], True, True,
                              r=(ptt, const_trk), w=(ps_trk[pm],))
                    Sx.tt(DVE, fs[:].rearrange("p (a b) -> p a b", a=4), ps[pm][:].rearrange("p (a b) -> p a b", a=4),
                          abs_bc[:, g * 128:(g + 1) * 128].unsqueeze(1).broadcast_to([128, 4, 128]), ALU.add,
                          r=(const_trk,), w=(ps_trk[pm], fst))
                    Sx.tt(DVE, tokT[:, g, :], fs[:], fu[:], ALU.mult, r=(fst, fut), w=(tok_trk[g],))
                mem_attn(tt)
                if tt == TT - 1:
                    release(slabs)
                out_proj(b, i, tt)
                pump(reserve=4)

        def mixer_b(b):
            i = 1
            for tt in range(TT):
                norm_tile(tt, 40)
            mem_kv(b, i)
            kvs = want(("kv", b))
            k_s, v_s = kvs[0:6], kvs[6:12]
            kslabs = [want(("b_k", b, s_))[0] for s_ in range(3)]
            for c in range(6):
                wv = slot3(kslabs[c // 2], 8, 256)
                co = (c % 2) * 128
                for tt in range(TT):
                    t0 = tt * TW
                    pr = 2 + (c * TT + tt) % 2
                    for kc in range(KC):
                        Sx.mm(ps[pr][:], wv[:, kc, co:co + 128], hT[:, kc, t0:t0 + TW], kc == 0, kc == KC - 1,
                              r=(slot_trk[kslabs[c // 2]], h_trk[kc][tt]), w=(ps_trk[pr],))
                    head_norm(pr, 7, TW, small[:, 6:7], slots[k_s[c]][:, t0:t0 + TW], slot_trk[k_s[c]])
            release(kslabs)
            vslabs = [want(("b_v", b, s_))[0] for s_ in range(3)]
            for tc_ in range(16):
                tt = tc_ // 4
                for s_ in range(3):
                    wv = slot3(vslabs[s_], 8, 256)
                    pb = 4 + (tc_ * 3 + s_) % 3
                    for kc in range(KC):
                        Sx.mm(ps[pb][:, 0:256], hT[:, kc, tc_ * 128:(tc_ + 1) * 128], wv[:, kc, :], kc == 0, kc == KC - 1,
                              r=(slot_trk[vslabs[s_]], h_trk[kc][tt]), w=(ps_trk[pb],))
                    for hh in range(2):
                        h_ = s_ * 2 + hh
                        vdst = slot3(v_s[h_], 16, 128)
                        Sx.copy(ACT, vdst[:, tc_, :], ps[pb][:, hh * 128:(hh + 1) * 128], r=(), w=(ps_trk[pb], slot_trk[v_s[h_]]))
            release(vslabs)
            for tt in range(TT):
                t0 = tt * TW
                qs = [want(("b_q", b, tt, s_))[0] for s_ in range(3)]
                for c in range(6):
                    wv = slot3(qs[c // 2], 8, 256)
                    co = (c % 2) * 128
                    pr = 2 + c % 2
                    for kc in range(KC):
                        Sx.mm(ps[pr][:], wv[:, kc, co:co + 128], hT[:, kc, t0:t0 + TW], kc == 0, kc == KC - 1,
                              r=(slot_trk[qs[c // 2]], h_trk[kc][tt]), w=(ps_trk[pr],))
                    head_norm(pr, 7, TW, small[:, 5:6], qT[:, c, :], q_trk[c])
                release(qs)
                qms = want(("b_qm", b, tt, 0))[0]
                qm_proj(tt, qms, small[:, 2:3])
                release([qms])
                pump(reserve=4)
                nk = (tt + 1) * 4
                for h_ in range(6):
                    vsl = slot3(v_s[h_], 16, 128)
                    steps = []
                    for c in range(2):
                        for kc in range(nk):
                            steps.append((c, kc))
                    scnt = [0]
                    inflight = []

                    def stageA(c, kc):
                        j = kc - tt * 4
                        c0 = max(j, 0) * 128
                        pss = 4 + scnt[0] % 3
                        scnt[0] += 1
                        Sx.mm(ps[pss][:, c0:TW], slots[k_s[h_]][c * 64:(c + 1) * 64, kc * 128:(kc + 1) * 128], qT[c * 64:(c + 1) * 64, h_, c0:TW],
                              True, True, r=(slot_trk[k_s[h_]], q_trk[h_]), w=(ps_trk[pss],))
                        return (c, kc, j, c0, pss)

                    def stageB(st):
                        c, kc, j, c0, pss = st
                        pt, ptt = pt_rot.next()
                        Sx.actf(pt[:, c0:TW], ps[pss][:, c0:TW], AF.Exp, r=(), w=(ps_trk[pss], ptt), scale=0.125)
                        if j >= 0:
                            Sx.tt(DVE, pt[:, c0:c0 + 128], pt[:, c0:c0 + 128], mask_bf[:], ALU.mult, r=(ptt, const_trk), w=(ptt,))
                        pn, pd = c * 2, c * 2 + 1
                        Sx.mm(ps[pn][:, c0:TW], vsl[:, kc, :], pt[:, c0:TW], kc == 0, kc == nk - 1,
                              r=(slot_trk[v_s[h_]], ptt), w=(ps_trk[pn],))
                        Sx.mm(ps[pd][:, c0:TW], ones_bf[:], pt[:, c0:TW], kc == 0, kc == nk - 1,
                              r=(const_trk, ptt), w=(ps_trk[pd],))

                    for si, (c, kc) in enumerate(steps):
                        inflight.append(stageA(c, kc))
                        if len(inflight) > 2:
                            stageB(inflight.pop(0))
                    while inflight:
                        stageB(inflight.pop(0))
                    fa, fat = ft_rot.next()
                    Sx.recip(fa[:], ps[1][:], r=(), w=(ps_trk[1], fat))
                    Sx.tt(DVE, fa[:], ps[0][:], fa[:], ALU.mult, r=(fat,), w=(ps_trk[0], fat))
                    fb, fbt = ft_rot.next()
                    Sx.recip(fb[:], ps[3][:], r=(), w=(ps_trk[3], fbt))
                    Sx.tt(DVE, fb[:], ps[2][:], fb[:], ALU.mult, r=(fbt,), w=(ps_trk[2], fbt))
                    Sx.stt(fa[:], fb[:], small[:, 8:9], fa[:], ALU.mult, ALU.add, r=(fbt, fat, small_trk), w=(fat,))
                    pt, ptt = pt_rot.next()
                    Sx.actf(pt[:], fa[:], AF.Square, r=(fat,), w=(ptt,))
                    Sx.mm(ps[7][:], ones_bf[:], pt[:], True, True, r=(ptt, const_trk), w=(ps_trk[7],))
                    fr, frt = rstd_from_psum(7, TW, 1.0 / 128)
                    Sx.stt(tokT[:, h_, :], fa[:], small[:, 7:8], fr[:], ALU.mult, ALU.mult, r=(fat, frt, small_trk), w=(tok_trk[h_],))
                mem_attn(tt)
                out_proj(b, i, tt)
            release(kvs)

        for b in range(nb):
            reg_ffn(b, 0, 0)
            reg_slabs(("memkv", b, 0), w_mem_kv[0], 0, 512)
            reg_slabs(("a_in", b), a_w_in[0], 0, 1792)
            for tt in range(TT):
                reg_wout(("wout", b, 0, tt), w_out[0])
            reg_ffn(b, 0, 1)
            reg_ffn(b, 1, 0)
            reg_slabs(("memkv", b, 1), w_mem_kv[1], 0, 512)
            wreg(("kv", b), 12, lambda sids: None)
            reg_slabs(("b_k", b), b_w_in[0], 768, 768)
            reg_slabs(("b_v", b), b_w_in[0], 1536, 768)
            for tt in range(TT):
                reg_slabs(("b_q", b, tt), b_w_in[0], 0, 768)
                reg_slabs(("b_qm", b, tt), b_w_in[0], 2304, 256)
                reg_wout(("wout", b, 1, tt), w_out[1])
            reg_ffn(b, 1, 1)

        for b in range(nb):
            load_x(b)
            st = 0
            for i in range(2):
                for sub in range(3):
                    if st >= stages:
                        break
                    if sub == 0:
                        ffn(b, i, 0)
                    elif sub == 1:
                        (mixer_a if i == 0 else mixer_b)(b)
                    else:
                        ffn(b, i, 1)
                    st += 1
            store_x(b)
        Sx.wait_all_dma(SP)

        block = es.enter_context(nc.Block())

        @block.tensor
        def _(h):
            for f in PE.prog:
                f(h)

        @block.scalar
        def _(h):
            for f in ACT.prog:
                f(h)

        @block.vector
        def _(h):
            for f in DVE.prog:
                f(h)

        @block.gpsimd
        def _(h):
            for f in POOL.prog:
                f(h)

        @block.sync
        def _(h):
            for f in SP.prog:
                f(h)
    return nc


_NC_CACHE = {}


def kernel(**inputs):
    x = np.ascontiguousarray(inputs["x"], dtype=np.float32)
    mem = np.ascontiguousarray(inputs["mem"], dtype=np.float32)
    if "nc" not in _NC_CACHE:
        _NC_CACHE["nc"] = build_nc()
    nc = _NC_CACHE["nc"]
    shared = {k: np.ascontiguousarray(v, dtype=np.float32) for k, v in inputs.items() if k not in ("x", "mem")}
    in_maps = []
    for c in range(N_CORES):
        m = dict(shared)
        m["x"] = x[c * NB_CORE:(c + 1) * NB_CORE]
        m["mem"] = mem[c * NB_CORE:(c + 1) * NB_CORE]
        in_maps.append(m)
    res = run_bass_kernel_spmd(nc, in_maps, core_ids=list(range(N_CORES)))
    return np.concatenate([r["y"] for r in res.results], axis=0)
```

```python
import math
from contextlib import ExitStack

import numpy as np
import concourse.bass as bass
import concourse.mybir as mybir
from concourse.bass_utils import run_bass_kernel_spmd

F32 = mybir.dt.float32
BF16 = mybir.dt.bfloat16
AF = mybir.ActivationFunctionType
ALU = mybir.AluOpType
AX = mybir.AxisListType

N_CORES = 8
D = 1024
S = 2048
MEM_L = 256
KC = 8
TT = 4
TW = 512
DFF = 2816
FC = 22
TOKW = 768
EPS = 1e-6
NB_CORE = 2
GROUP = 4
NDSEM = 12


class Trk:
    __slots__ = ("w", "r")

    def __init__(self):
        self.w = None
        self.r = {}


class Eng:
    def __init__(self, name, sem, is_pe=False):
        self.name = name
        self.sem = sem
        self.cnt = 0
        self.seen = {}
        self.prog = []
        self.is_pe = is_pe
        self.dsems = []
        self.dvals = []
        self.dnext = 0


class Sched:
    def __init__(self, nc, es):
        self.nc = nc
        self.engs = {}
        for name in ("pe", "act", "dve", "pool", "sp"):
            sem = es.enter_context(nc.semaphore("tl_" + name))
            self.engs[name] = Eng(name, sem, is_pe=(name == "pe"))
        for name in ("pool", "sp"):
            e = self.engs[name]
            for i in range(NDSEM):
                e.dsems.append(es.enter_context(nc.semaphore(f"d_{name}{i}")))
                e.dvals.append(0)
        self.pe, self.act, self.dve, self.pool, self.sp = (self.engs[n] for n in ("pe", "act", "dve", "pool", "sp"))

    def _wait(self, eng, tk):
        sem, val, src = tk
        key = sem.num
        if eng.seen.get(key, 0) >= val:
            return
        eng.seen[key] = val
        eng.prog.append(lambda h, sem=sem, val=val: h.wait_ge(sem, val))

    def _deps(self, eng, r, w):
        for t in r:
            tk = t.w
            if tk is not None:
                if tk[2] is eng:
                    if not eng.is_pe:
                        self._wait(eng, tk)
                else:
                    self._wait(eng, tk)
        for t in w:
            tk = t.w
            if tk is not None and tk[2] is not eng:
                self._wait(eng, tk)
            for tk in t.r.values():
                if tk[2] is not eng:
                    self._wait(eng, tk)

    def _mark(self, tk, r, w, key):
        for t in w:
            t.w = tk
            t.r = {}
        for t in r:
            t.r[key] = tk

    def op(self, eng, fn, r=(), w=()):
        self._deps(eng, r, w)
        eng.cnt += 1
        sem = eng.sem
        eng.prog.append(lambda h, fn=fn, sem=sem: fn(h).then_inc(sem, 1))
        tk = (sem, eng.cnt, eng)
        self._mark(tk, r, w, eng.name)
        return tk

    def dma(self, eng, out, in_, r=(), w=(), allow=False):
        self._deps(eng, r, w)
        i = eng.dnext
        eng.dnext = (i + 1) % len(eng.dsems)
        sem = eng.dsems[i]
        if eng.dvals[i] > 0:
            self._wait(eng, (sem, eng.dvals[i], None))
        eng.dvals[i] += 16
        val = eng.dvals[i]
        if allow:
            eng.prog.append(lambda h, out=out, in_=in_, sem=sem: h.dma_start(out=out, in_=in_, allow_slow_non_contiguous=True).then_inc(sem, 16))
        else:
            eng.prog.append(lambda h, out=out, in_=in_, sem=sem: h.dma_start(out=out, in_=in_).then_inc(sem, 16))
        tk = (sem, val, None)
        self._mark(tk, r, w, "dma_%s_%d" % (eng.name, i))
        return tk

    def wait_all_dma(self, eng):
        for e in (self.pool, self.sp):
            for i, sem in enumerate(e.dsems):
                if e.dvals[i] > 0:
                    self._wait(eng, (sem, e.dvals[i], None))

    def mm(self, out, lhsT, rhs, start, stop, r, w):
        return self.op(self.pe, lambda h: h.matmul(out, lhsT, rhs, start=start, stop=stop), r=r, w=w)

    def transpose(self, out, in_, ident, r, w):
        return self.op(self.pe, lambda h: h.transpose(out, in_, ident), r=r, w=w)

    def actf(self, out, in_, func, r, w, scale=None, bias=None):
        kw = {}
        if scale is not None:
            kw["scale"] = scale
        if bias is not None:
            kw["bias"] = bias
        return self.op(self.act, lambda h: h.activation(out=out, in_=in_, func=func, **kw), r=r, w=w)

    def tt(self, eng, out, in0, in1, op, r, w):
        return self.op(eng, lambda h: h.tensor_tensor(out=out, in0=in0, in1=in1, op=op), r=r, w=w)

    def ts(self, eng, out, in0, s1, s2, op0, op1, r, w):
        if op1 is None:
            return self.op(eng, lambda h: h.tensor_scalar(out=out, in0=in0, scalar1=s1, scalar2=None, op0=op0), r=r, w=w)
        return self.op(eng, lambda h: h.tensor_scalar(out=out, in0=in0, scalar1=s1, scalar2=s2, op0=op0, op1=op1), r=r, w=w)

    def stt(self, out, in0, scalar, in1, op0, op1, r, w):
        return self.op(self.dve, lambda h: h.scalar_tensor_tensor(out=out, in0=in0, scalar=scalar, in1=in1, op0=op0, op1=op1), r=r, w=w)

    def recip(self, out, in_, r, w):
        return self.op(self.dve, lambda h: h.reciprocal(out=out, in_=in_), r=r, w=w)

    def copy(self, eng, out, in_, r, w):
        if eng is self.act:
            return self.op(eng, lambda h: h.copy(out=out, in_=in_), r=r, w=w)
        return self.op(eng, lambda h: h.tensor_copy(out=out, in_=in_), r=r, w=w)


class Rot:
    def __init__(self, items):
        self.items = items
        self.i = 0

    def next(self):
        it = self.items[self.i]
        self.i = (self.i + 1) % len(self.items)
        return it


def build_nc(nb=NB_CORE, stages=6, nslot=None):
    nc = bass.Bass("TRN2", target_bir_lowering=False)
    dr = {}

    def din(name, shape):
        dr[name] = nc.dram_tensor(name, list(shape), F32, kind="ExternalInput").ap()
        return dr[name]

    x_d = din("x", (nb, S, D))
    mem_d = din("mem", (nb, MEM_L, D))
    ffn_norm = din("ffn_norm", (2, 2, D))
    ffn_w_in = din("ffn_w_in", (2, 2, D, 2 * DFF))
    ffn_w_out = din("ffn_w_out", (2, 2, DFF, D))
    mix_norm = din("mix_norm", (2, D))
    mem_norm = din("mem_norm", (2, D))
    w_mem_kv = din("w_mem_kv", (2, D, 512))
    memq_norm = din("memq_norm", (2, 64))
    memk_norm = din("memk_norm", (2, 64))
    w_out = din("w_out", (2, D, D))
    a_w_in = din("a_w_in", (1, D, 1792))
    a_v_norm = din("a_v_norm", (1, TOKW))
    a_w_s = din("a_w_s", (1, 6, 128, 128))
    a_b_s = din("a_b_s", (1, 6, 128))
    b_w_in = din("b_w_in", (1, D, 2560))
    b_q_norm = din("b_q_norm", (1, 64))
    b_k_norm = din("b_k_norm", (1, 64))
    b_lambda = din("b_lambda", (1, 4, 64))
    b_subln = din("b_subln", (1, 128))
    y_d = nc.dram_tensor("y", [nb, S, D], F32, kind="ExternalOutput").ap()

    es = ExitStack()
    with es:
        def sb(name, shape, dt):
            return es.enter_context(nc.sbuf_tensor(name, list(shape), dt))

        xT = sb("xT", (128, KC, S), F32)
        hT = sb("hT", (128, KC, S), BF16)
        Abuf = [sb(f"A{i}", (128, GROUP, TW), BF16) for i in range(2)]
        tokT = sb("tokT", (128, 8, TW), BF16)
        qT = sb("qT", (128, 8, TW), BF16)
        ptl = [sb(f"pt{i}", (128, TW), BF16) for i in range(3)]
        ftl = [sb(f"ft{i}", (128, TW), F32) for i in range(4)]
        ident = sb("ident", (128, 128), F32)
        ones_bf = sb("ones_bf", (128, 128), BF16)
        blk_bf = sb("blk_bf", (128, 128), BF16)
        mask_bf = sb("mask_bf", (128, 128), BF16)
        wsT = sb("wsT", (128, 6, 128), BF16)
        gcols = sb("gcols", (128, 64), F32)
        avn_bc = sb("avn_bc", (128, TOKW), F32)
        abs_bc = sb("abs_bc", (128, TOKW), F32)
        small = sb("small", (128, 32), F32)
        negh = sb("negh", (128, 4), F32)
        gstl = [sb(f"gst{i}", (128, 8), F32) for i in range(2)]
        kmT = sb("kmT", (128, 2, MEM_L), BF16)
        vm = sb("vm", (128, 2, 256), BF16)
        if nslot is None:
            nslot = 16
        slots = [sb(f"slot{i}", (128, 2048), BF16) for i in range(nslot)]
        ps = [es.enter_context(nc.psum_tensor(f"ps{i}", [128, TW], F32)) for i in range(8)]

        Sx = Sched(nc, es)
        PE, ACT, DVE, POOL, SP = Sx.pe, Sx.act, Sx.dve, Sx.pool, Sx.sp

        x_trk = [[Trk() for _ in range(TT)] for _ in range(KC)]
        h_trk = [[Trk() for _ in range(TT)] for _ in range(KC)]
        A_trk = [Trk(), Trk()]
        tok_trk = [Trk() for _ in range(8)]
        q_trk = [Trk() for _ in range(8)]
        ps_trk = [Trk() for _ in range(8)]
        slot_trk = [Trk() for _ in range(nslot)]
        const_trk = Trk()
        km_trk = Trk()
        vm_trk = Trk()
        small_trk = Trk()
        pt_rot = Rot([(ptl[i], Trk()) for i in range(3)])
        ft_rot = Rot([(ftl[i], Trk()) for i in range(4)])
        gst_rot = Rot([(gstl[i], Trk()) for i in range(2)])

        free_slots = list(range(nslot))

        def alloc(n):
            assert len(free_slots) >= n, "slot pool exhausted"
            got = free_slots[:n]
            del free_slots[:n]
            return got

        def release(ids):
            free_slots.extend(ids)

        wq = []
        wq_pos = [0]
        wq_map = {}

        def wreg(key, n, emit):
            ent = {"key": key, "n": n, "emit": emit, "slots": None}
            wq.append(ent)
            wq_map[key] = ent

        def pump(reserve=4):
            while wq_pos[0] < len(wq):
                ent = wq[wq_pos[0]]
                if len(free_slots) - reserve < ent["n"]:
                    break
                ent["slots"] = alloc(ent["n"])
                ent["emit"](ent["slots"])
                wq_pos[0] += 1

        def want(key):
            ent = wq_map[key]
            while ent["slots"] is None:
                nxt = wq[wq_pos[0]]
                nxt["slots"] = alloc(nxt["n"])
                nxt["emit"](nxt["slots"])
                wq_pos[0] += 1
            return ent["slots"]

        def wdma(slot_id, out_ap, in_ap):
            Sx.dma(POOL, out_ap, in_ap, r=(), w=(slot_trk[slot_id],))

        def slot3(sid, a, b):
            return slots[sid][:].rearrange("p (a b) -> p a b", a=a, b=b)

        def gload(col, vec_ap):
            Sx.dma(SP, gcols[:, col:col + KC], vec_ap.rearrange("(kc p) -> p kc", p=128), r=(), w=(const_trk,), allow=True)

        for i in range(2):
            for j in range(2):
                gload((i * 2 + j) * 8, ffn_norm[i, j])
            gload(32 + 8 * i, mix_norm[i])
            gload(48 + 8 * i, mem_norm[i])

        def hload(col, vec_ap):
            v = vec_ap.rearrange("(p o) -> p o", o=1)
            Sx.dma(SP, small[0:64, col:col + 1], v, r=(), w=(small_trk,), allow=True)
            Sx.dma(SP, small[64:128, col:col + 1], v, r=(), w=(small_trk,), allow=True)

        hload(1, memq_norm[0]); hload(2, memq_norm[1]); hload(3, memk_norm[0]); hload(4, memk_norm[1])
        hload(5, b_q_norm[0]); hload(6, b_k_norm[0])
        Sx.dma(SP, small[:, 7:8], b_subln[0].rearrange("(p o) -> p o", o=1), r=(), w=(small_trk,), allow=True)
        Sx.dma(SP, avn_bc[:], a_v_norm[0:1, :].broadcast_to([128, TOKW]), r=(), w=(const_trk,))
        Sx.dma(SP, abs_bc[:], a_b_s[0].rearrange("g t -> (g t)").rearrange("(o n) -> o n", o=1).broadcast_to([128, TOKW]), r=(), w=(const_trk,))

        Sx.op(POOL, lambda h: h.memset(small[:, 0:1], EPS), r=(), w=(small_trk,))
        Sx.op(POOL, lambda h: h.memset(ones_bf[:], 1.0), r=(), w=(const_trk,))
        Sx.op(POOL, lambda h: h.memset(negh[:], -0.5), r=(), w=(const_trk,))
        Sx.op(POOL, lambda h: h.memset(ident[:], 1.0), r=(), w=(const_trk,))
        Sx.op(POOL, lambda h: h.affine_select(out=ident[:], in_=ident[:], pattern=[[-1, 128]], compare_op=ALU.is_equal, fill=0.0, base=0, channel_multiplier=1), r=(), w=(const_trk,))
        Sx.op(POOL, lambda h: h.memset(mask_bf[:], 1.0), r=(), w=(const_trk,))
        Sx.op(POOL, lambda h: h.affine_select(out=mask_bf[:], in_=mask_bf[:], pattern=[[1, 128]], compare_op=ALU.is_ge, fill=0.0, base=0, channel_multiplier=-1), r=(), w=(const_trk,))
        Sx.op(POOL, lambda h: h.memset(blk_bf[:], 0.0), r=(), w=(const_trk,))
        Sx.op(POOL, lambda h: h.memset(blk_bf[0:64, 0:64], 1.0), r=(), w=(const_trk,))
        Sx.op(POOL, lambda h: h.memset(blk_bf[64:128, 64:128], 1.0), r=(), w=(const_trk,))

        lambda_init = 0.8 - 0.6 * math.exp(-0.3 * 1)
        lam_bc, lamt = ft_rot.next()
        Sx.dma(SP, lam_bc[:, 0:256], b_lambda[0].rearrange("a b -> (a b)").rearrange("(o n) -> o n", o=1).broadcast_to([128, 256]), r=(), w=(lamt,))
        f0, f0t = ft_rot.next()
        Sx.tt(DVE, f0[:, 0:64], lam_bc[:, 0:64], lam_bc[:, 64:128], ALU.mult, r=(lamt,), w=(f0t,))
        Sx.tt(DVE, f0[:, 64:128], lam_bc[:, 128:192], lam_bc[:, 192:256], ALU.mult, r=(lamt,), w=(f0t,))
        Sx.op(DVE, lambda h: h.tensor_reduce(out=small[:, 9:11], in_=f0[:, 0:128].rearrange("p (a b) -> p a b", a=2), op=ALU.add, axis=AX.X), r=(f0t,), w=(small_trk,))
        Sx.actf(small[:, 11:13], small[:, 9:11], AF.Exp, r=(small_trk,), w=(small_trk,))
        Sx.tt(DVE, small[:, 8:9], small[:, 12:13], small[:, 11:12], ALU.subtract, r=(small_trk,), w=(small_trk,))
        Sx.ts(DVE, small[:, 8:9], small[:, 8:9], -lambda_init, None, ALU.add, None, r=(small_trk,), w=(small_trk,))
        Sx.ts(DVE, small[:, 7:8], small[:, 7:8], 1.0 - lambda_init, None, ALU.mult, None, r=(small_trk,), w=(small_trk,))

        (sid,) = alloc(1)
        wsf = slots[sid][:].bitcast(F32).rearrange("p (g s) -> p g s", g=8)
        Sx.dma(SP, wsf[:, 0:6, :], a_w_s[0].rearrange("g t s -> t g s"), r=(), w=(slot_trk[sid],))
        for g in range(6):
            pb = g % 2
            Sx.transpose(ps[pb][:, 0:128], wsf[:, g, :], ident[:], r=(slot_trk[sid], const_trk), w=(ps_trk[pb],))
            Sx.tt(DVE, wsT[:, g, :], ps[pb][:, 0:128], mask_bf[:], ALU.mult, r=(const_trk,), w=(ps_trk[pb], const_trk))
        release([sid])

        eps_col = small[:, 0:1]

        def rstd_from_psum(pbank, ncols, scale, nparts=128):
            ft, ftt = ft_rot.next()
            Sx.actf(ft[0:nparts, 0:ncols], ps[pbank][0:nparts, 0:ncols], AF.Ln, r=(small_trk,), w=(ps_trk[pbank], ftt), scale=scale, bias=eps_col[0:nparts, :])
            Sx.actf(ft[0:nparts, 0:ncols], ft[0:nparts, 0:ncols], AF.Exp, r=(ftt,), w=(ftt,), scale=-0.5)
            return ft, ftt

        def norm_tile(tt, gcol0):
            t0 = tt * TW
            pb = 7
            for kc in range(KC):
                pt, ptt = pt_rot.next()
                Sx.actf(pt[:], xT[:, kc, t0:t0 + TW], AF.Square, r=(x_trk[kc][tt],), w=(ptt,))
                Sx.mm(ps[pb][:], ones_bf[:], pt[:], kc == 0, kc == KC - 1, r=(ptt, const_trk), w=(ps_trk[pb],))
            ft, ftt = rstd_from_psum(pb, TW, 1.0 / D)
            for kc in range(KC):
                Sx.stt(hT[:, kc, t0:t0 + TW], xT[:, kc, t0:t0 + TW], gcols[:, gcol0 + kc:gcol0 + kc + 1], ft[:], ALU.mult, ALU.mult,
                       r=(x_trk[kc][tt], ftt, const_trk), w=(h_trk[kc][tt],))

        def head_norm(pbank_raw, pbank_stat, ncols, gcol, out_ap, out_trk):
            pt, ptt = pt_rot.next()
            Sx.actf(pt[:, 0:ncols], ps[pbank_raw][:, 0:ncols], AF.Square, r=(), w=(ps_trk[pbank_raw], ptt))
            Sx.mm(ps[pbank_stat][:, 0:ncols], blk_bf[:], pt[:, 0:ncols], True, True, r=(ptt, const_trk), w=(ps_trk[pbank_stat],))
            ft, ftt = rstd_from_psum(pbank_stat, ncols, 1.0 / 64)
            Sx.stt(out_ap, ps[pbank_raw][:, 0:ncols], gcol, ft[:, 0:ncols], ALU.mult, ALU.mult,
                   r=(ftt, small_trk), w=(ps_trk[pbank_raw], out_trk))

        def load_x(b):
            for tt in range(TT):
                sids = alloc(4)
                for n in range(4):
                    tc_ = tt * 4 + n
                    Sx.dma(SP, slots[sids[n]][:].bitcast(F32), x_d[b, tc_ * 128:(tc_ + 1) * 128, :], r=(), w=(slot_trk[sids[n]],))
                for kc in range(KC):
                    pb = kc % 4
                    for n in range(4):
                        src = slots[sids[n]][:].bitcast(F32)
                        Sx.transpose(ps[pb][:, n * 128:(n + 1) * 128], src[:, kc * 128:(kc + 1) * 128], ident[:],
                                     r=(slot_trk[sids[n]], const_trk), w=(ps_trk[pb],))
                    eng = ACT if kc % 2 == 0 else DVE
                    Sx.copy(eng, xT[:, kc, tt * TW:(tt + 1) * TW], ps[pb][:], r=(), w=(ps_trk[pb], x_trk[kc][tt]))
                release(sids)

        def store_x(b):
            for tc_ in range(16):
                tt = tc_ // 4
                (sid,) = alloc(1)
                dst = slots[sid][:].bitcast(F32)
                for half in range(2):
                    pb = 4 + (tc_ * 2 + half) % 4
                    for q in range(4):
                        kc = half * 4 + q
                        Sx.transpose(ps[pb][:, q * 128:(q + 1) * 128], xT[:, kc, tc_ * 128:(tc_ + 1) * 128], ident[:],
                                     r=(x_trk[kc][tt], const_trk), w=(ps_trk[pb],))
                    eng = ACT if half == 0 else DVE
                    Sx.copy(eng, dst[:, half * 512:(half + 1) * 512], ps[pb][:], r=(), w=(ps_trk[pb], slot_trk[sid]))
                Sx.dma(SP, y_d[b, tc_ * 128:(tc_ + 1) * 128, :], dst, r=(slot_trk[sid],), w=())
                release([sid])

        def reg_ffn(b, i, j):
            win = ffn_w_in[i, j].rearrange("(kc p) f -> p kc f", p=128)
            wout = ffn_w_out[i, j]
            ngrp = (FC + GROUP - 1) // GROUP
            for g in range(ngrp):
                f0 = g * GROUP
                nf = min(GROUP, FC - f0)
                nsl = (nf + 1) // 2

                def emit(sids, f0=f0, nf=nf, nsl=nsl):
                    for s_ in range(nsl):
                        c0 = (f0 + 2 * s_) * 128
                        w_ = min(256, (f0 + nf) * 128 - c0)
                        wdma(sids[s_], slot3(sids[s_], 8, 256)[:, :, 0:w_], win[:, :, c0:c0 + w_])
                    for s_ in range(nsl):
                        c0 = DFF + (f0 + 2 * s_) * 128
                        w_ = min(256, DFF + (f0 + nf) * 128 - c0)
                        wdma(sids[nsl + s_], slot3(sids[nsl + s_], 8, 256)[:, :, 0:w_], win[:, :, c0:c0 + w_])
                    for s_ in range(nsl):
                        r0 = (f0 + 2 * s_) * 128
                        nr = min(2, f0 + nf - (f0 + 2 * s_))
                        wdma(sids[2 * nsl + s_], slot3(sids[2 * nsl + s_], 2, 1024)[:, 0:nr, :],
                             wout[r0:r0 + nr * 128, :].rearrange("(fc p) d -> p fc d", p=128))
                wreg(("ffn", b, i, j, g), 3 * nsl, emit)

        def reg_slabs(key, w2d, col0, ncol):
            wv = w2d.rearrange("(kc p) f -> p kc f", p=128)
            for s_ in range(ncol // 256):
                def emit(sids, s_=s_):
                    c0 = col0 + s_ * 256
                    wdma(sids[0], slot3(sids[0], 8, 256), wv[:, :, c0:c0 + 256])
                wreg(key + (s_,), 1, emit)

        def reg_wout(key, w2d):
            for s_ in range(4):
                def emit(sids, s_=s_):
                    wdma(sids[0], slot3(sids[0], 2, 1024), w2d[s_ * 256:(s_ + 1) * 256, :].rearrange("(c p) d -> p c d", p=128))
                wreg(key + (s_,), 1, emit)

        ycnt = [0]
        gucnt = [0]

        def ffn(b, i, j):
            for tt in range(TT):
                norm_tile(tt, (i * 2 + j) * 8)
            ngrp = (FC + GROUP - 1) // GROUP
            pending = [None]
            acnt = [0]
            for g in range(ngrp):
                f0 = g * GROUP
                nf = min(GROUP, FC - f0)
                nsl = (nf + 1) // 2
                sids = want(("ffn", b, i, j, g))
                pump(reserve=4)
                gate_s, up_s, out_s = sids[0:nsl], sids[nsl:2 * nsl], sids[2 * nsl:3 * nsl]
                for tt in range(TT):
                    t0 = tt * TW
                    ab = acnt[0] % 2
                    acnt[0] += 1
                    A = Abuf[ab]
                    for q in range(nf):
                        pg = (gucnt[0] % 2) * 2
                        pu = pg + 1
                        gucnt[0] += 1
                        gs = slot3(gate_s[q // 2], 8, 256)
                        us = slot3(up_s[q // 2], 8, 256)
                        co = (q % 2) * 128
                        for kc in range(KC):
                            Sx.mm(ps[pg][:], gs[:, kc, co:co + 128], hT[:, kc, t0:t0 + TW], kc == 0, kc == KC - 1,
                                  r=(slot_trk[gate_s[q // 2]], h_trk[kc][tt]), w=(ps_trk[pg],))
                        for kc in range(KC):
                            Sx.mm(ps[pu][:], us[:, kc, co:co + 128], hT[:, kc, t0:t0 + TW], kc == 0, kc == KC - 1,
                                  r=(slot_trk[up_s[q // 2]], h_trk[kc][tt]), w=(ps_trk[pu],))
                        ft, ftt = ft_rot.next()
                        Sx.actf(ft[:], ps[pg][:], AF.Silu, r=(), w=(ps_trk[pg], ftt))
                        Sx.tt(DVE, A[:, q, :], ft[:], ps[pu][:], ALU.mult, r=(ftt,), w=(ps_trk[pu], A_trk[ab]))
                    if pending[0] is not None:
                        pending[0]()

                    def ywork(tt=tt, t0=t0, A=A, ab=ab, nf=nf, out_s=out_s):
                        for dc in range(KC):
                            py = 4 + ycnt[0] % 3
                            ycnt[0] += 1
                            for q in range(nf):
                                ws_ = slot3(out_s[q // 2], 2, 1024)
                                Sx.mm(ps[py][:], ws_[:, q % 2, dc * 128:(dc + 1) * 128], A[:, q, :], q == 0, q == nf - 1,
                                      r=(slot_trk[out_s[q // 2]], A_trk[ab]), w=(ps_trk[py],))
                            Sx.stt(xT[:, dc, t0:t0 + TW], ps[py][:], 0.5, xT[:, dc, t0:t0 + TW], ALU.mult, ALU.add,
                                   r=(), w=(ps_trk[py], x_trk[dc][tt]))
                    pending[0] = ywork
                if g == ngrp - 1:
                    pending[0]()
                    pending[0] = None
                    release(sids)
                else:
                    pending[0]()
                    pending[0] = None
                    release(sids)

        def mem_kv(b, i):
            sm = alloc(2)
            (sh,) = alloc(1)
            mh = slot3(sh, 8, 256)
            for lc in range(2):
                mf = slots[sm[lc]][:].bitcast(F32)
                Sx.dma(SP, mf, mem_d[b, lc * 128:(lc + 1) * 128, :], r=(), w=(slot_trk[sm[lc]],))
                ft, ftt = ft_rot.next()
                for hf in range(2):
                    Sx.tt(DVE, ft[:], mf[:, hf * 512:(hf + 1) * 512], mf[:, hf * 512:(hf + 1) * 512], ALU.mult, r=(slot_trk[sm[lc]],), w=(ftt,))
                    Sx.op(DVE, lambda h, ft=ft, hf=hf: h.tensor_reduce(out=small[:, 16 + hf:17 + hf], in_=ft[:], op=ALU.add, axis=AX.X), r=(ftt,), w=(small_trk,))
                Sx.tt(DVE, small[:, 18:19], small[:, 16:17], small[:, 17:18], ALU.add, r=(small_trk,), w=(small_trk,))
                Sx.actf(small[:, 19:20], small[:, 18:19], AF.Ln, r=(small_trk,), w=(small_trk,), scale=1.0 / D, bias=eps_col)
                Sx.actf(small[:, 20:21], small[:, 19:20], AF.Exp, r=(small_trk,), w=(small_trk,), scale=-0.5)
                Sx.ts(DVE, mf, mf, small[:, 20:21], None, ALU.mult, None, r=(small_trk, slot_trk[sm[lc]]), w=(slot_trk[sm[lc]],))
            for kc in range(KC):
                pb = kc % 2
                for lc in range(2):
                    mf = slots[sm[lc]][:].bitcast(F32)
                    Sx.transpose(ps[pb][:, lc * 128:(lc + 1) * 128], mf[:, kc * 128:(kc + 1) * 128], ident[:],
                                 r=(slot_trk[sm[lc]], const_trk), w=(ps_trk[pb],))
                Sx.ts(DVE, mh[:, kc, :], ps[pb][:, 0:256], gcols[:, 48 + 8 * i + kc:48 + 8 * i + kc + 1], None, ALU.mult, None,
                      r=(const_trk,), w=(ps_trk[pb], slot_trk[sh]))
            release(sm)
            ks = want(("memkv", b, i, 0))[0]
            vs = want(("memkv", b, i, 1))[0]
            kw = slot3(ks, 8, 256)
            vw = slot3(vs, 8, 256)
            for hc in range(2):
                pr = 2 + hc
                for kc in range(KC):
                    Sx.mm(ps[pr][:, 0:256], kw[:, kc, hc * 128:(hc + 1) * 128], mh[:, kc, :], kc == 0, kc == KC - 1,
                          r=(slot_trk[ks], slot_trk[sh]), w=(ps_trk[pr],))
                head_norm(pr, 7, 256, small[:, 3 + i:4 + i], kmT[:, hc, :], km_trk)
            for lc in range(2):
                pr = 4 + lc
                for kc in range(KC):
                    Sx.mm(ps[pr][:, 0:256], mh[:, kc, lc * 128:(lc + 1) * 128], vw[:, kc, :], kc == 0, kc == KC - 1,
                          r=(slot_trk[vs], slot_trk[sh]), w=(ps_trk[pr],))
                Sx.copy(ACT, vm[:, lc, :], ps[pr][:, 0:256], r=(), w=(ps_trk[pr], vm_trk))
            release([ks, vs, sh])

        def mem_attn(tt):
            for hc in range(2):
                pnum, pden = 0, 1
                for hh in range(2):
                    hm = hc * 2 + hh
                    r0 = hh * 64
                    for lc in range(2):
                        pss = 4 + (hm * 2 + lc) % 3
                        Sx.mm(ps[pss][:], kmT[r0:r0 + 64, hc, lc * 128:(lc + 1) * 128], qT[r0:r0 + 64, 6 + hc, :], True, True,
                              r=(km_trk, q_trk[6 + hc]), w=(ps_trk[pss],))
                        pt, ptt = pt_rot.next()
                        Sx.actf(pt[:], ps[pss][:], AF.Exp, r=(), w=(ps_trk[pss], ptt), scale=0.125)
                        Sx.mm(ps[pnum][r0:r0 + 64, :], vm[:, lc, hm * 64:(hm + 1) * 64], pt[:], lc == 0, lc == 1,
                              r=(vm_trk, ptt), w=(ps_trk[pnum],))
                        Sx.mm(ps[pden][r0:r0 + 64, :], ones_bf[:, 0:64], pt[:], lc == 0, lc == 1,
                              r=(const_trk, ptt), w=(ps_trk[pden],))
                ft, ftt = ft_rot.next()
                Sx.actf(ft[:], ps[pden][:], AF.Ln, r=(), w=(ps_trk[pden], ftt))
                Sx.actf(ft[:], ft[:], AF.Exp, r=(ftt,), w=(ftt,), scale=-1.0)
                Sx.tt(DVE, tokT[:, 6 + hc, :], ps[pnum][:], ft[:], ALU.mult, r=(ftt,), w=(ps_trk[pnum], tok_trk[6 + hc]))

        def out_proj(b, i, tt):
            t0 = tt * TW
            ws_ = [want(("wout", b, i, tt, s_))[0] for s_ in range(4)]
            for dc in range(KC):
                py = 4 + ycnt[0] % 3
                ycnt[0] += 1
                for c in range(8):
                    wv = slot3(ws_[c // 2], 2, 1024)
                    Sx.mm(ps[py][:], wv[:, c % 2, dc * 128:(dc + 1) * 128], tokT[:, c, :], c == 0, c == 7,
                          r=(slot_trk[ws_[c // 2]], tok_trk[c]), w=(ps_trk[py],))
                Sx.tt(DVE, xT[:, dc, t0:t0 + TW], ps[py][:], xT[:, dc, t0:t0 + TW], ALU.add, r=(), w=(ps_trk[py], x_trk[dc][tt]))
            release(ws_)

        def qm_proj(tt, slab_sid, memq_col):
            t0 = tt * TW
            wv = slot3(slab_sid, 8, 256)
            for hc in range(2):
                pr = 2 + hc
                for kc in range(KC):
                    Sx.mm(ps[pr][:], wv[:, kc, hc * 128:(hc + 1) * 128], hT[:, kc, t0:t0 + TW], kc == 0, kc == KC - 1,
                          r=(slot_trk[slab_sid], h_trk[kc][tt]), w=(ps_trk[pr],))
                head_norm(pr, 7, TW, memq_col, qT[:, 6 + hc, :], q_trk[6 + hc])

        def mixer_a(b):
            i = 0
            for tt in range(TT):
                norm_tile(tt, 32)
            mem_kv(b, i)
            slabs = [want(("a_in", b, s_))[0] for s_ in range(7)]
            pump(reserve=4)
            for tt in range(TT):
                t0 = tt * TW
                qm_proj(tt, slabs[6], small[:, 1:2])
                for g in range(6):
                    us = slot3(slabs[g // 2], 8, 256)
                    vs = slot3(slabs[3 + g // 2], 8, 256)
                    co = (g % 2) * 128
                    pu, pv, pm = 0, 1, 2 + g % 2
                    for kc in range(KC):
                        Sx.mm(ps[pu][:], us[:, kc, co:co + 128], hT[:, kc, t0:t0 + TW], kc == 0, kc == KC - 1,
                              r=(slot_trk[slabs[g // 2]], h_trk[kc][tt]), w=(ps_trk[pu],))
                    for n in range(4):
                        for kc in range(KC):
                            Sx.mm(ps[pv][:, n * 128:(n + 1) * 128], hT[:, kc, t0 + n * 128:t0 + (n + 1) * 128], vs[:, kc, co:co + 128],
                                  kc == 0, kc == KC - 1, r=(slot_trk[slabs[3 + g // 2]], h_trk[kc][tt]), w=(ps_trk[pv],))
                    fu, fut = ft_rot.next()
                    Sx.actf(fu[:], ps[pu][:], AF.Gelu, r=(), w=(ps_trk[pu], fut))
                    fv, fvt = ft_rot.next()
                    Sx.actf(fv[:], ps[pv][:], AF.Gelu, r=(), w=(ps_trk[pv], fvt))
                    fs, fst = ft_rot.next()
                    Sx.tt(DVE, fs[:], fv[:], fv[:], ALU.mult, r=(fvt,), w=(fst,))
                    gs_, gst = gst_rot.next()
                    Sx.op(DVE, lambda h, fs=fs, gs_=gs_: h.tensor_reduce(out=gs_[:, 0:4], in_=fs[:].rearrange("p (a b) -> p a b", a=4), op=ALU.add, axis=AX.X),
                          r=(fst,), w=(gst,))
                    Sx.ts(DVE, gs_[:, 0:4], gs_[:, 0:4], 1.0 / 128, EPS, ALU.mult, ALU.add, r=(gst,), w=(gst,))
                    Sx.tt(POOL, gs_[:, 4:8], gs_[:, 0:4], negh[:, 0:4], ALU.pow, r=(gst, const_trk), w=(gst,))
                    pt, ptt = pt_rot.next()
                    for n in range(4):
                        Sx.stt(pt[:, n * 128:(n + 1) * 128], fv[:, n * 128:(n + 1) * 128], gs_[:, 4 + n:5 + n], avn_bc[:, g * 128:(g + 1) * 128],
                               ALU.mult, ALU.mult, r=(fvt, gst, const_trk), w=(ptt,))
                    for n in range(4):
                        Sx.mm(ps[pm][:, n * 128:(n + 1) * 128], pt[:, n * 128:(n + 1) * 128], wsT[:, g, :], True, True,
                              r=(ptt, const_trk), w=(ps_trk[pm],))
                    Sx.tt(DVE, fs[:].rearrange("p (a b) -> p a b", a=4), ps[pm][:].rearrange("p (a b) -> p a b", a=4),
                          abs_bc[:, g * 128:(g + 1) * 128].unsqueeze(1).broadcast_to([128, 4, 128]), ALU.add,
                          r=(const_trk,), w=(ps_trk[pm], fst))
                    Sx.tt(POOL, tokT[:, g, :], fs[:], fu[:], ALU.mult, r=(fst, fut), w=(tok_trk[g],))
                mem_attn(tt)
                if tt == TT - 1:
                    release(slabs)
                out_proj(b, i, tt)
                pump(reserve=4)

        def mixer_b(b):
            i = 1
            for tt in range(TT):
                norm_tile(tt, 40)
            mem_kv(b, i)
            kvs = want(("kv", b))
            k_s, v_s = kvs[0:6], kvs[6:12]
            kslabs = [want(("b_k", b, s_))[0] for s_ in range(3)]
            for c in range(6):
                wv = slot3(kslabs[c // 2], 8, 256)
                co = (c % 2) * 128
                for tt in range(TT):
                    t0 = tt * TW
                    pr = 2 + (c * TT + tt) % 2
                    for kc in range(KC):
                        Sx.mm(ps[pr][:], wv[:, kc, co:co + 128], hT[:, kc, t0:t0 + TW], kc == 0, kc == KC - 1,
                              r=(slot_trk[kslabs[c // 2]], h_trk[kc][tt]), w=(ps_trk[pr],))
                    head_norm(pr, 7, TW, small[:, 6:7], slots[k_s[c]][:, t0:t0 + TW], slot_trk[k_s[c]])
            release(kslabs)
            vslabs = [want(("b_v", b, s_))[0] for s_ in range(3)]
            for tc_ in range(16):
                tt = tc_ // 4
                for s_ in range(3):
                    wv = slot3(vslabs[s_], 8, 256)
                    pb = 4 + (tc_ * 3 + s_) % 3
                    for kc in range(KC):
                        Sx.mm(ps[pb][:, 0:256], hT[:, kc, tc_ * 128:(tc_ + 1) * 128], wv[:, kc, :], kc == 0, kc == KC - 1,
                              r=(slot_trk[vslabs[s_]], h_trk[kc][tt]), w=(ps_trk[pb],))
                    for hh in range(2):
                        h_ = s_ * 2 + hh
                        vdst = slot3(v_s[h_], 16, 128)
                        Sx.copy(ACT, vdst[:, tc_, :], ps[pb][:, hh * 128:(hh + 1) * 128], r=(), w=(ps_trk[pb], slot_trk[v_s[h_]]))
            release(vslabs)
            for tt in range(TT):
                t0 = tt * TW
                qs = [want(("b_q", b, tt, s_))[0] for s_ in range(3)]
                for c in range(6):
                    wv = slot3(qs[c // 2], 8, 256)
                    co = (c % 2) * 128
                    pr = 2 + c % 2
                    for kc in range(KC):
                        Sx.mm(ps[pr][:], wv[:, kc, co:co + 128], hT[:, kc, t0:t0 + TW], kc == 0, kc == KC - 1,
                              r=(slot_trk[qs[c // 2]], h_trk[kc][tt]), w=(ps_trk[pr],))
                    head_norm(pr, 7, TW, small[:, 5:6], qT[:, c, :], q_trk[c])
                release(qs)
                qms = want(("b_qm", b, tt, 0))[0]
                qm_proj(tt, qms, small[:, 2:3])
                release([qms])
                pump(reserve=4)
                nk = (tt + 1) * 4
                for h_ in range(6):
                    vsl = slot3(v_s[h_], 16, 128)
                    steps = []
                    for c in range(2):
                        for kc in range(nk):
                            steps.append((c, kc))
                    scnt = [0]
                    inflight = []

                    def stageA(c, kc):
                        j = kc - tt * 4
                        c0 = max(j, 0) * 128
                        pss = 4 + scnt[0] % 3
                        scnt[0] += 1
                        Sx.mm(ps[pss][:, c0:TW], slots[k_s[h_]][c * 64:(c + 1) * 64, kc * 128:(kc + 1) * 128], qT[c * 64:(c + 1) * 64, h_, c0:TW],
                              True, True, r=(slot_trk[k_s[h_]], q_trk[h_]), w=(ps_trk[pss],))
                        return (c, kc, j, c0, pss)

                    def stageB(st):
                        c, kc, j, c0, pss = st
                        pt, ptt = pt_rot.next()
                        Sx.actf(pt[:, c0:TW], ps[pss][:, c0:TW], AF.Exp, r=(), w=(ps_trk[pss], ptt), scale=0.125)
                        if j >= 0:
                            Sx.tt(DVE, pt[:, c0:c0 + 128], pt[:, c0:c0 + 128], mask_bf[:], ALU.mult, r=(ptt, const_trk), w=(ptt,))
                        pn, pd = c * 2, c * 2 + 1
                        Sx.mm(ps[pn][:, c0:TW], vsl[:, kc, :], pt[:, c0:TW], kc == 0, kc == nk - 1,
                              r=(slot_trk[v_s[h_]], ptt), w=(ps_trk[pn],))
                        Sx.mm(ps[pd][:, c0:TW], ones_bf[:], pt[:, c0:TW], kc == 0, kc == nk - 1,
                              r=(const_trk, ptt), w=(ps_trk[pd],))

                    for si, (c, kc) in enumerate(steps):
                        inflight.append(stageA(c, kc))
                        if len(inflight) > 2:
                            stageB(inflight.pop(0))
                    while inflight:
                        stageB(inflight.pop(0))
                    fa, fat = ft_rot.next()
                    fb, fbt = ft_rot.next()
                    Sx.actf(fa[:], ps[1][:], AF.Ln, r=(), w=(ps_trk[1], fat))
                    Sx.actf(fb[:], ps[3][:], AF.Ln, r=(), w=(ps_trk[3], fbt))
                    Sx.actf(fa[:], fa[:], AF.Exp, r=(fat,), w=(fat,), scale=-1.0)
                    Sx.actf(fb[:], fb[:], AF.Exp, r=(fbt,), w=(fbt,), scale=-1.0)
                    Sx.tt(DVE, fa[:], ps[0][:], fa[:], ALU.mult, r=(fat,), w=(ps_trk[0], fat))
                    Sx.tt(DVE, fb[:], ps[2][:], fb[:], ALU.mult, r=(fbt,), w=(ps_trk[2], fbt))
                    Sx.stt(fa[:], fb[:], small[:, 8:9], fa[:], ALU.mult, ALU.add, r=(fbt, fat, small_trk), w=(fat,))
                    pt, ptt = pt_rot.next()
                    Sx.tt(DVE, pt[:], fa[:], fa[:], ALU.mult, r=(fat,), w=(ptt,))
                    Sx.mm(ps[7][:], ones_bf[:], pt[:], True, True, r=(ptt, const_trk), w=(ps_trk[7],))
                    fr, frt = rstd_from_psum(7, TW, 1.0 / 128)
                    Sx.stt(tokT[:, h_, :], fa[:], small[:, 7:8], fr[:], ALU.mult, ALU.mult, r=(fat, frt, small_trk), w=(tok_trk[h_],))
                mem_attn(tt)
                out_proj(b, i, tt)
            release(kvs)

        for b in range(nb):
            reg_ffn(b, 0, 0)
            reg_slabs(("memkv", b, 0), w_mem_kv[0], 0, 512)
            reg_slabs(("a_in", b), a_w_in[0], 0, 1792)
            for tt in range(TT):
                reg_wout(("wout", b, 0, tt), w_out[0])
            reg_ffn(b, 0, 1)
            reg_ffn(b, 1, 0)
            reg_slabs(("memkv", b, 1), w_mem_kv[1], 0, 512)
            wreg(("kv", b), 12, lambda sids: None)
            reg_slabs(("b_k", b), b_w_in[0], 768, 768)
            reg_slabs(("b_v", b), b_w_in[0], 1536, 768)
            for tt in range(TT):
                reg_slabs(("b_q", b, tt), b_w_in[0], 0, 768)
                reg_slabs(("b_qm", b, tt), b_w_in[0], 2304, 256)
                reg_wout(("wout", b, 1, tt), w_out[1])
            reg_ffn(b, 1, 1)

        for b in range(nb):
            load_x(b)
            st = 0
            for i in range(2):
                for sub in range(3):
                    if st >= stages:
                        break
                    if sub == 0:
                        ffn(b, i, 0)
                    elif sub == 1:
                        (mixer_a if i == 0 else mixer_b)(b)
                    else:
                        ffn(b, i, 1)
                    st += 1
            store_x(b)
        Sx.wait_all_dma(SP)

        block = es.enter_context(nc.Block())

        @block.tensor
        def _(h):
            for f in PE.prog:
                f(h)

        @block.scalar
        def _(h):
            for f in ACT.prog:
                f(h)

        @block.vector
        def _(h):
            for f in DVE.prog:
                f(h)

        @block.gpsimd
        def _(h):
            for f in POOL.prog:
                f(h)

        @block.sync
        def _(h):
            for f in SP.prog:
                f(h)
    return nc


_NC_CACHE = {}


def kernel(**inputs):
    x = np.ascontiguousarray(inputs["x"], dtype=np.float32)
    mem = np.ascontiguousarray(inputs["mem"], dtype=np.float32)
    if "nc" not in _NC_CACHE:
        _NC_CACHE["nc"] = build_nc()
    nc = _NC_CACHE["nc"]
    shared = {k: np.ascontiguousarray(v, dtype=np.float32) for k, v in inputs.items() if k not in ("x", "mem")}
    in_maps = []
    for c in range(N_CORES):
        m = dict(shared)
        m["x"] = x[c * NB_CORE:(c + 1) * NB_CORE]
        m["mem"] = mem[c * NB_CORE:(c + 1) * NB_CORE]
        in_maps.append(m)
    res = run_bass_kernel_spmd(nc, in_maps, core_ids=list(range(N_CORES)))
    return np.concatenate([r["y"] for r in res.results], axis=0)
```

```python
import math
from contextlib import ExitStack

import numpy as np
import concourse.bass as bass
import concourse.mybir as mybir
from concourse.bass_utils import run_bass_kernel_spmd

F32 = mybir.dt.float32
BF16 = mybir.dt.bfloat16
AF = mybir.ActivationFunctionType
ALU = mybir.AluOpType
AX = mybir.AxisListType

N_CORES = 8
D = 1024
S = 2048
MEM_L = 256
KC = 8
TT = 4
TW = 512
DFF = 2816
FC = 22
TOKW = 768
EPS = 1e-6
NB_CORE = 2
GROUP = 4
NDSEM = 12


class Trk:
    __slots__ = ("w", "r")

    def __init__(self):
        self.w = None
        self.r = {}


class Eng:
    def __init__(self, name, sem, is_pe=False):
        self.name = name
        self.sem = sem
        self.cnt = 0
        self.seen = {}
        self.prog = []
        self.is_pe = is_pe
        self.dsems = []
        self.dvals = []
        self.dnext = 0


class Sched:
    def __init__(self, nc, es):
        self.nc = nc
        self.engs = {}
        for name in ("pe", "act", "dve", "pool", "sp"):
            sem = es.enter_context(nc.semaphore("tl_" + name))
            self.engs[name] = Eng(name, sem, is_pe=(name == "pe"))
        for name in ("pool", "sp"):
            e = self.engs[name]
            for i in range(NDSEM):
                e.dsems.append(es.enter_context(nc.semaphore(f"d_{name}{i}")))
                e.dvals.append(0)
        self.pe, self.act, self.dve, self.pool, self.sp = (self.engs[n] for n in ("pe", "act", "dve", "pool", "sp"))

    def _wait(self, eng, tk):
        sem, val, src = tk
        key = sem.num
        if eng.seen.get(key, 0) >= val:
            return
        eng.seen[key] = val
        eng.prog.append(lambda h, sem=sem, val=val: h.wait_ge(sem, val))

    def _deps(self, eng, r, w):
        for t in r:
            tk = t.w
            if tk is not None:
                if tk[2] is eng:
                    if not eng.is_pe:
                        self._wait(eng, tk)
                else:
                    self._wait(eng, tk)
        for t in w:
            tk = t.w
            if tk is not None and tk[2] is not eng:
                self._wait(eng, tk)
            for tk in t.r.values():
                if tk[2] is not eng:
                    self._wait(eng, tk)

    def _mark(self, tk, r, w, key):
        for t in w:
            t.w = tk
            t.r = {}
        for t in r:
            t.r[key] = tk

    def op(self, eng, fn, r=(), w=()):
        self._deps(eng, r, w)
        eng.cnt += 1
        sem = eng.sem
        eng.prog.append(lambda h, fn=fn, sem=sem: fn(h).then_inc(sem, 1))
        tk = (sem, eng.cnt, eng)
        self._mark(tk, r, w, eng.name)
        return tk

    def dma(self, eng, out, in_, r=(), w=(), allow=False):
        self._deps(eng, r, w)
        i = eng.dnext
        eng.dnext = (i + 1) % len(eng.dsems)
        sem = eng.dsems[i]
        if eng.dvals[i] > 0:
            self._wait(eng, (sem, eng.dvals[i], None))
        eng.dvals[i] += 16
        val = eng.dvals[i]
        if allow:
            eng.prog.append(lambda h, out=out, in_=in_, sem=sem: h.dma_start(out=out, in_=in_, allow_slow_non_contiguous=True).then_inc(sem, 16))
        else:
            eng.prog.append(lambda h, out=out, in_=in_, sem=sem: h.dma_start(out=out, in_=in_).then_inc(sem, 16))
        tk = (sem, val, None)
        self._mark(tk, r, w, "dma_%s_%d" % (eng.name, i))
        return tk

    def wait_all_dma(self, eng):
        for e in (self.pool, self.sp):
            for i, sem in enumerate(e.dsems):
                if e.dvals[i] > 0:
                    self._wait(eng, (sem, e.dvals[i], None))

    def mm(self, out, lhsT, rhs, start, stop, r, w):
        return self.op(self.pe, lambda h: h.matmul(out, lhsT, rhs, start=start, stop=stop), r=r, w=w)

    def transpose(self, out, in_, ident, r, w):
        return self.op(self.pe, lambda h: h.transpose(out, in_, ident), r=r, w=w)

    def actf(self, out, in_, func, r, w, scale=None, bias=None):
        kw = {}
        if scale is not None:
            kw["scale"] = scale
        if bias is not None:
            kw["bias"] = bias
        return self.op(self.act, lambda h: h.activation(out=out, in_=in_, func=func, **kw), r=r, w=w)

    def tt(self, eng, out, in0, in1, op, r, w):
        return self.op(eng, lambda h: h.tensor_tensor(out=out, in0=in0, in1=in1, op=op), r=r, w=w)

    def ts(self, eng, out, in0, s1, s2, op0, op1, r, w):
        if op1 is None:
            return self.op(eng, lambda h: h.tensor_scalar(out=out, in0=in0, scalar1=s1, scalar2=None, op0=op0), r=r, w=w)
        return self.op(eng, lambda h: h.tensor_scalar(out=out, in0=in0, scalar1=s1, scalar2=s2, op0=op0, op1=op1), r=r, w=w)

    def stt(self, out, in0, scalar, in1, op0, op1, r, w):
        return self.op(self.dve, lambda h: h.scalar_tensor_tensor(out=out, in0=in0, scalar=scalar, in1=in1, op0=op0, op1=op1), r=r, w=w)

    def recip(self, out, in_, r, w):
        return self.op(self.dve, lambda h: h.reciprocal(out=out, in_=in_), r=r, w=w)

    def copy(self, eng, out, in_, r, w):
        if eng is self.act:
            return self.op(eng, lambda h: h.copy(out=out, in_=in_), r=r, w=w)
        return self.op(eng, lambda h: h.tensor_copy(out=out, in_=in_), r=r, w=w)


class Rot:
    def __init__(self, items):
        self.items = items
        self.i = 0

    def next(self):
        it = self.items[self.i]
        self.i = (self.i + 1) % len(self.items)
        return it


def build_nc(nb=NB_CORE, stages=6, nslot=None):
    nc = bass.Bass("TRN2", target_bir_lowering=False)
    dr = {}

    def din(name, shape):
        dr[name] = nc.dram_tensor(name, list(shape), F32, kind="ExternalInput").ap()
        return dr[name]

    x_d = din("x", (nb, S, D))
    mem_d = din("mem", (nb, MEM_L, D))
    ffn_norm = din("ffn_norm", (2, 2, D))
    ffn_w_in = din("ffn_w_in", (2, 2, D, 2 * DFF))
    ffn_w_out = din("ffn_w_out", (2, 2, DFF, D))
    mix_norm = din("mix_norm", (2, D))
    mem_norm = din("mem_norm", (2, D))
    w_mem_kv = din("w_mem_kv", (2, D, 512))
    memq_norm = din("memq_norm", (2, 64))
    memk_norm = din("memk_norm", (2, 64))
    w_out = din("w_out", (2, D, D))
    a_w_in = din("a_w_in", (1, D, 1792))
    a_v_norm = din("a_v_norm", (1, TOKW))
    a_w_s = din("a_w_s", (1, 6, 128, 128))
    a_b_s = din("a_b_s", (1, 6, 128))
    b_w_in = din("b_w_in", (1, D, 2560))
    b_q_norm = din("b_q_norm", (1, 64))
    b_k_norm = din("b_k_norm", (1, 64))
    b_lambda = din("b_lambda", (1, 4, 64))
    b_subln = din("b_subln", (1, 128))
    y_d = nc.dram_tensor("y", [nb, S, D], F32, kind="ExternalOutput").ap()

    es = ExitStack()
    with es:
        def sb(name, shape, dt):
            return es.enter_context(nc.sbuf_tensor(name, list(shape), dt))

        xT = sb("xT", (128, KC, S), F32)
        hT = sb("hT", (128, KC, S), BF16)
        Abuf = [sb(f"A{i}", (128, GROUP, TW), BF16) for i in range(2)]
        tokT = sb("tokT", (128, 8, TW), BF16)
        qT = sb("qT", (128, 8, TW), BF16)
        ptl = [sb(f"pt{i}", (128, TW), BF16) for i in range(3)]
        ftl = [sb(f"ft{i}", (128, TW), F32) for i in range(4)]
        ident = sb("ident", (128, 128), F32)
        ones_bf = sb("ones_bf", (128, 128), BF16)
        blk_bf = sb("blk_bf", (128, 128), BF16)
        mask_bf = sb("mask_bf", (128, 128), BF16)
        wsT = sb("wsT", (128, 6, 128), BF16)
        gcols = sb("gcols", (128, 64), F32)
        avn_bc = sb("avn_bc", (128, TOKW), F32)
        abs_bc = sb("abs_bc", (128, TOKW), F32)
        small = sb("small", (128, 32), F32)
        negh = sb("negh", (128, 4), F32)
        gstl = [sb(f"gst{i}", (128, 8), F32) for i in range(2)]
        kmT = sb("kmT", (128, 2, MEM_L), BF16)
        vm = sb("vm", (128, 2, 256), BF16)
        if nslot is None:
            nslot = 16
        slots = [sb(f"slot{i}", (128, 2048), BF16) for i in range(nslot)]
        ps = [es.enter_context(nc.psum_tensor(f"ps{i}", [128, TW], F32)) for i in range(8)]

        Sx = Sched(nc, es)
        PE, ACT, DVE, POOL, SP = Sx.pe, Sx.act, Sx.dve, Sx.pool, Sx.sp

        x_trk = [[Trk() for _ in range(TT)] for _ in range(KC)]
        h_trk = [[Trk() for _ in range(TT)] for _ in range(KC)]
        A_trk = [Trk(), Trk()]
        tok_trk = [Trk() for _ in range(8)]
        q_trk = [Trk() for _ in range(8)]
        ps_trk = [Trk() for _ in range(8)]
        slot_trk = [Trk() for _ in range(nslot)]
        const_trk = Trk()
        km_trk = Trk()
        vm_trk = Trk()
        small_trk = Trk()
        pt_rot = Rot([(ptl[i], Trk()) for i in range(3)])
        ft_rot = Rot([(ftl[i], Trk()) for i in range(4)])
        gst_rot = Rot([(gstl[i], Trk()) for i in range(2)])

        free_slots = list(range(nslot))

        def alloc(n):
            assert len(free_slots) >= n, "slot pool exhausted"
            got = free_slots[:n]
            del free_slots[:n]
            return got

        def release(ids):
            free_slots.extend(ids)

        wq = []
        wq_pos = [0]
        wq_map = {}

        def wreg(key, n, emit):
            ent = {"key": key, "n": n, "emit": emit, "slots": None}
            wq.append(ent)
            wq_map[key] = ent

        def pump(reserve=4):
            while wq_pos[0] < len(wq):
                ent = wq[wq_pos[0]]
                if len(free_slots) - reserve < ent["n"]:
                    break
                ent["slots"] = alloc(ent["n"])
                ent["emit"](ent["slots"])
                wq_pos[0] += 1

        def want(key):
            ent = wq_map[key]
            while ent["slots"] is None:
                nxt = wq[wq_pos[0]]
                nxt["slots"] = alloc(nxt["n"])
                nxt["emit"](nxt["slots"])
                wq_pos[0] += 1
            return ent["slots"]

        def wdma(slot_id, out_ap, in_ap):
            Sx.dma(POOL, out_ap, in_ap, r=(), w=(slot_trk[slot_id],))

        def slot3(sid, a, b):
            return slots[sid][:].rearrange("p (a b) -> p a b", a=a, b=b)

        def gload(col, vec_ap):
            Sx.dma(SP, gcols[:, col:col + KC], vec_ap.rearrange("(kc p) -> p kc", p=128), r=(), w=(const_trk,), allow=True)

        for i in range(2):
            for j in range(2):
                gload((i * 2 + j) * 8, ffn_norm[i, j])
            gload(32 + 8 * i, mix_norm[i])
            gload(48 + 8 * i, mem_norm[i])

        def hload(col, vec_ap):
            v = vec_ap.rearrange("(p o) -> p o", o=1)
            Sx.dma(SP, small[0:64, col:col + 1], v, r=(), w=(small_trk,), allow=True)
            Sx.dma(SP, small[64:128, col:col + 1], v, r=(), w=(small_trk,), allow=True)

        hload(1, memq_norm[0]); hload(2, memq_norm[1]); hload(3, memk_norm[0]); hload(4, memk_norm[1])
        hload(5, b_q_norm[0]); hload(6, b_k_norm[0])
        Sx.dma(SP, small[:, 7:8], b_subln[0].rearrange("(p o) -> p o", o=1), r=(), w=(small_trk,), allow=True)
        Sx.dma(SP, avn_bc[:], a_v_norm[0:1, :].broadcast_to([128, TOKW]), r=(), w=(const_trk,))
        Sx.dma(SP, abs_bc[:], a_b_s[0].rearrange("g t -> (g t)").rearrange("(o n) -> o n", o=1).broadcast_to([128, TOKW]), r=(), w=(const_trk,))

        Sx.op(POOL, lambda h: h.memset(small[:, 0:1], EPS), r=(), w=(small_trk,))
        Sx.op(POOL, lambda h: h.memset(ones_bf[:], 1.0), r=(), w=(const_trk,))
        Sx.op(POOL, lambda h: h.memset(negh[:], -0.5), r=(), w=(const_trk,))
        Sx.op(POOL, lambda h: h.memset(ident[:], 1.0), r=(), w=(const_trk,))
        Sx.op(POOL, lambda h: h.affine_select(out=ident[:], in_=ident[:], pattern=[[-1, 128]], compare_op=ALU.is_equal, fill=0.0, base=0, channel_multiplier=1), r=(const_trk,), w=(const_trk,))
        Sx.op(POOL, lambda h: h.memset(mask_bf[:], 1.0), r=(), w=(const_trk,))
        Sx.op(POOL, lambda h: h.affine_select(out=mask_bf[:], in_=mask_bf[:], pattern=[[1, 128]], compare_op=ALU.is_ge, fill=0.0, base=0, channel_multiplier=-1), r=(const_trk,), w=(const_trk,))
        Sx.op(POOL, lambda h: h.memset(blk_bf[:], 0.0), r=(), w=(const_trk,))
        Sx.op(POOL, lambda h: h.memset(blk_bf[0:64, 0:64], 1.0), r=(), w=(const_trk,))
        Sx.op(POOL, lambda h: h.memset(blk_bf[64:128, 64:128], 1.0), r=(), w=(const_trk,))

        lambda_init = 0.8 - 0.6 * math.exp(-0.3 * 1)
        lam_bc, lamt = ft_rot.next()
        Sx.dma(SP, lam_bc[:, 0:256], b_lambda[0].rearrange("a b -> (a b)").rearrange("(o n) -> o n", o=1).broadcast_to([128, 256]), r=(), w=(lamt,))
        f0, f0t = ft_rot.next()
        Sx.tt(DVE, f0[:, 0:64], lam_bc[:, 0:64], lam_bc[:, 64:128], ALU.mult, r=(lamt,), w=(f0t,))
        Sx.tt(DVE, f0[:, 64:128], lam_bc[:, 128:192], lam_bc[:, 192:256], ALU.mult, r=(lamt,), w=(f0t,))
        Sx.op(DVE, lambda h: h.tensor_reduce(out=small[:, 9:11], in_=f0[:, 0:128].rearrange("p (a b) -> p a b", a=2), op=ALU.add, axis=AX.X), r=(f0t,), w=(small_trk,))
        Sx.actf(small[:, 11:13], small[:, 9:11], AF.Exp, r=(small_trk,), w=(small_trk,))
        Sx.tt(DVE, small[:, 8:9], small[:, 12:13], small[:, 11:12], ALU.subtract, r=(small_trk,), w=(small_trk,))
        Sx.ts(DVE, small[:, 8:9], small[:, 8:9], -lambda_init, None, ALU.add, None, r=(small_trk,), w=(small_trk,))
        Sx.ts(DVE, small[:, 7:8], small[:, 7:8], 1.0 - lambda_init, None, ALU.mult, None, r=(small_trk,), w=(small_trk,))

        (sid,) = alloc(1)
        wsf = slots[sid][:].bitcast(F32).rearrange("p (g s) -> p g s", g=8)
        Sx.dma(SP, wsf[:, 0:6, :], a_w_s[0].rearrange("g t s -> t g s"), r=(), w=(slot_trk[sid],))
        for g in range(6):
            pb = g % 2
            Sx.transpose(ps[pb][:, 0:128], wsf[:, g, :], ident[:], r=(slot_trk[sid], const_trk), w=(ps_trk[pb],))
            Sx.tt(DVE, wsT[:, g, :], ps[pb][:, 0:128], mask_bf[:], ALU.mult, r=(const_trk,), w=(ps_trk[pb], const_trk))
        release([sid])

        eps_col = small[:, 0:1]
        Sx.op(DVE, lambda h: h.memset(qT[:], 0.0), r=(), w=tuple(q_trk))

        def rstd_from_psum(pbank, ncols, scale, nparts=128):
            ft, ftt = ft_rot.next()
            Sx.actf(ft[0:nparts, 0:ncols], ps[pbank][0:nparts, 0:ncols], AF.Ln, r=(small_trk,), w=(ps_trk[pbank], ftt), scale=scale, bias=eps_col[0:nparts, :])
            Sx.actf(ft[0:nparts, 0:ncols], ft[0:nparts, 0:ncols], AF.Exp, r=(ftt,), w=(ftt,), scale=-0.5)
            return ft, ftt

        def norm_tile(tt, gcol0):
            t0 = tt * TW
            pb = 7
            for kc in range(KC):
                pt, ptt = pt_rot.next()
                Sx.actf(pt[:], xT[:, kc, t0:t0 + TW], AF.Square, r=(x_trk[kc][tt],), w=(ptt,))
                Sx.mm(ps[pb][:], ones_bf[:], pt[:], kc == 0, kc == KC - 1, r=(ptt, const_trk), w=(ps_trk[pb],))
            ft, ftt = rstd_from_psum(pb, TW, 1.0 / D)
            for kc in range(KC):
                Sx.stt(hT[:, kc, t0:t0 + TW], xT[:, kc, t0:t0 + TW], gcols[:, gcol0 + kc:gcol0 + kc + 1], ft[:], ALU.mult, ALU.mult,
                       r=(x_trk[kc][tt], ftt, const_trk), w=(h_trk[kc][tt],))

        def head_norm(pbank_raw, pbank_stat, ncols, gcol, outs):
            pt, ptt = pt_rot.next()
            Sx.actf(pt[:, 0:ncols], ps[pbank_raw][:, 0:ncols], AF.Square, r=(), w=(ps_trk[pbank_raw], ptt))
            Sx.mm(ps[pbank_stat][:, 0:ncols], blk_bf[:], pt[:, 0:ncols], True, True, r=(ptt, const_trk), w=(ps_trk[pbank_stat],))
            ft, ftt = rstd_from_psum(pbank_stat, ncols, 1.0 / 64)
            for (p0, p1, out_ap, out_trk) in outs:
                Sx.stt(out_ap, ps[pbank_raw][p0:p1, 0:ncols], gcol[p0:p1, :], ft[p0:p1, 0:ncols], ALU.mult, ALU.mult,
                       r=(ftt, small_trk), w=(ps_trk[pbank_raw], out_trk))

        def qz_chunk(zc):
            if zc < 4:
                return Abuf[0][:, zc, :], A_trk[0]
            if zc < 8:
                return Abuf[1][:, zc - 4, :], A_trk[1]
            return qT[:, zc - 8, :], q_trk[zc - 8]

        def load_x(b):
            for tt in range(TT):
                sids = alloc(4)
                for n in range(4):
                    tc_ = tt * 4 + n
                    Sx.dma(SP, slots[sids[n]][:].bitcast(F32), x_d[b, tc_ * 128:(tc_ + 1) * 128, :], r=(), w=(slot_trk[sids[n]],))
                for kc in range(KC):
                    pb = kc % 4
                    for n in range(4):
                        src = slots[sids[n]][:].bitcast(F32)
                        Sx.transpose(ps[pb][:, n * 128:(n + 1) * 128], src[:, kc * 128:(kc + 1) * 128], ident[:],
                                     r=(slot_trk[sids[n]], const_trk), w=(ps_trk[pb],))
                    eng = ACT if kc % 2 == 0 else DVE
                    Sx.copy(eng, xT[:, kc, tt * TW:(tt + 1) * TW], ps[pb][:], r=(), w=(ps_trk[pb], x_trk[kc][tt]))
                release(sids)

        def store_x(b):
            for tc_ in range(16):
                tt = tc_ // 4
                (sid,) = alloc(1)
                dst = slots[sid][:].bitcast(F32)
                for half in range(2):
                    pb = 4 + (tc_ * 2 + half) % 4
                    for q in range(4):
                        kc = half * 4 + q
                        Sx.transpose(ps[pb][:, q * 128:(q + 1) * 128], xT[:, kc, tc_ * 128:(tc_ + 1) * 128], ident[:],
                                     r=(x_trk[kc][tt], const_trk), w=(ps_trk[pb],))
                    eng = ACT if half == 0 else DVE
                    Sx.copy(eng, dst[:, half * 512:(half + 1) * 512], ps[pb][:], r=(), w=(ps_trk[pb], slot_trk[sid]))
                Sx.dma(SP, y_d[b, tc_ * 128:(tc_ + 1) * 128, :], dst, r=(slot_trk[sid],), w=())
                release([sid])

        def reg_ffn(b, i, j):
            win = ffn_w_in[i, j].rearrange("(kc p) f -> p kc f", p=128)
            wout = ffn_w_out[i, j]
            ngrp = (FC + GROUP - 1) // GROUP
            for g in range(ngrp):
                f0 = g * GROUP
                nf = min(GROUP, FC - f0)
                nsl = (nf + 1) // 2

                def emit(sids, f0=f0, nf=nf, nsl=nsl):
                    for s_ in range(nsl):
                        c0 = (f0 + 2 * s_) * 128
                        w_ = min(256, (f0 + nf) * 128 - c0)
                        wdma(sids[s_], slot3(sids[s_], 8, 256)[:, :, 0:w_], win[:, :, c0:c0 + w_])
                    for s_ in range(nsl):
                        c0 = DFF + (f0 + 2 * s_) * 128
                        w_ = min(256, DFF + (f0 + nf) * 128 - c0)
                        wdma(sids[nsl + s_], slot3(sids[nsl + s_], 8, 256)[:, :, 0:w_], win[:, :, c0:c0 + w_])
                    for s_ in range(nsl):
                        r0 = (f0 + 2 * s_) * 128
                        nr = min(2, f0 + nf - (f0 + 2 * s_))
                        wdma(sids[2 * nsl + s_], slot3(sids[2 * nsl + s_], 2, 1024)[:, 0:nr, :],
                             wout[r0:r0 + nr * 128, :].rearrange("(fc p) d -> p fc d", p=128))
                wreg(("ffn", b, i, j, g), 3 * nsl, emit)

        def reg_slabs(key, w2d, col0, ncol):
            wv = w2d.rearrange("(kc p) f -> p kc f", p=128)
            for s_ in range(ncol // 256):
                def emit(sids, s_=s_):
                    c0 = col0 + s_ * 256
                    wdma(sids[0], slot3(sids[0], 8, 256), wv[:, :, c0:c0 + 256])
                wreg(key + (s_,), 1, emit)

        def reg_wout(key, w2d):
            for s_ in range(4):
                def emit(sids, s_=s_):
                    wdma(sids[0], slot3(sids[0], 2, 1024), w2d[s_ * 256:(s_ + 1) * 256, :].rearrange("(c p) d -> p c d", p=128))
                wreg(key + (s_,), 1, emit)

        ycnt = [0]
        gucnt = [0]

        def ffn(b, i, j):
            for tt in range(TT):
                norm_tile(tt, (i * 2 + j) * 8)
            ngrp = (FC + GROUP - 1) // GROUP
            pending = [None]
            acnt = [0]
            for g in range(ngrp):
                f0 = g * GROUP
                nf = min(GROUP, FC - f0)
                nsl = (nf + 1) // 2
                sids = want(("ffn", b, i, j, g))
                pump(reserve=4)
                gate_s, up_s, out_s = sids[0:nsl], sids[nsl:2 * nsl], sids[2 * nsl:3 * nsl]
                for tt in range(TT):
                    t0 = tt * TW
                    ab = acnt[0] % 2
                    acnt[0] += 1
                    A = Abuf[ab]
                    for q in range(nf):
                        pg = (gucnt[0] % 2) * 2
                        pu = pg + 1
                        gucnt[0] += 1
                        gs = slot3(gate_s[q // 2], 8, 256)
                        us = slot3(up_s[q // 2], 8, 256)
                        co = (q % 2) * 128
                        for kc in range(KC):
                            Sx.mm(ps[pg][:], gs[:, kc, co:co + 128], hT[:, kc, t0:t0 + TW], kc == 0, kc == KC - 1,
                                  r=(slot_trk[gate_s[q // 2]], h_trk[kc][tt]), w=(ps_trk[pg],))
                        for kc in range(KC):
                            Sx.mm(ps[pu][:], us[:, kc, co:co + 128], hT[:, kc, t0:t0 + TW], kc == 0, kc == KC - 1,
                                  r=(slot_trk[up_s[q // 2]], h_trk[kc][tt]), w=(ps_trk[pu],))
                        ft, ftt = ft_rot.next()
                        Sx.actf(ft[:], ps[pg][:], AF.Silu, r=(), w=(ps_trk[pg], ftt))
                        Sx.tt(DVE, A[:, q, :], ft[:], ps[pu][:], ALU.mult, r=(ftt,), w=(ps_trk[pu], A_trk[ab]))
                    if pending[0] is not None:
                        pending[0]()

                    def ywork(tt=tt, t0=t0, A=A, ab=ab, nf=nf, out_s=out_s):
                        for dc in range(KC):
                            py = 4 + ycnt[0] % 3
                            ycnt[0] += 1
                            for q in range(nf):
                                ws_ = slot3(out_s[q // 2], 2, 1024)
                                Sx.mm(ps[py][:], ws_[:, q % 2, dc * 128:(dc + 1) * 128], A[:, q, :], q == 0, q == nf - 1,
                                      r=(slot_trk[out_s[q // 2]], A_trk[ab]), w=(ps_trk[py],))
                            Sx.stt(xT[:, dc, t0:t0 + TW], ps[py][:], 0.5, xT[:, dc, t0:t0 + TW], ALU.mult, ALU.add,
                                   r=(), w=(ps_trk[py], x_trk[dc][tt]))
                    pending[0] = ywork
                if g == ngrp - 1:
                    pending[0]()
                    pending[0] = None
                    release(sids)
                else:
                    pending[0]()
                    pending[0] = None
                    release(sids)

        def mem_kv(b, i):
            sm = alloc(2)
            (sh,) = alloc(1)
            mh = slot3(sh, 8, 256)
            for lc in range(2):
                mf = slots[sm[lc]][:].bitcast(F32)
                Sx.dma(SP, mf, mem_d[b, lc * 128:(lc + 1) * 128, :], r=(), w=(slot_trk[sm[lc]],))
                ft, ftt = ft_rot.next()
                for hf in range(2):
                    Sx.tt(DVE, ft[:], mf[:, hf * 512:(hf + 1) * 512], mf[:, hf * 512:(hf + 1) * 512], ALU.mult, r=(slot_trk[sm[lc]],), w=(ftt,))
                    Sx.op(DVE, lambda h, ft=ft, hf=hf: h.tensor_reduce(out=small[:, 16 + hf:17 + hf], in_=ft[:], op=ALU.add, axis=AX.X), r=(ftt,), w=(small_trk,))
                Sx.tt(DVE, small[:, 18:19], small[:, 16:17], small[:, 17:18], ALU.add, r=(small_trk,), w=(small_trk,))
                Sx.actf(small[:, 19:20], small[:, 18:19], AF.Ln, r=(small_trk,), w=(small_trk,), scale=1.0 / D, bias=eps_col)
                Sx.actf(small[:, 20:21], small[:, 19:20], AF.Exp, r=(small_trk,), w=(small_trk,), scale=-0.5)
                Sx.ts(DVE, mf, mf, small[:, 20:21], None, ALU.mult, None, r=(small_trk, slot_trk[sm[lc]]), w=(slot_trk[sm[lc]],))
            for kc in range(KC):
                pb = kc % 2
                for lc in range(2):
                    mf = slots[sm[lc]][:].bitcast(F32)
                    Sx.transpose(ps[pb][:, lc * 128:(lc + 1) * 128], mf[:, kc * 128:(kc + 1) * 128], ident[:],
                                 r=(slot_trk[sm[lc]], const_trk), w=(ps_trk[pb],))
                Sx.ts(DVE, mh[:, kc, :], ps[pb][:, 0:256], gcols[:, 48 + 8 * i + kc:48 + 8 * i + kc + 1], None, ALU.mult, None,
                      r=(const_trk,), w=(ps_trk[pb], slot_trk[sh]))
            release(sm)
            ks = want(("memkv", b, i, 0))[0]
            vs = want(("memkv", b, i, 1))[0]
            kw = slot3(ks, 8, 256)
            vw = slot3(vs, 8, 256)
            for hc in range(2):
                pr = 2 + hc
                for kc in range(KC):
                    Sx.mm(ps[pr][:, 0:256], kw[:, kc, hc * 128:(hc + 1) * 128], mh[:, kc, :], kc == 0, kc == KC - 1,
                          r=(slot_trk[ks], slot_trk[sh]), w=(ps_trk[pr],))
                head_norm(pr, 7, 256, small[:, 3 + i:4 + i], [(0, 128, kmT[:, hc, :], km_trk)])
            for lc in range(2):
                pr = 4 + lc
                for kc in range(KC):
                    Sx.mm(ps[pr][:, 0:256], mh[:, kc, lc * 128:(lc + 1) * 128], vw[:, kc, :], kc == 0, kc == KC - 1,
                          r=(slot_trk[vs], slot_trk[sh]), w=(ps_trk[pr],))
                Sx.copy(ACT, vm[:, lc, :], ps[pr][:, 0:256], r=(), w=(ps_trk[pr], vm_trk))
            release([ks, vs, sh])

        def mem_attn(tt):
            for hc in range(2):
                for hh in range(2):
                    hm = hc * 2 + hh
                    pnum, pden = 2 * hh, 2 * hh + 1
                    for lc in range(2):
                        pss = 4 + (hm * 2 + lc) % 3
                        Sx.mm(ps[pss][:], kmT[:, hc, lc * 128:(lc + 1) * 128], qT[:, 4 + hm, :], True, True,
                              r=(km_trk, q_trk[4 + hm]), w=(ps_trk[pss],))
                        pt, ptt = pt_rot.next()
                        Sx.actf(pt[:], ps[pss][:], AF.Exp, r=(), w=(ps_trk[pss], ptt), scale=0.125)
                        Sx.mm(ps[pnum][:], vm[:, lc, hc * 128:(hc + 1) * 128], pt[:], lc == 0, lc == 1,
                              r=(vm_trk, ptt), w=(ps_trk[pnum],))
                        Sx.mm(ps[pden][:], ones_bf[:], pt[:], lc == 0, lc == 1,
                              r=(const_trk, ptt), w=(ps_trk[pden],))
                for hh in range(2):
                    pnum, pden = 2 * hh, 2 * hh + 1
                    r0 = hh * 64
                    ft, ftt = ft_rot.next()
                    Sx.actf(ft[r0:r0 + 64, :], ps[pden][r0:r0 + 64, :], AF.Ln, r=(), w=(ps_trk[pden], ftt))
                    Sx.actf(ft[r0:r0 + 64, :], ft[r0:r0 + 64, :], AF.Exp, r=(ftt,), w=(ftt,), scale=-1.0)
                    Sx.tt(DVE, tokT[r0:r0 + 64, 6 + hc, :], ps[pnum][r0:r0 + 64, :], ft[r0:r0 + 64, :], ALU.mult, r=(ftt,), w=(ps_trk[pnum], tok_trk[6 + hc]))

        def out_proj(b, i, tt):
            t0 = tt * TW
            ws_ = [want(("wout", b, i, tt, s_))[0] for s_ in range(4)]
            for dc in range(KC):
                py = 4 + ycnt[0] % 3
                ycnt[0] += 1
                for c in range(8):
                    wv = slot3(ws_[c // 2], 2, 1024)
                    Sx.mm(ps[py][:], wv[:, c % 2, dc * 128:(dc + 1) * 128], tokT[:, c, :], c == 0, c == 7,
                          r=(slot_trk[ws_[c // 2]], tok_trk[c]), w=(ps_trk[py],))
                Sx.tt(DVE, xT[:, dc, t0:t0 + TW], ps[py][:], xT[:, dc, t0:t0 + TW], ALU.add, r=(), w=(ps_trk[py], x_trk[dc][tt]))
            release(ws_)

        def qm_proj(tt, slab_sid, memq_col):
            t0 = tt * TW
            wv = slot3(slab_sid, 8, 256)
            for hc in range(2):
                pr = 2 + hc
                for kc in range(KC):
                    Sx.mm(ps[pr][:], wv[:, kc, hc * 128:(hc + 1) * 128], hT[:, kc, t0:t0 + TW], kc == 0, kc == KC - 1,
                          r=(slot_trk[slab_sid], h_trk[kc][tt]), w=(ps_trk[pr],))
                head_norm(pr, 7, TW, memq_col, [(0, 64, qT[0:64, 4 + 2 * hc, :], q_trk[4 + 2 * hc]), (64, 128, qT[64:128, 5 + 2 * hc, :], q_trk[5 + 2 * hc])])

        def mixer_a(b):
            i = 0
            for tt in range(TT):
                norm_tile(tt, 32)
            mem_kv(b, i)
            slabs = [want(("a_in", b, s_))[0] for s_ in range(7)]
            pump(reserve=4)
            for tt in range(TT):
                t0 = tt * TW
                qm_proj(tt, slabs[6], small[:, 1:2])
                for g in range(6):
                    us = slot3(slabs[g // 2], 8, 256)
                    vs = slot3(slabs[3 + g // 2], 8, 256)
                    co = (g % 2) * 128
                    pu, pv, pm = 0, 1, 2 + g % 2
                    for kc in range(KC):
                        Sx.mm(ps[pu][:], us[:, kc, co:co + 128], hT[:, kc, t0:t0 + TW], kc == 0, kc == KC - 1,
                              r=(slot_trk[slabs[g // 2]], h_trk[kc][tt]), w=(ps_trk[pu],))
                    for n in range(4):
                        for kc in range(KC):
                            Sx.mm(ps[pv][:, n * 128:(n + 1) * 128], hT[:, kc, t0 + n * 128:t0 + (n + 1) * 128], vs[:, kc, co:co + 128],
                                  kc == 0, kc == KC - 1, r=(slot_trk[slabs[3 + g // 2]], h_trk[kc][tt]), w=(ps_trk[pv],))
                    fu, fut = ft_rot.next()
                    Sx.actf(fu[:], ps[pu][:], AF.Gelu, r=(), w=(ps_trk[pu], fut))
                    fv, fvt = ft_rot.next()
                    Sx.actf(fv[:], ps[pv][:], AF.Gelu, r=(), w=(ps_trk[pv], fvt))
                    fs, fst = ft_rot.next()
                    Sx.tt(DVE, fs[:], fv[:], fv[:], ALU.mult, r=(fvt,), w=(fst,))
                    gs_, gst = gst_rot.next()
                    Sx.op(DVE, lambda h, fs=fs, gs_=gs_: h.tensor_reduce(out=gs_[:, 0:4], in_=fs[:].rearrange("p (a b) -> p a b", a=4), op=ALU.add, axis=AX.X),
                          r=(fst,), w=(gst,))
                    Sx.ts(DVE, gs_[:, 0:4], gs_[:, 0:4], 1.0 / 128, EPS, ALU.mult, ALU.add, r=(gst,), w=(gst,))
                    Sx.tt(POOL, gs_[:, 4:8], gs_[:, 0:4], negh[:, 0:4], ALU.pow, r=(gst, const_trk), w=(gst,))
                    pt, ptt = pt_rot.next()
                    for n in range(4):
                        Sx.stt(pt[:, n * 128:(n + 1) * 128], fv[:, n * 128:(n + 1) * 128], gs_[:, 4 + n:5 + n], avn_bc[:, g * 128:(g + 1) * 128],
                               ALU.mult, ALU.mult, r=(fvt, gst, const_trk), w=(ptt,))
                    for n in range(4):
                        Sx.mm(ps[pm][:, n * 128:(n + 1) * 128], pt[:, n * 128:(n + 1) * 128], wsT[:, g, :], True, True,
                              r=(ptt, const_trk), w=(ps_trk[pm],))
                    Sx.tt(DVE, fs[:].rearrange("p (a b) -> p a b", a=4), ps[pm][:].rearrange("p (a b) -> p a b", a=4),
                          abs_bc[:, g * 128:(g + 1) * 128].unsqueeze(1).broadcast_to([128, 4, 128]), ALU.add,
                          r=(const_trk,), w=(ps_trk[pm], fst))
                    Sx.tt(POOL, tokT[:, g, :], fs[:], fu[:], ALU.mult, r=(fst, fut), w=(tok_trk[g],))
                mem_attn(tt)
                if tt == TT - 1:
                    release(slabs)
                out_proj(b, i, tt)
                pump(reserve=4)

        def mixer_b(b):
            i = 1
            for ab in range(2):
                Sx.op(DVE, lambda h, ab=ab: h.memset(Abuf[ab][:], 0.0), r=(), w=(A_trk[ab],))
            for tt in range(TT):
                norm_tile(tt, 40)
            mem_kv(b, i)
            kvs = want(("kv", b))
            k_s, v_s = kvs[0:6], kvs[6:12]
            kslabs = [want(("b_k", b, s_))[0] for s_ in range(3)]
            for c in range(6):
                wv = slot3(kslabs[c // 2], 8, 256)
                co = (c % 2) * 128
                for tt in range(TT):
                    t0 = tt * TW
                    pr = 2 + (c * TT + tt) % 2
                    for kc in range(KC):
                        Sx.mm(ps[pr][:], wv[:, kc, co:co + 128], hT[:, kc, t0:t0 + TW], kc == 0, kc == KC - 1,
                              r=(slot_trk[kslabs[c // 2]], h_trk[kc][tt]), w=(ps_trk[pr],))
                    head_norm(pr, 7, TW, small[:, 6:7], [(0, 128, slots[k_s[c]][:, t0:t0 + TW], slot_trk[k_s[c]])])
            release(kslabs)
            vslabs = [want(("b_v", b, s_))[0] for s_ in range(3)]
            for tc_ in range(16):
                tt = tc_ // 4
                for s_ in range(3):
                    wv = slot3(vslabs[s_], 8, 256)
                    pb = 4 + (tc_ * 3 + s_) % 3
                    for kc in range(KC):
                        Sx.mm(ps[pb][:, 0:256], hT[:, kc, tc_ * 128:(tc_ + 1) * 128], wv[:, kc, :], kc == 0, kc == KC - 1,
                              r=(slot_trk[vslabs[s_]], h_trk[kc][tt]), w=(ps_trk[pb],))
                    for hh in range(2):
                        h_ = s_ * 2 + hh
                        vdst = slot3(v_s[h_], 16, 128)
                        Sx.copy(ACT, vdst[:, tc_, :], ps[pb][:, hh * 128:(hh + 1) * 128], r=(), w=(ps_trk[pb], slot_trk[v_s[h_]]))
            release(vslabs)
            for tt in range(TT):
                t0 = tt * TW
                qs = [want(("b_q", b, tt, s_))[0] for s_ in range(3)]
                for c in range(6):
                    wv = slot3(qs[c // 2], 8, 256)
                    co = (c % 2) * 128
                    pr = 2 + c % 2
                    for kc in range(KC):
                        Sx.mm(ps[pr][:], wv[:, kc, co:co + 128], hT[:, kc, t0:t0 + TW], kc == 0, kc == KC - 1,
                              r=(slot_trk[qs[c // 2]], h_trk[kc][tt]), w=(ps_trk[pr],))
                    za, zat = qz_chunk(2 * c)
                    zb, zbt = qz_chunk(2 * c + 1)
                    head_norm(pr, 7, TW, small[:, 5:6], [(0, 64, za[0:64, :], zat), (64, 128, zb[64:128, :], zbt)])
                release(qs)
                qms = want(("b_qm", b, tt, 0))[0]
                qm_proj(tt, qms, small[:, 2:3])
                release([qms])
                pump(reserve=4)
                nk = (tt + 1) * 4
                for h_ in range(6):
                    vsl = slot3(v_s[h_], 16, 128)
                    steps = []
                    for c in range(2):
                        for kc in range(nk):
                            steps.append((c, kc))
                    scnt = [0]
                    inflight = []

                    def stageA(c, kc):
                        j = kc - tt * 4
                        c0 = max(j, 0) * 128
                        pss = 4 + scnt[0] % 3
                        scnt[0] += 1
                        zq, zqt = qz_chunk(2 * h_ + c)
                        Sx.mm(ps[pss][:, c0:TW], slots[k_s[h_]][:, kc * 128:(kc + 1) * 128], zq[:, c0:TW],
                              True, True, r=(slot_trk[k_s[h_]], zqt), w=(ps_trk[pss],))
                        return (c, kc, j, c0, pss)

                    def stageB(st):
                        c, kc, j, c0, pss = st
                        pt, ptt = pt_rot.next()
                        Sx.actf(pt[:, c0:TW], ps[pss][:, c0:TW], AF.Exp, r=(), w=(ps_trk[pss], ptt), scale=0.125)
                        if j >= 0:
                            Sx.tt(DVE, pt[:, c0:c0 + 128], pt[:, c0:c0 + 128], mask_bf[:], ALU.mult, r=(ptt, const_trk), w=(ptt,))
                        pn, pd = c * 2, c * 2 + 1
                        Sx.mm(ps[pn][:, c0:TW], vsl[:, kc, :], pt[:, c0:TW], kc == 0, kc == nk - 1,
                              r=(slot_trk[v_s[h_]], ptt), w=(ps_trk[pn],))
                        Sx.mm(ps[pd][:, c0:TW], ones_bf[:], pt[:, c0:TW], kc == 0, kc == nk - 1,
                              r=(const_trk, ptt), w=(ps_trk[pd],))

                    for si, (c, kc) in enumerate(steps):
                        inflight.append(stageA(c, kc))
                        if len(inflight) > 2:
                            stageB(inflight.pop(0))
                    while inflight:
                        stageB(inflight.pop(0))
                    fa, fat = ft_rot.next()
                    fb, fbt = ft_rot.next()
                    Sx.actf(fa[:], ps[1][:], AF.Ln, r=(), w=(ps_trk[1], fat))
                    Sx.actf(fb[:], ps[3][:], AF.Ln, r=(), w=(ps_trk[3], fbt))
                    Sx.actf(fa[:], fa[:], AF.Exp, r=(fat,), w=(fat,), scale=-1.0)
                    Sx.actf(fb[:], fb[:], AF.Exp, r=(fbt,), w=(fbt,), scale=-1.0)
                    Sx.tt(DVE, fa[:], ps[0][:], fa[:], ALU.mult, r=(fat,), w=(ps_trk[0], fat))
                    Sx.tt(DVE, fb[:], ps[2][:], fb[:], ALU.mult, r=(fbt,), w=(ps_trk[2], fbt))
                    Sx.stt(fa[:], fb[:], small[:, 8:9], fa[:], ALU.mult, ALU.add, r=(fbt, fat, small_trk), w=(fat,))
                    pt, ptt = pt_rot.next()
                    Sx.tt(DVE, pt[:], fa[:], fa[:], ALU.mult, r=(fat,), w=(ptt,))
                    Sx.mm(ps[7][:], ones_bf[:], pt[:], True, True, r=(ptt, const_trk), w=(ps_trk[7],))
                    fr, frt = rstd_from_psum(7, TW, 1.0 / 128)
                    Sx.stt(tokT[:, h_, :], fa[:], small[:, 7:8], fr[:], ALU.mult, ALU.mult, r=(fat, frt, small_trk), w=(tok_trk[h_],))
                mem_attn(tt)
                out_proj(b, i, tt)
            release(kvs)

        for b in range(nb):
            reg_ffn(b, 0, 0)
            reg_slabs(("memkv", b, 0), w_mem_kv[0], 0, 512)
            reg_slabs(("a_in", b), a_w_in[0], 0, 1792)
            for tt in range(TT):
                reg_wout(("wout", b, 0, tt), w_out[0])
            reg_ffn(b, 0, 1)
            reg_ffn(b, 1, 0)
            reg_slabs(("memkv", b, 1), w_mem_kv[1], 0, 512)
            wreg(("kv", b), 12, lambda sids: None)
            reg_slabs(("b_k", b), b_w_in[0], 768, 768)
            reg_slabs(("b_v", b), b_w_in[0], 1536, 768)
            for tt in range(TT):
                reg_slabs(("b_q", b, tt), b_w_in[0], 0, 768)
                reg_slabs(("b_qm", b, tt), b_w_in[0], 2304, 256)
                reg_wout(("wout", b, 1, tt), w_out[1])
            reg_ffn(b, 1, 1)

        for b in range(nb):
            load_x(b)
            st = 0
            for i in range(2):
                for sub in range(3):
                    if st >= stages:
                        break
                    if sub == 0:
                        ffn(b, i, 0)
                    elif sub == 1:
                        (mixer_a if i == 0 else mixer_b)(b)
                    else:
                        ffn(b, i, 1)
                    st += 1
            store_x(b)
        Sx.wait_all_dma(SP)

        block = es.enter_context(nc.Block())

        @block.tensor
        def _(h):
            for f in PE.prog:
                f(h)

        @block.scalar
        def _(h):
            for f in ACT.prog:
                f(h)

        @block.vector
        def _(h):
            for f in DVE.prog:
                f(h)

        @block.gpsimd
        def _(h):
            for f in POOL.prog:
                f(h)

        @block.sync
        def _(h):
            for f in SP.prog:
                f(h)
    return nc


_NC_CACHE = {}


def kernel(**inputs):
    x = np.ascontiguousarray(inputs["x"], dtype=np.float32)
    mem = np.ascontiguousarray(inputs["mem"], dtype=np.float32)
    if "nc" not in _NC_CACHE:
        _NC_CACHE["nc"] = build_nc()
    nc = _NC_CACHE["nc"]
    shared = {k: np.ascontiguousarray(v, dtype=np.float32) for k, v in inputs.items() if k not in ("x", "mem")}
    in_maps = []
    for c in range(N_CORES):
        m = dict(shared)
        m["x"] = x[c * NB_CORE:(c + 1) * NB_CORE]
        m["mem"] = mem[c * NB_CORE:(c + 1) * NB_CORE]
        in_maps.append(m)
    res = run_bass_kernel_spmd(nc, in_maps, core_ids=list(range(N_CORES)))
    return np.concatenate([r["y"] for r in res.results], axis=0)
```

```python
import math
from contextlib import ExitStack

import numpy as np
import concourse.bass as bass
import concourse.mybir as mybir
from concourse.bass_utils import run_bass_kernel_spmd

F32 = mybir.dt.float32
BF16 = mybir.dt.bfloat16
AF = mybir.ActivationFunctionType
ALU = mybir.AluOpType
AX = mybir.AxisListType

N_CORES = 8
D = 1024
S = 2048
MEM_L = 256
KC = 8
TT = 4
TW = 512
DFF = 2816
FC = 22
TOKW = 768
EPS = 1e-6
NB_CORE = 2
GROUP = 4
NDSEM = 12


class Trk:
    __slots__ = ("w", "r")

    def __init__(self):
        self.w = None
        self.r = {}


class Eng:
    def __init__(self, name, sem, is_pe=False):
        self.name = name
        self.sem = sem
        self.cnt = 0
        self.seen = {}
        self.prog = []
        self.is_pe = is_pe
        self.dsems = []
        self.dvals = []
        self.dnext = 0


class Sched:
    def __init__(self, nc, es):
        self.nc = nc
        self.engs = {}
        for name in ("pe", "act", "dve", "pool", "sp"):
            sem = es.enter_context(nc.semaphore("tl_" + name))
            self.engs[name] = Eng(name, sem, is_pe=(name == "pe"))
        for name in ("pool", "sp"):
            e = self.engs[name]
            for i in range(NDSEM):
                e.dsems.append(es.enter_context(nc.semaphore(f"d_{name}{i}")))
                e.dvals.append(0)
        self.pe, self.act, self.dve, self.pool, self.sp = (self.engs[n] for n in ("pe", "act", "dve", "pool", "sp"))

    def _wait(self, eng, tk):
        sem, val, src = tk
        key = sem.num
        if eng.seen.get(key, 0) >= val:
            return
        eng.seen[key] = val
        eng.prog.append(lambda h, sem=sem, val=val: h.wait_ge(sem, val))

    def _deps(self, eng, r, w):
        for t in r:
            tk = t.w
            if tk is not None:
                if tk[2] is eng:
                    if not eng.is_pe:
                        self._wait(eng, tk)
                else:
                    self._wait(eng, tk)
        for t in w:
            tk = t.w
            if tk is not None and tk[2] is not eng:
                self._wait(eng, tk)
            for tk in t.r.values():
                if tk[2] is not eng:
                    self._wait(eng, tk)

    def _mark(self, tk, r, w, key):
        for t in w:
            t.w = tk
            t.r = {}
        for t in r:
            t.r[key] = tk

    def op(self, eng, fn, r=(), w=()):
        self._deps(eng, r, w)
        eng.cnt += 1
        sem = eng.sem
        eng.prog.append(lambda h, fn=fn, sem=sem: fn(h).then_inc(sem, 1))
        tk = (sem, eng.cnt, eng)
        self._mark(tk, r, w, eng.name)
        return tk

    def dma(self, eng, out, in_, r=(), w=(), allow=False):
        self._deps(eng, r, w)
        i = eng.dnext
        eng.dnext = (i + 1) % len(eng.dsems)
        sem = eng.dsems[i]
        if eng.dvals[i] > 0:
            self._wait(eng, (sem, eng.dvals[i], None))
        eng.dvals[i] += 16
        val = eng.dvals[i]
        if allow:
            eng.prog.append(lambda h, out=out, in_=in_, sem=sem: h.dma_start(out=out, in_=in_, allow_slow_non_contiguous=True).then_inc(sem, 16))
        else:
            eng.prog.append(lambda h, out=out, in_=in_, sem=sem: h.dma_start(out=out, in_=in_).then_inc(sem, 16))
        tk = (sem, val, None)
        self._mark(tk, r, w, "dma_%s_%d" % (eng.name, i))
        return tk

    def wait_all_dma(self, eng):
        for e in (self.pool, self.sp):
            for i, sem in enumerate(e.dsems):
                if e.dvals[i] > 0:
                    self._wait(eng, (sem, e.dvals[i], None))

    def mm(self, out, lhsT, rhs, start, stop, r, w):
        return self.op(self.pe, lambda h: h.matmul(out, lhsT, rhs, start=start, stop=stop), r=r, w=w)

    def transpose(self, out, in_, ident, r, w):
        return self.op(self.pe, lambda h: h.transpose(out, in_, ident), r=r, w=w)

    def actf(self, out, in_, func, r, w, scale=None, bias=None):
        kw = {}
        if scale is not None:
            kw["scale"] = scale
        if bias is not None:
            kw["bias"] = bias
        return self.op(self.act, lambda h: h.activation(out=out, in_=in_, func=func, **kw), r=r, w=w)

    def tt(self, eng, out, in0, in1, op, r, w):
        return self.op(eng, lambda h: h.tensor_tensor(out=out, in0=in0, in1=in1, op=op), r=r, w=w)

    def ts(self, eng, out, in0, s1, s2, op0, op1, r, w):
        if op1 is None:
            return self.op(eng, lambda h: h.tensor_scalar(out=out, in0=in0, scalar1=s1, scalar2=None, op0=op0), r=r, w=w)
        return self.op(eng, lambda h: h.tensor_scalar(out=out, in0=in0, scalar1=s1, scalar2=s2, op0=op0, op1=op1), r=r, w=w)

    def stt(self, out, in0, scalar, in1, op0, op1, r, w):
        return self.op(self.dve, lambda h: h.scalar_tensor_tensor(out=out, in0=in0, scalar=scalar, in1=in1, op0=op0, op1=op1), r=r, w=w)

    def recip(self, out, in_, r, w):
        return self.op(self.dve, lambda h: h.reciprocal(out=out, in_=in_), r=r, w=w)

    def copy(self, eng, out, in_, r, w):
        if eng is self.act:
            return self.op(eng, lambda h: h.copy(out=out, in_=in_), r=r, w=w)
        return self.op(eng, lambda h: h.tensor_copy(out=out, in_=in_), r=r, w=w)


class Rot:
    def __init__(self, items):
        self.items = items
        self.i = 0

    def next(self):
        it = self.items[self.i]
        self.i = (self.i + 1) % len(self.items)
        return it


def build_nc(nb=NB_CORE, stages=6, nslot=None):
    nc = bass.Bass("TRN2", target_bir_lowering=False)
    dr = {}

    def din(name, shape):
        dr[name] = nc.dram_tensor(name, list(shape), F32, kind="ExternalInput").ap()
        return dr[name]

    x_d = din("x", (nb, S, D))
    mem_d = din("mem", (nb, MEM_L, D))
    ffn_norm = din("ffn_norm", (2, 2, D))
    ffn_w_in = din("ffn_w_in", (2, 2, D, 2 * DFF))
    ffn_w_out = din("ffn_w_out", (2, 2, DFF, D))
    mix_norm = din("mix_norm", (2, D))
    mem_norm = din("mem_norm", (2, D))
    w_mem_kv = din("w_mem_kv", (2, D, 512))
    memq_norm = din("memq_norm", (2, 64))
    memk_norm = din("memk_norm", (2, 64))
    w_out = din("w_out", (2, D, D))
    a_w_in = din("a_w_in", (1, D, 1792))
    a_v_norm = din("a_v_norm", (1, TOKW))
    a_w_s = din("a_w_s", (1, 6, 128, 128))
    a_b_s = din("a_b_s", (1, 6, 128))
    b_w_in = din("b_w_in", (1, D, 2560))
    b_q_norm = din("b_q_norm", (1, 64))
    b_k_norm = din("b_k_norm", (1, 64))
    b_lambda = din("b_lambda", (1, 4, 64))
    b_subln = din("b_subln", (1, 128))
    y_d = nc.dram_tensor("y", [nb, S, D], F32, kind="ExternalOutput").ap()

    es = ExitStack()
    with es:
        def sb(name, shape, dt):
            return es.enter_context(nc.sbuf_tensor(name, list(shape), dt))

        xT = sb("xT", (128, KC, S), F32)
        hT = sb("hT", (128, KC, S), BF16)
        Abuf = [sb(f"A{i}", (128, GROUP, TW), BF16) for i in range(2)]
        tokT = sb("tokT", (128, 8, TW), BF16)
        qT = sb("qT", (128, 8, TW), BF16)
        ptl = [sb(f"pt{i}", (128, TW), BF16) for i in range(3)]
        ftl = [sb(f"ft{i}", (128, TW), F32) for i in range(4)]
        ident = sb("ident", (128, 128), F32)
        ones_bf = sb("ones_bf", (128, 128), BF16)
        blk_bf = sb("blk_bf", (128, 128), BF16)
        mask_bf = sb("mask_bf", (128, 128), BF16)
        wsT = sb("wsT", (128, 6, 128), BF16)
        gcols = sb("gcols", (128, 64), F32)
        avn_bc = sb("avn_bc", (128, TOKW), F32)
        abs_bc = sb("abs_bc", (128, TOKW), F32)
        small = sb("small", (128, 32), F32)
        negh = sb("negh", (128, 4), F32)
        gstl = [sb(f"gst{i}", (128, 8), F32) for i in range(2)]
        kmT = sb("kmT", (128, 2, MEM_L), BF16)
        vm = sb("vm", (128, 2, 256), BF16)
        if nslot is None:
            nslot = 16
        slots = [sb(f"slot{i}", (128, 2048), BF16) for i in range(nslot)]
        ps = [es.enter_context(nc.psum_tensor(f"ps{i}", [128, TW], F32)) for i in range(8)]

        Sx = Sched(nc, es)
        PE, ACT, DVE, POOL, SP = Sx.pe, Sx.act, Sx.dve, Sx.pool, Sx.sp

        x_trk = [[Trk() for _ in range(TT)] for _ in range(KC)]
        h_trk = [[Trk() for _ in range(TT)] for _ in range(KC)]
        A_trk = [Trk(), Trk()]
        tok_trk = [Trk() for _ in range(8)]
        q_trk = [Trk() for _ in range(8)]
        ps_trk = [Trk() for _ in range(8)]
        slot_trk = [Trk() for _ in range(nslot)]
        const_trk = Trk()
        km_trk = Trk()
        vm_trk = Trk()
        small_trk = Trk()
        pt_rot = Rot([(ptl[i], Trk()) for i in range(3)])
        ft_rot = Rot([(ftl[i], Trk()) for i in range(4)])
        gst_rot = Rot([(gstl[i], Trk()) for i in range(2)])

        free_slots = list(range(nslot))

        def alloc(n):
            assert len(free_slots) >= n, "slot pool exhausted"
            got = free_slots[:n]
            del free_slots[:n]
            return got

        def release(ids):
            free_slots.extend(ids)

        wq = []
        wq_pos = [0]
        wq_map = {}

        def wreg(key, n, emit):
            ent = {"key": key, "n": n, "emit": emit, "slots": None}
            wq.append(ent)
            wq_map[key] = ent

        def pump(reserve=4):
            while wq_pos[0] < len(wq):
                ent = wq[wq_pos[0]]
                if len(free_slots) - reserve < ent["n"]:
                    break
                ent["slots"] = alloc(ent["n"])
                ent["emit"](ent["slots"])
                wq_pos[0] += 1

        def want(key):
            ent = wq_map[key]
            while ent["slots"] is None:
                nxt = wq[wq_pos[0]]
                nxt["slots"] = alloc(nxt["n"])
                nxt["emit"](nxt["slots"])
                wq_pos[0] += 1
            return ent["slots"]

        def wdma(slot_id, out_ap, in_ap):
            Sx.dma(POOL, out_ap, in_ap, r=(), w=(slot_trk[slot_id],))

        def slot3(sid, a, b):
            return slots[sid][:].rearrange("p (a b) -> p a b", a=a, b=b)

        def gload(col, vec_ap):
            Sx.dma(SP, gcols[:, col:col + KC], vec_ap.rearrange("(kc p) -> p kc", p=128), r=(), w=(const_trk,), allow=True)

        for i in range(2):
            for j in range(2):
                gload((i * 2 + j) * 8, ffn_norm[i, j])
            gload(32 + 8 * i, mix_norm[i])
            gload(48 + 8 * i, mem_norm[i])

        def hload(col, vec_ap):
            v = vec_ap.rearrange("(p o) -> p o", o=1)
            Sx.dma(SP, small[0:64, col:col + 1], v, r=(), w=(small_trk,), allow=True)
            Sx.dma(SP, small[64:128, col:col + 1], v, r=(), w=(small_trk,), allow=True)

        hload(1, memq_norm[0]); hload(2, memq_norm[1]); hload(3, memk_norm[0]); hload(4, memk_norm[1])
        hload(5, b_q_norm[0]); hload(6, b_k_norm[0])
        Sx.dma(SP, small[:, 7:8], b_subln[0].rearrange("(p o) -> p o", o=1), r=(), w=(small_trk,), allow=True)
        Sx.dma(SP, avn_bc[:], a_v_norm[0:1, :].broadcast_to([128, TOKW]), r=(), w=(const_trk,))
        Sx.dma(SP, abs_bc[:], a_b_s[0].rearrange("g t -> (g t)").rearrange("(o n) -> o n", o=1).broadcast_to([128, TOKW]), r=(), w=(const_trk,))

        Sx.op(POOL, lambda h: h.memset(small[:, 0:1], EPS), r=(), w=(small_trk,))
        Sx.op(POOL, lambda h: h.memset(ones_bf[:], 1.0), r=(), w=(const_trk,))
        Sx.op(POOL, lambda h: h.memset(negh[:], -0.5), r=(), w=(const_trk,))
        Sx.op(POOL, lambda h: h.memset(ident[:], 1.0), r=(), w=(const_trk,))
        Sx.op(POOL, lambda h: h.affine_select(out=ident[:], in_=ident[:], pattern=[[-1, 128]], compare_op=ALU.is_equal, fill=0.0, base=0, channel_multiplier=1), r=(const_trk,), w=(const_trk,))
        Sx.op(POOL, lambda h: h.memset(mask_bf[:], 1.0), r=(), w=(const_trk,))
        Sx.op(POOL, lambda h: h.affine_select(out=mask_bf[:], in_=mask_bf[:], pattern=[[1, 128]], compare_op=ALU.is_ge, fill=0.0, base=0, channel_multiplier=-1), r=(const_trk,), w=(const_trk,))
        Sx.op(POOL, lambda h: h.memset(blk_bf[:], 0.0), r=(), w=(const_trk,))
        Sx.op(POOL, lambda h: h.memset(blk_bf[0:64, 0:64], 1.0), r=(), w=(const_trk,))
        Sx.op(POOL, lambda h: h.memset(blk_bf[64:128, 64:128], 1.0), r=(), w=(const_trk,))

        lambda_init = 0.8 - 0.6 * math.exp(-0.3 * 1)
        lam_bc, lamt = ft_rot.next()
        Sx.dma(SP, lam_bc[:, 0:256], b_lambda[0].rearrange("a b -> (a b)").rearrange("(o n) -> o n", o=1).broadcast_to([128, 256]), r=(), w=(lamt,))
        f0, f0t = ft_rot.next()
        Sx.tt(DVE, f0[:, 0:64], lam_bc[:, 0:64], lam_bc[:, 64:128], ALU.mult, r=(lamt,), w=(f0t,))
        Sx.tt(DVE, f0[:, 64:128], lam_bc[:, 128:192], lam_bc[:, 192:256], ALU.mult, r=(lamt,), w=(f0t,))
        Sx.op(DVE, lambda h: h.tensor_reduce(out=small[:, 9:11], in_=f0[:, 0:128].rearrange("p (a b) -> p a b", a=2), op=ALU.add, axis=AX.X), r=(f0t,), w=(small_trk,))
        Sx.actf(small[:, 11:13], small[:, 9:11], AF.Exp, r=(small_trk,), w=(small_trk,))
        Sx.tt(DVE, small[:, 8:9], small[:, 12:13], small[:, 11:12], ALU.subtract, r=(small_trk,), w=(small_trk,))
        Sx.ts(DVE, small[:, 8:9], small[:, 8:9], -lambda_init, None, ALU.add, None, r=(small_trk,), w=(small_trk,))
        Sx.ts(DVE, small[:, 7:8], small[:, 7:8], 1.0 - lambda_init, None, ALU.mult, None, r=(small_trk,), w=(small_trk,))

        (sid,) = alloc(1)
        wsf = slots[sid][:].bitcast(F32).rearrange("p (g s) -> p g s", g=8)
        Sx.dma(SP, wsf[:, 0:6, :], a_w_s[0].rearrange("g t s -> t g s"), r=(), w=(slot_trk[sid],))
        for g in range(6):
            pb = g % 2
            Sx.transpose(ps[pb][:, 0:128], wsf[:, g, :], ident[:], r=(slot_trk[sid], const_trk), w=(ps_trk[pb],))
            Sx.tt(DVE, wsT[:, g, :], ps[pb][:, 0:128], mask_bf[:], ALU.mult, r=(const_trk,), w=(ps_trk[pb], const_trk))
        release([sid])

        eps_col = small[:, 0:1]
        Sx.op(DVE, lambda h: h.memset(qT[:], 0.0), r=(), w=tuple(q_trk))

        def rstd_from_psum(pbank, ncols, scale, nparts=128):
            ft, ftt = ft_rot.next()
            Sx.actf(ft[0:nparts, 0:ncols], ps[pbank][0:nparts, 0:ncols], AF.Ln, r=(small_trk,), w=(ps_trk[pbank], ftt), scale=scale, bias=eps_col[0:nparts, :])
            Sx.actf(ft[0:nparts, 0:ncols], ft[0:nparts, 0:ncols], AF.Exp, r=(ftt,), w=(ftt,), scale=-0.5)
            return ft, ftt

        def norm_tile(tt, gcol0):
            t0 = tt * TW
            pb = 7
            for kc in range(KC):
                pt, ptt = pt_rot.next()
                Sx.actf(pt[:], xT[:, kc, t0:t0 + TW], AF.Square, r=(x_trk[kc][tt],), w=(ptt,))
                Sx.mm(ps[pb][:], ones_bf[:], pt[:], kc == 0, kc == KC - 1, r=(ptt, const_trk), w=(ps_trk[pb],))
            ft, ftt = rstd_from_psum(pb, TW, 1.0 / D)
            for kc in range(KC):
                Sx.stt(hT[:, kc, t0:t0 + TW], xT[:, kc, t0:t0 + TW], gcols[:, gcol0 + kc:gcol0 + kc + 1], ft[:], ALU.mult, ALU.mult,
                       r=(x_trk[kc][tt], ftt, const_trk), w=(h_trk[kc][tt],))

        def head_norm(pbank_raw, pbank_stat, ncols, gcol, outs):
            pt, ptt = pt_rot.next()
            Sx.actf(pt[:, 0:ncols], ps[pbank_raw][:, 0:ncols], AF.Square, r=(), w=(ps_trk[pbank_raw], ptt))
            Sx.mm(ps[pbank_stat][:, 0:ncols], blk_bf[:], pt[:, 0:ncols], True, True, r=(ptt, const_trk), w=(ps_trk[pbank_stat],))
            ft, ftt = rstd_from_psum(pbank_stat, ncols, 1.0 / 64)
            for (p0, p1, out_ap, out_trk) in outs:
                Sx.stt(out_ap, ps[pbank_raw][p0:p1, 0:ncols], gcol[p0:p1, :], ft[p0:p1, 0:ncols], ALU.mult, ALU.mult,
                       r=(ftt, small_trk), w=(ps_trk[pbank_raw], out_trk))

        def qz_chunk(zc):
            if zc < 4:
                return Abuf[0][:, zc, :], A_trk[0]
            if zc < 8:
                return Abuf[1][:, zc - 4, :], A_trk[1]
            return qT[:, zc - 8, :], q_trk[zc - 8]

        def load_x(b):
            for tt in range(TT):
                sids = alloc(4)
                for n in range(4):
                    tc_ = tt * 4 + n
                    Sx.dma(SP, slots[sids[n]][:].bitcast(F32), x_d[b, tc_ * 128:(tc_ + 1) * 128, :], r=(), w=(slot_trk[sids[n]],))
                for kc in range(KC):
                    pb = kc % 4
                    for n in range(4):
                        src = slots[sids[n]][:].bitcast(F32)
                        Sx.transpose(ps[pb][:, n * 128:(n + 1) * 128], src[:, kc * 128:(kc + 1) * 128], ident[:],
                                     r=(slot_trk[sids[n]], const_trk), w=(ps_trk[pb],))
                    eng = ACT if kc % 2 == 0 else DVE
                    Sx.copy(eng, xT[:, kc, tt * TW:(tt + 1) * TW], ps[pb][:], r=(), w=(ps_trk[pb], x_trk[kc][tt]))
                release(sids)

        def store_x(b):
            for tc_ in range(16):
                tt = tc_ // 4
                (sid,) = alloc(1)
                dst = slots[sid][:].bitcast(F32)
                for half in range(2):
                    pb = 4 + (tc_ * 2 + half) % 4
                    for q in range(4):
                        kc = half * 4 + q
                        Sx.transpose(ps[pb][:, q * 128:(q + 1) * 128], xT[:, kc, tc_ * 128:(tc_ + 1) * 128], ident[:],
                                     r=(x_trk[kc][tt], const_trk), w=(ps_trk[pb],))
                    eng = ACT if half == 0 else DVE
                    Sx.copy(eng, dst[:, half * 512:(half + 1) * 512], ps[pb][:], r=(), w=(ps_trk[pb], slot_trk[sid]))
                Sx.dma(SP, y_d[b, tc_ * 128:(tc_ + 1) * 128, :], dst, r=(slot_trk[sid],), w=())
                release([sid])

        def reg_ffn(b, i, j):
            win = ffn_w_in[i, j].rearrange("(kc p) f -> p kc f", p=128)
            wout = ffn_w_out[i, j]
            ngrp = (FC + GROUP - 1) // GROUP
            for g in range(ngrp):
                f0 = g * GROUP
                nf = min(GROUP, FC - f0)
                nsl = (nf + 1) // 2

                def emit(sids, f0=f0, nf=nf, nsl=nsl):
                    for s_ in range(nsl):
                        c0 = (f0 + 2 * s_) * 128
                        w_ = min(256, (f0 + nf) * 128 - c0)
                        wdma(sids[s_], slot3(sids[s_], 8, 256)[:, :, 0:w_], win[:, :, c0:c0 + w_])
                    for s_ in range(nsl):
                        c0 = DFF + (f0 + 2 * s_) * 128
                        w_ = min(256, DFF + (f0 + nf) * 128 - c0)
                        wdma(sids[nsl + s_], slot3(sids[nsl + s_], 8, 256)[:, :, 0:w_], win[:, :, c0:c0 + w_])
                    for s_ in range(nsl):
                        r0 = (f0 + 2 * s_) * 128
                        nr = min(2, f0 + nf - (f0 + 2 * s_))
                        wdma(sids[2 * nsl + s_], slot3(sids[2 * nsl + s_], 2, 1024)[:, 0:nr, :],
                             wout[r0:r0 + nr * 128, :].rearrange("(fc p) d -> p fc d", p=128))
                wreg(("ffn", b, i, j, g), 3 * nsl, emit)

        def reg_slabs(key, w2d, col0, ncol):
            wv = w2d.rearrange("(kc p) f -> p kc f", p=128)
            for s_ in range(ncol // 256):
                def emit(sids, s_=s_):
                    c0 = col0 + s_ * 256
                    wdma(sids[0], slot3(sids[0], 8, 256), wv[:, :, c0:c0 + 256])
                wreg(key + (s_,), 1, emit)

        def reg_wout(key, w2d):
            for s_ in range(4):
                def emit(sids, s_=s_):
                    wdma(sids[0], slot3(sids[0], 2, 1024), w2d[s_ * 256:(s_ + 1) * 256, :].rearrange("(c p) d -> p c d", p=128))
                wreg(key + (s_,), 1, emit)

        ycnt = [0]
        gucnt = [0]

        def ffn(b, i, j):
            for tt in range(TT):
                norm_tile(tt, (i * 2 + j) * 8)
            ngrp = (FC + GROUP - 1) // GROUP
            pending = [None]
            acnt = [0]
            for g in range(ngrp):
                f0 = g * GROUP
                nf = min(GROUP, FC - f0)
                nsl = (nf + 1) // 2
                sids = want(("ffn", b, i, j, g))
                pump(reserve=4)
                gate_s, up_s, out_s = sids[0:nsl], sids[nsl:2 * nsl], sids[2 * nsl:3 * nsl]
                for tt in range(TT):
                    t0 = tt * TW
                    ab = acnt[0] % 2
                    acnt[0] += 1
                    A = Abuf[ab]
                    for q in range(nf):
                        pg = (gucnt[0] % 2) * 2
                        pu = pg + 1
                        gucnt[0] += 1
                        gs = slot3(gate_s[q // 2], 8, 256)
                        us = slot3(up_s[q // 2], 8, 256)
                        co = (q % 2) * 128
                        for kc in range(KC):
                            Sx.mm(ps[pg][:], gs[:, kc, co:co + 128], hT[:, kc, t0:t0 + TW], kc == 0, kc == KC - 1,
                                  r=(slot_trk[gate_s[q // 2]], h_trk[kc][tt]), w=(ps_trk[pg],))
                        for kc in range(KC):
                            Sx.mm(ps[pu][:], us[:, kc, co:co + 128], hT[:, kc, t0:t0 + TW], kc == 0, kc == KC - 1,
                                  r=(slot_trk[up_s[q // 2]], h_trk[kc][tt]), w=(ps_trk[pu],))
                        ft, ftt = ft_rot.next()
                        Sx.actf(ft[:], ps[pg][:], AF.Silu, r=(), w=(ps_trk[pg], ftt))
                        Sx.tt(DVE, A[:, q, :], ft[:], ps[pu][:], ALU.mult, r=(ftt,), w=(ps_trk[pu], A_trk[ab]))
                    if pending[0] is not None:
                        pending[0]()

                    def ywork(tt=tt, t0=t0, A=A, ab=ab, nf=nf, out_s=out_s):
                        for dc in range(KC):
                            py = 4 + ycnt[0] % 3
                            ycnt[0] += 1
                            for q in range(nf):
                                ws_ = slot3(out_s[q // 2], 2, 1024)
                                Sx.mm(ps[py][:], ws_[:, q % 2, dc * 128:(dc + 1) * 128], A[:, q, :], q == 0, q == nf - 1,
                                      r=(slot_trk[out_s[q // 2]], A_trk[ab]), w=(ps_trk[py],))
                            Sx.stt(xT[:, dc, t0:t0 + TW], ps[py][:], 0.5, xT[:, dc, t0:t0 + TW], ALU.mult, ALU.add,
                                   r=(), w=(ps_trk[py], x_trk[dc][tt]))
                    pending[0] = ywork
                if g == ngrp - 1:
                    pending[0]()
                    pending[0] = None
                    release(sids)
                else:
                    pending[0]()
                    pending[0] = None
                    release(sids)

        def mem_kv(b, i):
            sm = alloc(2)
            (sh,) = alloc(1)
            mh = slot3(sh, 8, 256)
            for lc in range(2):
                mf = slots[sm[lc]][:].bitcast(F32)
                Sx.dma(SP, mf, mem_d[b, lc * 128:(lc + 1) * 128, :], r=(), w=(slot_trk[sm[lc]],))
                ft, ftt = ft_rot.next()
                for hf in range(2):
                    Sx.tt(DVE, ft[:], mf[:, hf * 512:(hf + 1) * 512], mf[:, hf * 512:(hf + 1) * 512], ALU.mult, r=(slot_trk[sm[lc]],), w=(ftt,))
                    Sx.op(DVE, lambda h, ft=ft, hf=hf: h.tensor_reduce(out=small[:, 16 + hf:17 + hf], in_=ft[:], op=ALU.add, axis=AX.X), r=(ftt,), w=(small_trk,))
                Sx.tt(DVE, small[:, 18:19], small[:, 16:17], small[:, 17:18], ALU.add, r=(small_trk,), w=(small_trk,))
                Sx.actf(small[:, 19:20], small[:, 18:19], AF.Ln, r=(small_trk,), w=(small_trk,), scale=1.0 / D, bias=eps_col)
                Sx.actf(small[:, 20:21], small[:, 19:20], AF.Exp, r=(small_trk,), w=(small_trk,), scale=-0.5)
                Sx.ts(DVE, mf, mf, small[:, 20:21], None, ALU.mult, None, r=(small_trk, slot_trk[sm[lc]]), w=(slot_trk[sm[lc]],))
            for kc in range(KC):
                pb = kc % 2
                for lc in range(2):
                    mf = slots[sm[lc]][:].bitcast(F32)
                    Sx.transpose(ps[pb][:, lc * 128:(lc + 1) * 128], mf[:, kc * 128:(kc + 1) * 128], ident[:],
                                 r=(slot_trk[sm[lc]], const_trk), w=(ps_trk[pb],))
                Sx.ts(DVE, mh[:, kc, :], ps[pb][:, 0:256], gcols[:, 48 + 8 * i + kc:48 + 8 * i + kc + 1], None, ALU.mult, None,
                      r=(const_trk,), w=(ps_trk[pb], slot_trk[sh]))
            release(sm)
            ks = want(("memkv", b, i, 0))[0]
            vs = want(("memkv", b, i, 1))[0]
            kw = slot3(ks, 8, 256)
            vw = slot3(vs, 8, 256)
            for hc in range(2):
                pr = 2 + hc
                for kc in range(KC):
                    Sx.mm(ps[pr][:, 0:256], kw[:, kc, hc * 128:(hc + 1) * 128], mh[:, kc, :], kc == 0, kc == KC - 1,
                          r=(slot_trk[ks], slot_trk[sh]), w=(ps_trk[pr],))
                head_norm(pr, 7, 256, small[:, 3 + i:4 + i], [(0, 128, kmT[:, hc, :], km_trk)])
            for lc in range(2):
                pr = 4 + lc
                for kc in range(KC):
                    Sx.mm(ps[pr][:, 0:256], mh[:, kc, lc * 128:(lc + 1) * 128], vw[:, kc, :], kc == 0, kc == KC - 1,
                          r=(slot_trk[vs], slot_trk[sh]), w=(ps_trk[pr],))
                Sx.copy(ACT, vm[:, lc, :], ps[pr][:, 0:256], r=(), w=(ps_trk[pr], vm_trk))
            release([ks, vs, sh])

        def mem_attn(tt):
            for hc in range(2):
                for hh in range(2):
                    hm = hc * 2 + hh
                    pnum, pden = 2 * hh, 2 * hh + 1
                    for lc in range(2):
                        pss = 4 + (hm * 2 + lc) % 3
                        Sx.mm(ps[pss][:], kmT[:, hc, lc * 128:(lc + 1) * 128], qT[:, 4 + hm, :], True, True,
                              r=(km_trk, q_trk[4 + hm]), w=(ps_trk[pss],))
                        pt, ptt = pt_rot.next()
                        Sx.actf(pt[:], ps[pss][:], AF.Exp, r=(), w=(ps_trk[pss], ptt), scale=0.125)
                        Sx.mm(ps[pnum][:], vm[:, lc, hc * 128:(hc + 1) * 128], pt[:], lc == 0, lc == 1,
                              r=(vm_trk, ptt), w=(ps_trk[pnum],))
                        Sx.mm(ps[pden][:], ones_bf[:], pt[:], lc == 0, lc == 1,
                              r=(const_trk, ptt), w=(ps_trk[pden],))
                for hh in range(2):
                    pnum, pden = 2 * hh, 2 * hh + 1
                    r0 = hh * 64
                    ft, ftt = ft_rot.next()
                    Sx.actf(ft[r0:r0 + 64, :], ps[pden][r0:r0 + 64, :], AF.Ln, r=(), w=(ps_trk[pden], ftt))
                    Sx.actf(ft[r0:r0 + 64, :], ft[r0:r0 + 64, :], AF.Exp, r=(ftt,), w=(ftt,), scale=-1.0)
                    Sx.tt(DVE, tokT[r0:r0 + 64, 6 + hc, :], ps[pnum][r0:r0 + 64, :], ft[r0:r0 + 64, :], ALU.mult, r=(ftt,), w=(ps_trk[pnum], tok_trk[6 + hc]))

        def out_proj(b, i, tt):
            t0 = tt * TW
            ws_ = [want(("wout", b, i, tt, s_))[0] for s_ in range(4)]
            for dc in range(KC):
                py = 4 + ycnt[0] % 3
                ycnt[0] += 1
                for c in range(8):
                    wv = slot3(ws_[c // 2], 2, 1024)
                    Sx.mm(ps[py][:], wv[:, c % 2, dc * 128:(dc + 1) * 128], tokT[:, c, :], c == 0, c == 7,
                          r=(slot_trk[ws_[c // 2]], tok_trk[c]), w=(ps_trk[py],))
                Sx.tt(DVE, xT[:, dc, t0:t0 + TW], ps[py][:], xT[:, dc, t0:t0 + TW], ALU.add, r=(), w=(ps_trk[py], x_trk[dc][tt]))
            release(ws_)

        def qm_proj(tt, slab_sid, memq_col):
            t0 = tt * TW
            wv = slot3(slab_sid, 8, 256)
            for hc in range(2):
                pr = 2 + hc
                for kc in range(KC):
                    Sx.mm(ps[pr][:], wv[:, kc, hc * 128:(hc + 1) * 128], hT[:, kc, t0:t0 + TW], kc == 0, kc == KC - 1,
                          r=(slot_trk[slab_sid], h_trk[kc][tt]), w=(ps_trk[pr],))
                head_norm(pr, 7, TW, memq_col, [(0, 64, qT[0:64, 4 + 2 * hc, :], q_trk[4 + 2 * hc]), (64, 128, qT[64:128, 5 + 2 * hc, :], q_trk[5 + 2 * hc])])

        def mixer_a(b):
            i = 0
            for tt in range(TT):
                norm_tile(tt, 32)
            mem_kv(b, i)
            slabs = [want(("a_in", b, s_))[0] for s_ in range(7)]
            pump(reserve=4)
            for tt in range(TT):
                t0 = tt * TW
                qm_proj(tt, slabs[6], small[:, 1:2])
                for g in range(6):
                    us = slot3(slabs[g // 2], 8, 256)
                    vs = slot3(slabs[3 + g // 2], 8, 256)
                    co = (g % 2) * 128
                    pu, pv, pm = 0, 1, 2 + g % 2
                    for kc in range(KC):
                        Sx.mm(ps[pu][:], us[:, kc, co:co + 128], hT[:, kc, t0:t0 + TW], kc == 0, kc == KC - 1,
                              r=(slot_trk[slabs[g // 2]], h_trk[kc][tt]), w=(ps_trk[pu],))
                    for n in range(4):
                        for kc in range(KC):
                            Sx.mm(ps[pv][:, n * 128:(n + 1) * 128], hT[:, kc, t0 + n * 128:t0 + (n + 1) * 128], vs[:, kc, co:co + 128],
                                  kc == 0, kc == KC - 1, r=(slot_trk[slabs[3 + g // 2]], h_trk[kc][tt]), w=(ps_trk[pv],))
                    fu, fut = ft_rot.next()
                    Sx.actf(fu[:], ps[pu][:], AF.Gelu, r=(), w=(ps_trk[pu], fut))
                    fv, fvt = ft_rot.next()
                    Sx.actf(fv[:], ps[pv][:], AF.Gelu, r=(), w=(ps_trk[pv], fvt))
                    fs, fst = ft_rot.next()
                    Sx.tt(DVE, fs[:], fv[:], fv[:], ALU.mult, r=(fvt,), w=(fst,))
                    gs_, gst = gst_rot.next()
                    Sx.op(DVE, lambda h, fs=fs, gs_=gs_: h.tensor_reduce(out=gs_[:, 0:4], in_=fs[:].rearrange("p (a b) -> p a b", a=4), op=ALU.add, axis=AX.X),
                          r=(fst,), w=(gst,))
                    Sx.ts(DVE, gs_[:, 0:4], gs_[:, 0:4], 1.0 / 128, EPS, ALU.mult, ALU.add, r=(gst,), w=(gst,))
                    Sx.tt(POOL, gs_[:, 4:8], gs_[:, 0:4], negh[:, 0:4], ALU.pow, r=(gst, const_trk), w=(gst,))
                    pt, ptt = pt_rot.next()
                    for n in range(4):
                        Sx.stt(pt[:, n * 128:(n + 1) * 128], fv[:, n * 128:(n + 1) * 128], gs_[:, 4 + n:5 + n], avn_bc[:, g * 128:(g + 1) * 128],
                               ALU.mult, ALU.mult, r=(fvt, gst, const_trk), w=(ptt,))
                    for n in range(4):
                        Sx.mm(ps[pm][:, n * 128:(n + 1) * 128], pt[:, n * 128:(n + 1) * 128], wsT[:, g, :], True, True,
                              r=(ptt, const_trk), w=(ps_trk[pm],))
                    Sx.tt(DVE, fs[:].rearrange("p (a b) -> p a b", a=4), ps[pm][:].rearrange("p (a b) -> p a b", a=4),
                          abs_bc[:, g * 128:(g + 1) * 128].unsqueeze(1).broadcast_to([128, 4, 128]), ALU.add,
                          r=(const_trk,), w=(ps_trk[pm], fst))
                    Sx.tt(POOL, tokT[:, g, :], fs[:], fu[:], ALU.mult, r=(fst, fut), w=(tok_trk[g],))
                mem_attn(tt)
                if tt == TT - 1:
                    release(slabs)
                out_proj(b, i, tt)
                pump(reserve=4)

        def mixer_b(b):
            i = 1
            for ab in range(2):
                Sx.op(DVE, lambda h, ab=ab: h.memset(Abuf[ab][:], 0.0), r=(), w=(A_trk[ab],))
            for tt in range(TT):
                norm_tile(tt, 40)
            mem_kv(b, i)
            kvs = want(("kv", b))
            k_s, v_s = kvs[0:6], kvs[6:12]
            kslabs = [want(("b_k", b, s_))[0] for s_ in range(3)]
            for c in range(6):
                wv = slot3(kslabs[c // 2], 8, 256)
                co = (c % 2) * 128
                for tt in range(TT):
                    t0 = tt * TW
                    pr = 2 + (c * TT + tt) % 2
                    for kc in range(KC):
                        Sx.mm(ps[pr][:], wv[:, kc, co:co + 128], hT[:, kc, t0:t0 + TW], kc == 0, kc == KC - 1,
                              r=(slot_trk[kslabs[c // 2]], h_trk[kc][tt]), w=(ps_trk[pr],))
                    head_norm(pr, 7, TW, small[:, 6:7], [(0, 128, slots[k_s[c]][:, t0:t0 + TW], slot_trk[k_s[c]])])
            release(kslabs)
            vslabs = [want(("b_v", b, s_))[0] for s_ in range(3)]
            for tc_ in range(16):
                tt = tc_ // 4
                for s_ in range(3):
                    wv = slot3(vslabs[s_], 8, 256)
                    pb = 4 + (tc_ * 3 + s_) % 3
                    for kc in range(KC):
                        Sx.mm(ps[pb][:, 0:256], hT[:, kc, tc_ * 128:(tc_ + 1) * 128], wv[:, kc, :], kc == 0, kc == KC - 1,
                              r=(slot_trk[vslabs[s_]], h_trk[kc][tt]), w=(ps_trk[pb],))
                    for hh in range(2):
                        h_ = s_ * 2 + hh
                        vdst = slot3(v_s[h_], 16, 128)
                        Sx.copy(ACT if hh == 0 else DVE, vdst[:, tc_, :], ps[pb][:, hh * 128:(hh + 1) * 128], r=(), w=(ps_trk[pb], slot_trk[v_s[h_]]))
            release(vslabs)
            for tt in range(TT):
                t0 = tt * TW
                qs = [want(("b_q", b, tt, s_))[0] for s_ in range(3)]
                for c in range(6):
                    wv = slot3(qs[c // 2], 8, 256)
                    co = (c % 2) * 128
                    pr = 2 + c % 2
                    for kc in range(KC):
                        Sx.mm(ps[pr][:], wv[:, kc, co:co + 128], hT[:, kc, t0:t0 + TW], kc == 0, kc == KC - 1,
                              r=(slot_trk[qs[c // 2]], h_trk[kc][tt]), w=(ps_trk[pr],))
                    za, zat = qz_chunk(2 * c)
                    zb, zbt = qz_chunk(2 * c + 1)
                    head_norm(pr, 7, TW, small[:, 5:6], [(0, 64, za[0:64, :], zat), (64, 128, zb[64:128, :], zbt)])
                release(qs)
                qms = want(("b_qm", b, tt, 0))[0]
                qm_proj(tt, qms, small[:, 2:3])
                release([qms])
                pump(reserve=4)
                nk = (tt + 1) * 4
                deferred = []

                def tick():
                    for d_ in deferred:
                        d_[0] -= 1
                    while deferred and deferred[0][0] <= 0:
                        deferred.pop(0)[1]()

                for h_ in range(6):
                    vsl = slot3(v_s[h_], 16, 128)
                    steps = []
                    for c in range(2):
                        for kc in range(nk):
                            steps.append((c, kc))
                    scnt = [0]
                    inflight = []
                    hst = {}

                    def stageA(c, kc, h_=h_):
                        j = kc - tt * 4
                        c0 = max(j, 0) * 128
                        pss = 4 + scnt[0] % 3
                        scnt[0] += 1
                        zq, zqt = qz_chunk(2 * h_ + c)
                        Sx.mm(ps[pss][:, c0:TW], slots[k_s[h_]][:, kc * 128:(kc + 1) * 128], zq[:, c0:TW],
                              True, True, r=(slot_trk[k_s[h_]], zqt), w=(ps_trk[pss],))
                        return (c, kc, j, c0, pss)

                    def evac(c, hst=hst, h_=h_):
                        f_, ft_ = ft_rot.next()
                        pn, pd = c * 2, c * 2 + 1
                        Sx.actf(f_[:], ps[pd][:], AF.Ln, r=(), w=(ps_trk[pd], ft_))
                        Sx.actf(f_[:], f_[:], AF.Exp, r=(ft_,), w=(ft_,), scale=-1.0)
                        Sx.tt(DVE, f_[:], ps[pn][:], f_[:], ALU.mult, r=(ft_,), w=(ps_trk[pn], ft_))
                        hst[c] = (f_, ft_)
                        if c == 1:
                            fa, fat = hst[0]
                            fb, fbt = hst[1]
                            Sx.stt(fa[:], fb[:], small[:, 8:9], fa[:], ALU.mult, ALU.add, r=(fbt, fat, small_trk), w=(fat,))
                            pt, ptt = pt_rot.next()
                            Sx.tt(DVE, pt[:], fa[:], fa[:], ALU.mult, r=(fat,), w=(ptt,))
                            Sx.mm(ps[7][:], ones_bf[:], pt[:], True, True, r=(ptt, const_trk), w=(ps_trk[7],))

                            def finalize(fa=fa, fat=fat, h_=h_):
                                fr, frt = rstd_from_psum(7, TW, 1.0 / 128)
                                Sx.stt(tokT[:, h_, :], fa[:], small[:, 7:8], fr[:], ALU.mult, ALU.mult, r=(fat, frt, small_trk), w=(tok_trk[h_],))
                            deferred.append([2, finalize])

                    def stageB(st, vsl=vsl, h_=h_, evac=evac):
                        c, kc, j, c0, pss = st
                        pt, ptt = pt_rot.next()
                        Sx.actf(pt[:, c0:TW], ps[pss][:, c0:TW], AF.Exp, r=(), w=(ps_trk[pss], ptt), scale=0.125)
                        if j >= 0:
                            Sx.tt(DVE, pt[:, c0:c0 + 128], pt[:, c0:c0 + 128], mask_bf[:], ALU.mult, r=(ptt, const_trk), w=(ptt,))
                        pn, pd = c * 2, c * 2 + 1
                        Sx.mm(ps[pn][:, c0:TW], vsl[:, kc, :], pt[:, c0:TW], kc == 0, kc == nk - 1,
                              r=(slot_trk[v_s[h_]], ptt), w=(ps_trk[pn],))
                        Sx.mm(ps[pd][:, c0:TW], ones_bf[:], pt[:, c0:TW], kc == 0, kc == nk - 1,
                              r=(const_trk, ptt), w=(ps_trk[pd],))
                        tick()
                        if kc == nk - 1:
                            deferred.append([2, lambda c=c, evac=evac: evac(c)])

                    for si, (c, kc) in enumerate(steps):
                        inflight.append(stageA(c, kc))
                        if len(inflight) > 2:
                            stageB(inflight.pop(0))
                    while inflight:
                        stageB(inflight.pop(0))
                while deferred:
                    deferred.pop(0)[1]()
                mem_attn(tt)
                out_proj(b, i, tt)
            release(kvs)

        for b in range(nb):
            reg_ffn(b, 0, 0)
            reg_slabs(("memkv", b, 0), w_mem_kv[0], 0, 512)
            reg_slabs(("a_in", b), a_w_in[0], 0, 1792)
            for tt in range(TT):
                reg_wout(("wout", b, 0, tt), w_out[0])
            reg_ffn(b, 0, 1)
            reg_ffn(b, 1, 0)
            reg_slabs(("memkv", b, 1), w_mem_kv[1], 0, 512)
            wreg(("kv", b), 12, lambda sids: None)
            reg_slabs(("b_k", b), b_w_in[0], 768, 768)
            reg_slabs(("b_v", b), b_w_in[0], 1536, 768)
            for tt in range(TT):
                reg_slabs(("b_q", b, tt), b_w_in[0], 0, 768)
                reg_slabs(("b_qm", b, tt), b_w_in[0], 2304, 256)
                reg_wout(("wout", b, 1, tt), w_out[1])
            reg_ffn(b, 1, 1)

        for b in range(nb):
            load_x(b)
            st = 0
            for i in range(2):
                for sub in range(3):
                    if st >= stages:
                        break
                    if sub == 0:
                        ffn(b, i, 0)
                    elif sub == 1:
                        (mixer_a if i == 0 else mixer_b)(b)
                    else:
                        ffn(b, i, 1)
                    st += 1
            store_x(b)
        Sx.wait_all_dma(SP)

        block = es.enter_context(nc.Block())

        @block.tensor
        def _(h):
            for f in PE.prog:
                f(h)

        @block.scalar
        def _(h):
            for f in ACT.prog:
                f(h)

        @block.vector
        def _(h):
            for f in DVE.prog:
                f(h)

        @block.gpsimd
        def _(h):
            for f in POOL.prog:
                f(h)

        @block.sync
        def _(h):
            for f in SP.prog:
                f(h)
    return nc


_NC_CACHE = {}


def kernel(**inputs):
    x = np.ascontiguousarray(inputs["x"], dtype=np.float32)
    mem = np.ascontiguousarray(inputs["mem"], dtype=np.float32)
    if "nc" not in _NC_CACHE:
        _NC_CACHE["nc"] = build_nc()
    nc = _NC_CACHE["nc"]
    shared = {k: np.ascontiguousarray(v, dtype=np.float32) for k, v in inputs.items() if k not in ("x", "mem")}
    in_maps = []
    for c in range(N_CORES):
        m = dict(shared)
        m["x"] = x[c * NB_CORE:(c + 1) * NB_CORE]
        m["mem"] = mem[c * NB_CORE:(c + 1) * NB_CORE]
        in_maps.append(m)
    res = run_bass_kernel_spmd(nc, in_maps, core_ids=list(range(N_CORES)))
    return np.concatenate([r["y"] for r in res.results], axis=0)
```

```python
import math
from contextlib import ExitStack

import numpy as np
import concourse.bass as bass
import concourse.mybir as mybir
from concourse.bass_utils import run_bass_kernel_spmd

F32 = mybir.dt.float32
BF16 = mybir.dt.bfloat16
AF = mybir.ActivationFunctionType
ALU = mybir.AluOpType
AX = mybir.AxisListType

N_CORES = 8
D = 1024
S = 2048
MEM_L = 256
KC = 8
TT = 4
TW = 512
DFF = 2816
FC = 22
TOKW = 768
EPS = 1e-6
NB_CORE = 2
GROUP = 4
NDSEM = 12


class Trk:
    __slots__ = ("w", "r")

    def __init__(self):
        self.w = None
        self.r = {}


class Eng:
    def __init__(self, name, sem, is_pe=False):
        self.name = name
        self.sem = sem
        self.cnt = 0
        self.seen = {}
        self.prog = []
        self.is_pe = is_pe
        self.strict = False
        self.dsems = []
        self.dvals = []
        self.dnext = 0


class Sched:
    def __init__(self, nc, es):
        self.nc = nc
        self.engs = {}
        for name in ("pe", "act", "dve", "pool", "sp"):
            sem = es.enter_context(nc.semaphore("tl_" + name))
            self.engs[name] = Eng(name, sem, is_pe=(name == "pe"))
        for name in ("pool", "sp"):
            e = self.engs[name]
            for i in range(NDSEM):
                e.dsems.append(es.enter_context(nc.semaphore(f"d_{name}{i}")))
                e.dvals.append(0)
        self.pe, self.act, self.dve, self.pool, self.sp = (self.engs[n] for n in ("pe", "act", "dve", "pool", "sp"))

    def _wait(self, eng, tk):
        sem, val, src = tk
        key = sem.num
        if eng.seen.get(key, 0) >= val:
            return
        eng.seen[key] = val
        eng.prog.append(lambda h, sem=sem, val=val: h.wait_ge(sem, val))

    def _deps(self, eng, r, w):
        for t in r:
            tk = t.w
            if tk is not None:
                if tk[2] is eng:
                    if not eng.is_pe:
                        self._wait(eng, tk)
                else:
                    self._wait(eng, tk)
        strict = eng.strict
        for t in w:
            tk = t.w
            if tk is not None and (tk[2] is not eng or strict):
                self._wait(eng, tk)
            for tk in t.r.values():
                if tk[2] is not eng or strict:
                    self._wait(eng, tk)

    def _mark(self, tk, r, w, key):
        for t in w:
            t.w = tk
            t.r = {}
        for t in r:
            t.r[key] = tk

    def op(self, eng, fn, r=(), w=()):
        self._deps(eng, r, w)
        eng.cnt += 1
        sem = eng.sem
        eng.prog.append(lambda h, fn=fn, sem=sem: fn(h).then_inc(sem, 1))
        tk = (sem, eng.cnt, eng)
        self._mark(tk, r, w, eng.name)
        return tk

    def dma(self, eng, out, in_, r=(), w=(), allow=False):
        self._deps(eng, r, w)
        i = eng.dnext
        eng.dnext = (i + 1) % len(eng.dsems)
        sem = eng.dsems[i]
        if eng.dvals[i] > 0:
            self._wait(eng, (sem, eng.dvals[i], None))
        eng.dvals[i] += 16
        val = eng.dvals[i]
        if allow:
            eng.prog.append(lambda h, out=out, in_=in_, sem=sem: h.dma_start(out=out, in_=in_, allow_slow_non_contiguous=True).then_inc(sem, 16))
        else:
            eng.prog.append(lambda h, out=out, in_=in_, sem=sem: h.dma_start(out=out, in_=in_).then_inc(sem, 16))
        tk = (sem, val, None)
        self._mark(tk, r, w, "dma_%s_%d" % (eng.name, i))
        return tk

    def wait_all_dma(self, eng):
        for e in (self.pool, self.sp):
            for i, sem in enumerate(e.dsems):
                if e.dvals[i] > 0:
                    self._wait(eng, (sem, e.dvals[i], None))

    def mm(self, out, lhsT, rhs, start, stop, r, w):
        return self.op(self.pe, lambda h: h.matmul(out, lhsT, rhs, start=start, stop=stop), r=r, w=w)

    def transpose(self, out, in_, ident, r, w):
        return self.op(self.pe, lambda h: h.transpose(out, in_, ident), r=r, w=w)

    def actf(self, out, in_, func, r, w, scale=None, bias=None):
        kw = {}
        if scale is not None:
            kw["scale"] = scale
        if bias is not None:
            kw["bias"] = bias
        return self.op(self.act, lambda h: h.activation(out=out, in_=in_, func=func, **kw), r=r, w=w)

    def tt(self, eng, out, in0, in1, op, r, w):
        return self.op(eng, lambda h: h.tensor_tensor(out=out, in0=in0, in1=in1, op=op), r=r, w=w)

    def ts(self, eng, out, in0, s1, s2, op0, op1, r, w):
        if op1 is None:
            return self.op(eng, lambda h: h.tensor_scalar(out=out, in0=in0, scalar1=s1, scalar2=None, op0=op0), r=r, w=w)
        return self.op(eng, lambda h: h.tensor_scalar(out=out, in0=in0, scalar1=s1, scalar2=s2, op0=op0, op1=op1), r=r, w=w)

    def stt(self, out, in0, scalar, in1, op0, op1, r, w):
        return self.op(self.dve, lambda h: h.scalar_tensor_tensor(out=out, in0=in0, scalar=scalar, in1=in1, op0=op0, op1=op1), r=r, w=w)

    def recip(self, out, in_, r, w):
        return self.op(self.dve, lambda h: h.reciprocal(out=out, in_=in_), r=r, w=w)

    def copy(self, eng, out, in_, r, w):
        if eng is self.act:
            return self.op(eng, lambda h: h.copy(out=out, in_=in_), r=r, w=w)
        return self.op(eng, lambda h: h.tensor_copy(out=out, in_=in_), r=r, w=w)


class Rot:
    def __init__(self, items):
        self.items = items
        self.i = 0

    def next(self):
        it = self.items[self.i]
        self.i = (self.i + 1) % len(self.items)
        return it


def build_nc(nb=NB_CORE, stages=6, nslot=None):
    nc = bass.Bass("TRN2", target_bir_lowering=False)
    dr = {}

    def din(name, shape):
        dr[name] = nc.dram_tensor(name, list(shape), F32, kind="ExternalInput").ap()
        return dr[name]

    x_d = din("x", (nb, S, D))
    mem_d = din("mem", (nb, MEM_L, D))
    ffn_norm = din("ffn_norm", (2, 2, D))
    ffn_w_in = din("ffn_w_in", (2, 2, D, 2 * DFF))
    ffn_w_out = din("ffn_w_out", (2, 2, DFF, D))
    mix_norm = din("mix_norm", (2, D))
    mem_norm = din("mem_norm", (2, D))
    w_mem_kv = din("w_mem_kv", (2, D, 512))
    memq_norm = din("memq_norm", (2, 64))
    memk_norm = din("memk_norm", (2, 64))
    w_out = din("w_out", (2, D, D))
    a_w_in = din("a_w_in", (1, D, 1792))
    a_v_norm = din("a_v_norm", (1, TOKW))
    a_w_s = din("a_w_s", (1, 6, 128, 128))
    a_b_s = din("a_b_s", (1, 6, 128))
    b_w_in = din("b_w_in", (1, D, 2560))
    b_q_norm = din("b_q_norm", (1, 64))
    b_k_norm = din("b_k_norm", (1, 64))
    b_lambda = din("b_lambda", (1, 4, 64))
    b_subln = din("b_subln", (1, 128))
    y_d = nc.dram_tensor("y", [nb, S, D], F32, kind="ExternalOutput").ap()

    es = ExitStack()
    with es:
        def sb(name, shape, dt):
            return es.enter_context(nc.sbuf_tensor(name, list(shape), dt))

        xT = sb("xT", (128, KC, S), F32)
        hT = sb("hT", (128, KC, S), BF16)
        Abuf = [sb(f"A{i}", (128, GROUP, TW), BF16) for i in range(2)]
        tokT = sb("tokT", (128, 8, TW), BF16)
        qT = sb("qT", (128, 8, TW), BF16)
        ptl = [sb(f"pt{i}", (128, TW), BF16) for i in range(3)]
        ftl = [sb(f"ft{i}", (128, TW), F32) for i in range(4)]
        ident = sb("ident", (128, 128), F32)
        ones_bf = sb("ones_bf", (128, 128), BF16)
        blk_bf = sb("blk_bf", (128, 128), BF16)
        mask_bf = sb("mask_bf", (128, 128), BF16)
        wsT = sb("wsT", (128, 6, 128), BF16)
        gcols = sb("gcols", (128, 64), F32)
        avn_bc = sb("avn_bc", (128, TOKW), F32)
        abs_bc = sb("abs_bc", (128, TOKW), F32)
        small = sb("small", (128, 32), F32)
        negh = sb("negh", (128, 4), F32)
        gstl = [sb(f"gst{i}", (128, 8), F32) for i in range(3)]
        kmT = sb("kmT", (128, 2, MEM_L), BF16)
        vm = sb("vm", (128, 2, 256), BF16)
        if nslot is None:
            nslot = 16
        slots = [sb(f"slot{i}", (128, 2048), BF16) for i in range(nslot)]
        ps = [es.enter_context(nc.psum_tensor(f"ps{i}", [128, TW], F32)) for i in range(8)]

        Sx = Sched(nc, es)
        PE, ACT, DVE, POOL, SP = Sx.pe, Sx.act, Sx.dve, Sx.pool, Sx.sp

        x_trk = [[Trk() for _ in range(TT)] for _ in range(KC)]
        h_trk = [[Trk() for _ in range(TT)] for _ in range(KC)]
        A_trk = [Trk(), Trk()]
        tok_trk = [Trk() for _ in range(8)]
        q_trk = [Trk() for _ in range(8)]
        ps_trk = [Trk() for _ in range(8)]
        slot_trk = [Trk() for _ in range(nslot)]
        const_trk = Trk()
        km_trk = Trk()
        vm_trk = Trk()
        small_trk = Trk()
        pt_rot = Rot([(ptl[i], Trk()) for i in range(3)])
        ft_rot = Rot([(ftl[i], Trk()) for i in range(4)])
        gst_rot = Rot([(gstl[i], Trk()) for i in range(3)])
        at_tiles = []
        for ab in range(2):
            af = Abuf[ab][:].rearrange("p a b -> p (a b)").bitcast(F32)
            at_tiles += [af[:, 0:TW], af[:, TW:2 * TW]]
        at_trk = [Trk() for _ in range(4)]

        free_slots = list(range(nslot))

        def alloc(n):
            assert len(free_slots) >= n, "slot pool exhausted"
            got = free_slots[:n]
            del free_slots[:n]
            return got

        def release(ids):
            free_slots.extend(ids)

        wq = []
        wq_pos = [0]
        wq_map = {}

        def wreg(key, n, emit):
            ent = {"key": key, "n": n, "emit": emit, "slots": None}
            wq.append(ent)
            wq_map[key] = ent

        def pump(reserve=4):
            while wq_pos[0] < len(wq):
                ent = wq[wq_pos[0]]
                if len(free_slots) - reserve < ent["n"]:
                    break
                ent["slots"] = alloc(ent["n"])
                ent["emit"](ent["slots"])
                wq_pos[0] += 1

        def want(key):
            ent = wq_map[key]
            while ent["slots"] is None:
                nxt = wq[wq_pos[0]]
                nxt["slots"] = alloc(nxt["n"])
                nxt["emit"](nxt["slots"])
                wq_pos[0] += 1
            return ent["slots"]

        def wdma(slot_id, out_ap, in_ap):
            Sx.dma(POOL, out_ap, in_ap, r=(), w=(slot_trk[slot_id],))

        def slot3(sid, a, b):
            return slots[sid][:].rearrange("p (a b) -> p a b", a=a, b=b)

        def gload(col, vec_ap):
            Sx.dma(SP, gcols[:, col:col + KC], vec_ap.rearrange("(kc p) -> p kc", p=128), r=(), w=(const_trk,), allow=True)

        for i in range(2):
            for j in range(2):
                gload((i * 2 + j) * 8, ffn_norm[i, j])
            gload(32 + 8 * i, mix_norm[i])
            gload(48 + 8 * i, mem_norm[i])

        def hload(col, vec_ap):
            v = vec_ap.rearrange("(p o) -> p o", o=1)
            Sx.dma(SP, small[0:64, col:col + 1], v, r=(), w=(small_trk,), allow=True)
            Sx.dma(SP, small[64:128, col:col + 1], v, r=(), w=(small_trk,), allow=True)

        hload(1, memq_norm[0]); hload(2, memq_norm[1]); hload(3, memk_norm[0]); hload(4, memk_norm[1])
        hload(5, b_q_norm[0]); hload(6, b_k_norm[0])
        Sx.dma(SP, small[:, 7:8], b_subln[0].rearrange("(p o) -> p o", o=1), r=(), w=(small_trk,), allow=True)
        Sx.dma(SP, avn_bc[:], a_v_norm[0:1, :].broadcast_to([128, TOKW]), r=(), w=(const_trk,))
        Sx.dma(SP, abs_bc[:], a_b_s[0].rearrange("g t -> (g t)").rearrange("(o n) -> o n", o=1).broadcast_to([128, TOKW]), r=(), w=(const_trk,))

        Sx.op(POOL, lambda h: h.memset(small[:, 0:1], EPS), r=(), w=(small_trk,))
        Sx.op(POOL, lambda h: h.memset(ones_bf[:], 1.0), r=(), w=(const_trk,))
        Sx.op(POOL, lambda h: h.memset(negh[:], -0.5), r=(), w=(const_trk,))
        Sx.op(POOL, lambda h: h.memset(ident[:], 1.0), r=(), w=(const_trk,))
        Sx.op(POOL, lambda h: h.affine_select(out=ident[:], in_=ident[:], pattern=[[-1, 128]], compare_op=ALU.is_equal, fill=0.0, base=0, channel_multiplier=1), r=(const_trk,), w=(const_trk,))
        Sx.op(POOL, lambda h: h.memset(mask_bf[:], 1.0), r=(), w=(const_trk,))
        Sx.op(POOL, lambda h: h.affine_select(out=mask_bf[:], in_=mask_bf[:], pattern=[[1, 128]], compare_op=ALU.is_ge, fill=0.0, base=0, channel_multiplier=-1), r=(const_trk,), w=(const_trk,))
        Sx.op(POOL, lambda h: h.memset(blk_bf[:], 0.0), r=(), w=(const_trk,))
        Sx.op(POOL, lambda h: h.memset(blk_bf[0:64, 0:64], 1.0), r=(), w=(const_trk,))
        Sx.op(POOL, lambda h: h.memset(blk_bf[64:128, 64:128], 1.0), r=(), w=(const_trk,))

        lambda_init = 0.8 - 0.6 * math.exp(-0.3 * 1)
        lam_bc, lamt = ft_rot.next()
        Sx.dma(SP, lam_bc[:, 0:256], b_lambda[0].rearrange("a b -> (a b)").rearrange("(o n) -> o n", o=1).broadcast_to([128, 256]), r=(), w=(lamt,))
        f0, f0t = ft_rot.next()
        Sx.tt(DVE, f0[:, 0:64], lam_bc[:, 0:64], lam_bc[:, 64:128], ALU.mult, r=(lamt,), w=(f0t,))
        Sx.tt(DVE, f0[:, 64:128], lam_bc[:, 128:192], lam_bc[:, 192:256], ALU.mult, r=(lamt,), w=(f0t,))
        Sx.op(DVE, lambda h: h.tensor_reduce(out=small[:, 9:11], in_=f0[:, 0:128].rearrange("p (a b) -> p a b", a=2), op=ALU.add, axis=AX.X), r=(f0t,), w=(small_trk,))
        Sx.actf(small[:, 11:13], small[:, 9:11], AF.Exp, r=(small_trk,), w=(small_trk,))
        Sx.tt(DVE, small[:, 8:9], small[:, 12:13], small[:, 11:12], ALU.subtract, r=(small_trk,), w=(small_trk,))
        Sx.ts(DVE, small[:, 8:9], small[:, 8:9], -lambda_init, None, ALU.add, None, r=(small_trk,), w=(small_trk,))
        Sx.ts(DVE, small[:, 7:8], small[:, 7:8], 1.0 - lambda_init, None, ALU.mult, None, r=(small_trk,), w=(small_trk,))

        (sid,) = alloc(1)
        wsf = slots[sid][:].bitcast(F32).rearrange("p (g s) -> p g s", g=8)
        Sx.dma(SP, wsf[:, 0:6, :], a_w_s[0].rearrange("g t s -> t g s"), r=(), w=(slot_trk[sid],))
        for g in range(6):
            pb = g % 2
            Sx.transpose(ps[pb][:, 0:128], wsf[:, g, :], ident[:], r=(slot_trk[sid], const_trk), w=(ps_trk[pb],))
            Sx.tt(DVE, wsT[:, g, :], ps[pb][:, 0:128], mask_bf[:], ALU.mult, r=(const_trk,), w=(ps_trk[pb], const_trk))
        release([sid])

        eps_col = small[:, 0:1]
        Sx.op(DVE, lambda h: h.memset(qT[:], 0.0), r=(), w=tuple(q_trk))

        def rstd_from_psum(pbank, ncols, scale, nparts=128):
            ft, ftt = ft_rot.next()
            Sx.actf(ft[0:nparts, 0:ncols], ps[pbank][0:nparts, 0:ncols], AF.Ln, r=(small_trk,), w=(ps_trk[pbank], ftt), scale=scale, bias=eps_col[0:nparts, :])
            Sx.actf(ft[0:nparts, 0:ncols], ft[0:nparts, 0:ncols], AF.Exp, r=(ftt,), w=(ftt,), scale=-0.5)
            return ft, ftt

        def norm_tile(tt, gcol0):
            t0 = tt * TW
            pb = 7
            for kc in range(KC):
                pt, ptt = pt_rot.next()
                Sx.actf(pt[:], xT[:, kc, t0:t0 + TW], AF.Square, r=(x_trk[kc][tt],), w=(ptt,))
                Sx.mm(ps[pb][:], ones_bf[:], pt[:], kc == 0, kc == KC - 1, r=(ptt, const_trk), w=(ps_trk[pb],))
            ft, ftt = rstd_from_psum(pb, TW, 1.0 / D)
            for kc in range(KC):
                Sx.stt(hT[:, kc, t0:t0 + TW], xT[:, kc, t0:t0 + TW], gcols[:, gcol0 + kc:gcol0 + kc + 1], ft[:], ALU.mult, ALU.mult,
                       r=(x_trk[kc][tt], ftt, const_trk), w=(h_trk[kc][tt],))

        def head_norm(pbank_raw, pbank_stat, ncols, gcol, outs):
            pt, ptt = pt_rot.next()
            Sx.actf(pt[:, 0:ncols], ps[pbank_raw][:, 0:ncols], AF.Square, r=(), w=(ps_trk[pbank_raw], ptt))
            Sx.mm(ps[pbank_stat][:, 0:ncols], blk_bf[:], pt[:, 0:ncols], True, True, r=(ptt, const_trk), w=(ps_trk[pbank_stat],))
            ft, ftt = rstd_from_psum(pbank_stat, ncols, 1.0 / 64)
            for (p0, p1, out_ap, out_trk) in outs:
                Sx.stt(out_ap, ps[pbank_raw][p0:p1, 0:ncols], gcol[p0:p1, :], ft[p0:p1, 0:ncols], ALU.mult, ALU.mult,
                       r=(ftt, small_trk), w=(ps_trk[pbank_raw], out_trk))

        def head_norm_multi(items, ncols):
            sq = []
            for k, (pr, gcol, outs) in enumerate(items):
                pt, ptt = pt_rot.next()
                Sx.actf(pt[:, 0:ncols], ps[pr][:, 0:ncols], AF.Square, r=(), w=(ps_trk[pr], ptt))
                sq.append((pt, ptt))
            for k, (pr, gcol, outs) in enumerate(items):
                pt, ptt = sq[k]
                Sx.mm(ps[6 + k][:, 0:ncols], blk_bf[:], pt[:, 0:ncols], True, True, r=(ptt, const_trk), w=(ps_trk[6 + k],))
            fts = []
            for k in range(len(items)):
                ft, ftt = ft_rot.next()
                Sx.actf(ft[:, 0:ncols], ps[6 + k][:, 0:ncols], AF.Ln, r=(small_trk,), w=(ps_trk[6 + k], ftt), scale=1.0 / 64, bias=eps_col)
                fts.append((ft, ftt))
            for k in range(len(items)):
                ft, ftt = fts[k]
                Sx.actf(ft[:, 0:ncols], ft[:, 0:ncols], AF.Exp, r=(ftt,), w=(ftt,), scale=-0.5)
            for k, (pr, gcol, outs) in enumerate(items):
                ft, ftt = fts[k]
                for (p0, p1, out_ap, out_trk) in outs:
                    Sx.stt(out_ap, ps[pr][p0:p1, 0:ncols], gcol[p0:p1, :], ft[p0:p1, 0:ncols], ALU.mult, ALU.mult,
                           r=(ftt, small_trk), w=(ps_trk[pr], out_trk))

        def qz_chunk(zc):
            if zc < 4:
                return Abuf[0][:, zc, :], A_trk[0]
            if zc < 8:
                return Abuf[1][:, zc - 4, :], A_trk[1]
            return qT[:, zc - 8, :], q_trk[zc - 8]

        def load_x(b):
            for tt in range(TT):
                sids = alloc(4)
                for n in range(4):
                    tc_ = tt * 4 + n
                    Sx.dma(SP, slots[sids[n]][:].bitcast(F32), x_d[b, tc_ * 128:(tc_ + 1) * 128, :], r=(), w=(slot_trk[sids[n]],))
                for kc in range(KC):
                    pb = kc % 4
                    for n in range(4):
                        src = slots[sids[n]][:].bitcast(F32)
                        Sx.transpose(ps[pb][:, n * 128:(n + 1) * 128], src[:, kc * 128:(kc + 1) * 128], ident[:],
                                     r=(slot_trk[sids[n]], const_trk), w=(ps_trk[pb],))
                    eng = ACT if kc % 2 == 0 else DVE
                    Sx.copy(eng, xT[:, kc, tt * TW:(tt + 1) * TW], ps[pb][:], r=(), w=(ps_trk[pb], x_trk[kc][tt]))
                release(sids)

        def store_x(b):
            for tc_ in range(16):
                tt = tc_ // 4
                (sid,) = alloc(1)
                dst = slots[sid][:].bitcast(F32)
                for half in range(2):
                    pb = 4 + (tc_ * 2 + half) % 4
                    for q in range(4):
                        kc = half * 4 + q
                        Sx.transpose(ps[pb][:, q * 128:(q + 1) * 128], xT[:, kc, tc_ * 128:(tc_ + 1) * 128], ident[:],
                                     r=(x_trk[kc][tt], const_trk), w=(ps_trk[pb],))
                    eng = ACT if half == 0 else DVE
                    Sx.copy(eng, dst[:, half * 512:(half + 1) * 512], ps[pb][:], r=(), w=(ps_trk[pb], slot_trk[sid]))
                Sx.dma(SP, y_d[b, tc_ * 128:(tc_ + 1) * 128, :], dst, r=(slot_trk[sid],), w=())
                release([sid])

        def reg_ffn(b, i, j):
            win = ffn_w_in[i, j].rearrange("(kc p) f -> p kc f", p=128)
            wout = ffn_w_out[i, j]
            ngrp = (FC + GROUP - 1) // GROUP
            for g in range(ngrp):
                f0 = g * GROUP
                nf = min(GROUP, FC - f0)
                nsl = (nf + 1) // 2

                def emit(sids, f0=f0, nf=nf, nsl=nsl):
                    for s_ in range(nsl):
                        c0 = (f0 + 2 * s_) * 128
                        w_ = min(256, (f0 + nf) * 128 - c0)
                        wdma(sids[s_], slot3(sids[s_], 8, 256)[:, :, 0:w_], win[:, :, c0:c0 + w_])
                    for s_ in range(nsl):
                        c0 = DFF + (f0 + 2 * s_) * 128
                        w_ = min(256, DFF + (f0 + nf) * 128 - c0)
                        wdma(sids[nsl + s_], slot3(sids[nsl + s_], 8, 256)[:, :, 0:w_], win[:, :, c0:c0 + w_])
                    for s_ in range(nsl):
                        r0 = (f0 + 2 * s_) * 128
                        nr = min(2, f0 + nf - (f0 + 2 * s_))
                        wdma(sids[2 * nsl + s_], slot3(sids[2 * nsl + s_], 2, 1024)[:, 0:nr, :],
                             wout[r0:r0 + nr * 128, :].rearrange("(fc p) d -> p fc d", p=128))
                wreg(("ffn", b, i, j, g), 3 * nsl, emit)

        def reg_slabs(key, w2d, col0, ncol):
            wv = w2d.rearrange("(kc p) f -> p kc f", p=128)
            for s_ in range(ncol // 256):
                def emit(sids, s_=s_):
                    c0 = col0 + s_ * 256
                    wdma(sids[0], slot3(sids[0], 8, 256), wv[:, :, c0:c0 + 256])
                wreg(key + (s_,), 1, emit)

        def reg_wout(key, w2d):
            for s_ in range(4):
                def emit(sids, s_=s_):
                    wdma(sids[0], slot3(sids[0], 2, 1024), w2d[s_ * 256:(s_ + 1) * 256, :].rearrange("(c p) d -> p c d", p=128))
                wreg(key + (s_,), 1, emit)

        ycnt = [0]
        gucnt = [0]

        def ffn(b, i, j):
            for tt in range(TT):
                norm_tile(tt, (i * 2 + j) * 8)
            ngrp = (FC + GROUP - 1) // GROUP
            pending = [None]
            acnt = [0]
            for g in range(ngrp):
                f0 = g * GROUP
                nf = min(GROUP, FC - f0)
                nsl = (nf + 1) // 2
                sids = want(("ffn", b, i, j, g))
                pump(reserve=4)
                gate_s, up_s, out_s = sids[0:nsl], sids[nsl:2 * nsl], sids[2 * nsl:3 * nsl]
                for tt in range(TT):
                    t0 = tt * TW
                    ab = acnt[0] % 2
                    acnt[0] += 1
                    A = Abuf[ab]
                    for q in range(nf):
                        pg = (gucnt[0] % 2) * 2
                        pu = pg + 1
                        gucnt[0] += 1
                        gs = slot3(gate_s[q // 2], 8, 256)
                        us = slot3(up_s[q // 2], 8, 256)
                        co = (q % 2) * 128
                        for kc in range(KC):
                            Sx.mm(ps[pg][:], gs[:, kc, co:co + 128], hT[:, kc, t0:t0 + TW], kc == 0, kc == KC - 1,
                                  r=(slot_trk[gate_s[q // 2]], h_trk[kc][tt]), w=(ps_trk[pg],))
                        for kc in range(KC):
                            Sx.mm(ps[pu][:], us[:, kc, co:co + 128], hT[:, kc, t0:t0 + TW], kc == 0, kc == KC - 1,
                                  r=(slot_trk[up_s[q // 2]], h_trk[kc][tt]), w=(ps_trk[pu],))
                        ft, ftt = ft_rot.next()
                        Sx.actf(ft[:], ps[pg][:], AF.Silu, r=(), w=(ps_trk[pg], ftt))
                        Sx.tt(DVE, A[:, q, :], ft[:], ps[pu][:], ALU.mult, r=(ftt,), w=(ps_trk[pu], A_trk[ab]))
                    if pending[0] is not None:
                        pending[0]()

                    def ywork(tt=tt, t0=t0, A=A, ab=ab, nf=nf, out_s=out_s):
                        for dc in range(KC):
                            py = 4 + ycnt[0] % 3
                            ycnt[0] += 1
                            for q in range(nf):
                                ws_ = slot3(out_s[q // 2], 2, 1024)
                                Sx.mm(ps[py][:], ws_[:, q % 2, dc * 128:(dc + 1) * 128], A[:, q, :], q == 0, q == nf - 1,
                                      r=(slot_trk[out_s[q // 2]], A_trk[ab]), w=(ps_trk[py],))
                            Sx.stt(xT[:, dc, t0:t0 + TW], ps[py][:], 0.5, xT[:, dc, t0:t0 + TW], ALU.mult, ALU.add,
                                   r=(), w=(ps_trk[py], x_trk[dc][tt]))
                    pending[0] = ywork
                if g == ngrp - 1:
                    pending[0]()
                    pending[0] = None
                    release(sids)
                else:
                    pending[0]()
                    pending[0] = None
                    release(sids)

        def mem_kv(b, i):
            sm = alloc(2)
            (sh,) = alloc(1)
            mh = slot3(sh, 8, 256)
            for lc in range(2):
                mf = slots[sm[lc]][:].bitcast(F32)
                Sx.dma(SP, mf, mem_d[b, lc * 128:(lc + 1) * 128, :], r=(), w=(slot_trk[sm[lc]],))
                ft, ftt = ft_rot.next()
                for hf in range(2):
                    Sx.tt(DVE, ft[:], mf[:, hf * 512:(hf + 1) * 512], mf[:, hf * 512:(hf + 1) * 512], ALU.mult, r=(slot_trk[sm[lc]],), w=(ftt,))
                    Sx.op(DVE, lambda h, ft=ft, hf=hf: h.tensor_reduce(out=small[:, 16 + hf:17 + hf], in_=ft[:], op=ALU.add, axis=AX.X), r=(ftt,), w=(small_trk,))
                Sx.tt(DVE, small[:, 18:19], small[:, 16:17], small[:, 17:18], ALU.add, r=(small_trk,), w=(small_trk,))
                Sx.actf(small[:, 19:20], small[:, 18:19], AF.Ln, r=(small_trk,), w=(small_trk,), scale=1.0 / D, bias=eps_col)
                Sx.actf(small[:, 20:21], small[:, 19:20], AF.Exp, r=(small_trk,), w=(small_trk,), scale=-0.5)
                Sx.ts(DVE, mf, mf, small[:, 20:21], None, ALU.mult, None, r=(small_trk, slot_trk[sm[lc]]), w=(slot_trk[sm[lc]],))
            for kc in range(KC):
                pb = kc % 2
                for lc in range(2):
                    mf = slots[sm[lc]][:].bitcast(F32)
                    Sx.transpose(ps[pb][:, lc * 128:(lc + 1) * 128], mf[:, kc * 128:(kc + 1) * 128], ident[:],
                                 r=(slot_trk[sm[lc]], const_trk), w=(ps_trk[pb],))
                Sx.ts(DVE, mh[:, kc, :], ps[pb][:, 0:256], gcols[:, 48 + 8 * i + kc:48 + 8 * i + kc + 1], None, ALU.mult, None,
                      r=(const_trk,), w=(ps_trk[pb], slot_trk[sh]))
            release(sm)
            ks = want(("memkv", b, i, 0))[0]
            vs = want(("memkv", b, i, 1))[0]
            kw = slot3(ks, 8, 256)
            vw = slot3(vs, 8, 256)
            for hc in range(2):
                pr = 2 + hc
                for kc in range(KC):
                    Sx.mm(ps[pr][:, 0:256], kw[:, kc, hc * 128:(hc + 1) * 128], mh[:, kc, :], kc == 0, kc == KC - 1,
                          r=(slot_trk[ks], slot_trk[sh]), w=(ps_trk[pr],))
                head_norm(pr, 7, 256, small[:, 3 + i:4 + i], [(0, 128, kmT[:, hc, :], km_trk)])
            for lc in range(2):
                pr = 4 + lc
                for kc in range(KC):
                    Sx.mm(ps[pr][:, 0:256], mh[:, kc, lc * 128:(lc + 1) * 128], vw[:, kc, :], kc == 0, kc == KC - 1,
                          r=(slot_trk[vs], slot_trk[sh]), w=(ps_trk[pr],))
                Sx.copy(ACT, vm[:, lc, :], ps[pr][:, 0:256], r=(), w=(ps_trk[pr], vm_trk))
            release([ks, vs, sh])

        def mem_attn(tt):
            for hc in range(2):
                for hh in range(2):
                    hm = hc * 2 + hh
                    pnum, pden = 2 * hh, 2 * hh + 1
                    for lc in range(2):
                        pss = 4 + (hm * 2 + lc) % 3
                        Sx.mm(ps[pss][:], kmT[:, hc, lc * 128:(lc + 1) * 128], qT[:, 4 + hm, :], True, True,
                              r=(km_trk, q_trk[4 + hm]), w=(ps_trk[pss],))
                        pt, ptt = pt_rot.next()
                        Sx.actf(pt[:], ps[pss][:], AF.Exp, r=(), w=(ps_trk[pss], ptt), scale=0.125)
                        Sx.mm(ps[pnum][:], vm[:, lc, hc * 128:(hc + 1) * 128], pt[:], lc == 0, lc == 1,
                              r=(vm_trk, ptt), w=(ps_trk[pnum],))
                        Sx.mm(ps[pden][:], ones_bf[:], pt[:], lc == 0, lc == 1,
                              r=(const_trk, ptt), w=(ps_trk[pden],))
                for hh in range(2):
                    pnum, pden = 2 * hh, 2 * hh + 1
                    r0 = hh * 64
                    ft, ftt = ft_rot.next()
                    Sx.actf(ft[r0:r0 + 64, :], ps[pden][r0:r0 + 64, :], AF.Ln, r=(), w=(ps_trk[pden], ftt))
                    Sx.actf(ft[r0:r0 + 64, :], ft[r0:r0 + 64, :], AF.Exp, r=(ftt,), w=(ftt,), scale=-1.0)
                    Sx.tt(DVE, tokT[r0:r0 + 64, 6 + hc, :], ps[pnum][r0:r0 + 64, :], ft[r0:r0 + 64, :], ALU.mult, r=(ftt,), w=(ps_trk[pnum], tok_trk[6 + hc]))

        def out_proj(b, i, tt):
            t0 = tt * TW
            ws_ = [want(("wout", b, i, tt, s_))[0] for s_ in range(4)]
            for dc in range(KC):
                py = 4 + ycnt[0] % 3
                ycnt[0] += 1
                for c in range(8):
                    wv = slot3(ws_[c // 2], 2, 1024)
                    Sx.mm(ps[py][:], wv[:, c % 2, dc * 128:(dc + 1) * 128], tokT[:, c, :], c == 0, c == 7,
                          r=(slot_trk[ws_[c // 2]], tok_trk[c]), w=(ps_trk[py],))
                Sx.tt(DVE, xT[:, dc, t0:t0 + TW], ps[py][:], xT[:, dc, t0:t0 + TW], ALU.add, r=(), w=(ps_trk[py], x_trk[dc][tt]))
            release(ws_)

        def qm_proj(tt, slab_sid, memq_col):
            t0 = tt * TW
            wv = slot3(slab_sid, 8, 256)
            items = []
            for hc in range(2):
                pr = 4 + hc
                for kc in range(KC):
                    Sx.mm(ps[pr][:], wv[:, kc, hc * 128:(hc + 1) * 128], hT[:, kc, t0:t0 + TW], kc == 0, kc == KC - 1,
                          r=(slot_trk[slab_sid], h_trk[kc][tt]), w=(ps_trk[pr],))
                items.append((pr, memq_col, [(0, 64, qT[0:64, 4 + 2 * hc, :], q_trk[4 + 2 * hc]), (64, 128, qT[64:128, 5 + 2 * hc, :], q_trk[5 + 2 * hc])]))
            head_norm_multi(items, TW)

        def mixer_a(b):
            i = 0
            base_items = ft_rot.items
            for ab in range(2):
                Sx.op(DVE, lambda h, ab=ab: h.memset(Abuf[ab][:], 0.0), r=(), w=(A_trk[ab], at_trk[2 * ab], at_trk[2 * ab + 1]))
            ft_rot.items = base_items + [(at_tiles[k], at_trk[k]) for k in range(4)]
            ft_rot.i = 0
            for tt in range(TT):
                norm_tile(tt, 32)
            mem_kv(b, i)
            slabs = [want(("a_in", b, s_))[0] for s_ in range(7)]
            pump(reserve=4)
            for tt in range(TT):
                t0 = tt * TW
                qm_proj(tt, slabs[6], small[:, 1:2])

                def front(g, tt=tt, t0=t0):
                    us = slot3(slabs[g // 2], 8, 256)
                    vs = slot3(slabs[3 + g // 2], 8, 256)
                    co = (g % 2) * 128
                    pu, pv = (g % 2) * 2, (g % 2) * 2 + 1
                    for kc in range(KC):
                        Sx.mm(ps[pu][:], us[:, kc, co:co + 128], hT[:, kc, t0:t0 + TW], kc == 0, kc == KC - 1,
                              r=(slot_trk[slabs[g // 2]], h_trk[kc][tt]), w=(ps_trk[pu],))
                    for n in range(4):
                        for kc in range(KC):
                            Sx.mm(ps[pv][:, n * 128:(n + 1) * 128], hT[:, kc, t0 + n * 128:t0 + (n + 1) * 128], vs[:, kc, co:co + 128],
                                  kc == 0, kc == KC - 1, r=(slot_trk[slabs[3 + g // 2]], h_trk[kc][tt]), w=(ps_trk[pv],))
                    fu, fut = ft_rot.next()
                    Sx.actf(fu[:], ps[pu][:], AF.Gelu, r=(), w=(ps_trk[pu], fut))
                    fv, fvt = ft_rot.next()
                    Sx.actf(fv[:], ps[pv][:], AF.Gelu, r=(), w=(ps_trk[pv], fvt))
                    fs, fst = ft_rot.next()
                    Sx.tt(DVE, fs[:], fv[:], fv[:], ALU.mult, r=(fvt,), w=(fst,))
                    gs_, gst = gst_rot.next()
                    Sx.op(DVE, lambda h, fs=fs, gs_=gs_: h.tensor_reduce(out=gs_[:, 0:4], in_=fs[:].rearrange("p (a b) -> p a b", a=4), op=ALU.add, axis=AX.X),
                          r=(fst,), w=(gst,))
                    Sx.ts(DVE, gs_[:, 0:4], gs_[:, 0:4], 1.0 / 128, EPS, ALU.mult, ALU.add, r=(gst,), w=(gst,))
                    Sx.tt(POOL, gs_[:, 4:8], gs_[:, 0:4], negh[:, 0:4], ALU.pow, r=(gst, const_trk), w=(gst,))
                    return (fu, fut, fv, fvt, fs, fst, gs_, gst)

                def back(g, st):
                    fu, fut, fv, fvt, fs, fst, gs_, gst = st
                    pm = 4 + g % 2
                    pt, ptt = pt_rot.next()
                    for n in range(4):
                        Sx.stt(pt[:, n * 128:(n + 1) * 128], fv[:, n * 128:(n + 1) * 128], gs_[:, 4 + n:5 + n], avn_bc[:, g * 128:(g + 1) * 128],
                               ALU.mult, ALU.mult, r=(fvt, gst, const_trk), w=(ptt,))
                    for n in range(4):
                        Sx.mm(ps[pm][:, n * 128:(n + 1) * 128], pt[:, n * 128:(n + 1) * 128], wsT[:, g, :], True, True,
                              r=(ptt, const_trk), w=(ps_trk[pm],))
                    Sx.tt(DVE, fs[:].rearrange("p (a b) -> p a b", a=4), ps[pm][:].rearrange("p (a b) -> p a b", a=4),
                          abs_bc[:, g * 128:(g + 1) * 128].unsqueeze(1).broadcast_to([128, 4, 128]), ALU.add,
                          r=(const_trk,), w=(ps_trk[pm], fst))
                    Sx.tt(POOL, tokT[:, g, :], fs[:], fu[:], ALU.mult, r=(fst, fut), w=(tok_trk[g],))

                prev = front(0)
                for g in range(1, 6):
                    cur = front(g)
                    back(g - 1, prev)
                    prev = cur
                back(5, prev)
                mem_attn(tt)
                if tt == TT - 1:
                    release(slabs)
                out_proj(b, i, tt)
                pump(reserve=4)
            ft_rot.items = base_items
            ft_rot.i = 0
            for ab in range(2):
                Sx.op(DVE, lambda h, ab=ab: h.memset(Abuf[ab][:], 0.0), r=(), w=(A_trk[ab], at_trk[2 * ab], at_trk[2 * ab + 1]))

        def mixer_b(b):
            i = 1
            for ab in range(2):
                Sx.op(DVE, lambda h, ab=ab: h.memset(Abuf[ab][:], 0.0), r=(), w=(A_trk[ab],))
            for tt in range(TT):
                norm_tile(tt, 40)
            mem_kv(b, i)
            kvs = want(("kv", b))
            k_s, v_s = kvs[0:6], kvs[6:12]
            kslabs = [want(("b_k", b, s_))[0] for s_ in range(3)]
            for c in range(6):
                wv = slot3(kslabs[c // 2], 8, 256)
                co = (c % 2) * 128
                for tp in range(TT // 2):
                    items = []
                    for k in range(2):
                        tt = tp * 2 + k
                        t0 = tt * TW
                        pr = ((c * 2 + tp) % 3) * 2 + k
                        for kc in range(KC):
                            Sx.mm(ps[pr][:], wv[:, kc, co:co + 128], hT[:, kc, t0:t0 + TW], kc == 0, kc == KC - 1,
                                  r=(slot_trk[kslabs[c // 2]], h_trk[kc][tt]), w=(ps_trk[pr],))
                        items.append((pr, small[:, 6:7], [(0, 128, slots[k_s[c]][:, t0:t0 + TW], slot_trk[k_s[c]])]))
                    head_norm_multi(items, TW)
            release(kslabs)
            vslabs = [want(("b_v", b, s_))[0] for s_ in range(3)]
            for tc_ in range(16):
                tt = tc_ // 4
                for s_ in range(3):
                    wv = slot3(vslabs[s_], 8, 256)
                    pb = 4 + (tc_ * 3 + s_) % 3
                    for kc in range(KC):
                        Sx.mm(ps[pb][:, 0:256], hT[:, kc, tc_ * 128:(tc_ + 1) * 128], wv[:, kc, :], kc == 0, kc == KC - 1,
                              r=(slot_trk[vslabs[s_]], h_trk[kc][tt]), w=(ps_trk[pb],))
                    for hh in range(2):
                        h_ = s_ * 2 + hh
                        vdst = slot3(v_s[h_], 16, 128)
                        Sx.copy(ACT if hh == 0 else DVE, vdst[:, tc_, :], ps[pb][:, hh * 128:(hh + 1) * 128], r=(), w=(ps_trk[pb], slot_trk[v_s[h_]]))
            release(vslabs)
            for tt in range(TT):
                t0 = tt * TW
                qs = [want(("b_q", b, tt, s_))[0] for s_ in range(3)]
                for cp in range(3):
                    items = []
                    for k in range(2):
                        c = cp * 2 + k
                        wv = slot3(qs[c // 2], 8, 256)
                        co = (c % 2) * 128
                        pr = cp * 2 + k
                        for kc in range(KC):
                            Sx.mm(ps[pr][:], wv[:, kc, co:co + 128], hT[:, kc, t0:t0 + TW], kc == 0, kc == KC - 1,
                                  r=(slot_trk[qs[c // 2]], h_trk[kc][tt]), w=(ps_trk[pr],))
                        za, zat = qz_chunk(2 * c)
                        zb, zbt = qz_chunk(2 * c + 1)
                        items.append((pr, small[:, 5:6], [(0, 64, za[0:64, :], zat), (64, 128, zb[64:128, :], zbt)]))
                    head_norm_multi(items, TW)
                release(qs)
                qms = want(("b_qm", b, tt, 0))[0]
                qm_proj(tt, qms, small[:, 2:3])
                release([qms])
                pump(reserve=4)
                nk = (tt + 1) * 4
                deferred = []

                def tick():
                    for d_ in deferred:
                        d_[0] -= 1
                    while deferred and deferred[0][0] <= 0:
                        deferred.pop(0)[1]()

                for h_ in range(6):
                    vsl = slot3(v_s[h_], 16, 128)
                    steps = []
                    for c in range(2):
                        for kc in range(nk):
                            steps.append((c, kc))
                    scnt = [0]
                    inflight = []
                    hst = {}

                    def stageA(c, kc, h_=h_):
                        j = kc - tt * 4
                        c0 = max(j, 0) * 128
                        pss = 4 + scnt[0] % 3
                        scnt[0] += 1
                        zq, zqt = qz_chunk(2 * h_ + c)
                        Sx.mm(ps[pss][:, c0:TW], slots[k_s[h_]][:, kc * 128:(kc + 1) * 128], zq[:, c0:TW],
                              True, True, r=(slot_trk[k_s[h_]], zqt), w=(ps_trk[pss],))
                        return (c, kc, j, c0, pss)

                    def evac(c, hst=hst, h_=h_):
                        f_, ft_ = ft_rot.next()
                        pn, pd = c * 2, c * 2 + 1
                        Sx.actf(f_[:], ps[pd][:], AF.Ln, r=(), w=(ps_trk[pd], ft_))
                        Sx.actf(f_[:], f_[:], AF.Exp, r=(ft_,), w=(ft_,), scale=-1.0)
                        Sx.tt(DVE, f_[:], ps[pn][:], f_[:], ALU.mult, r=(ft_,), w=(ps_trk[pn], ft_))
                        hst[c] = (f_, ft_)
                        if c == 1:
                            fa, fat = hst[0]
                            fb, fbt = hst[1]
                            Sx.stt(fa[:], fb[:], small[:, 8:9], fa[:], ALU.mult, ALU.add, r=(fbt, fat, small_trk), w=(fat,))
                            pt, ptt = pt_rot.next()
                            Sx.tt(DVE, pt[:], fa[:], fa[:], ALU.mult, r=(fat,), w=(ptt,))
                            Sx.mm(ps[7][:], ones_bf[:], pt[:], True, True, r=(ptt, const_trk), w=(ps_trk[7],))

                            def finalize(fa=fa, fat=fat, h_=h_):
                                fr, frt = rstd_from_psum(7, TW, 1.0 / 128)
                                Sx.stt(tokT[:, h_, :], fa[:], small[:, 7:8], fr[:], ALU.mult, ALU.mult, r=(fat, frt, small_trk), w=(tok_trk[h_],))
                            deferred.append([2, finalize])

                    def stageB(st, vsl=vsl, h_=h_, evac=evac):
                        c, kc, j, c0, pss = st
                        pt, ptt = pt_rot.next()
                        Sx.actf(pt[:, c0:TW], ps[pss][:, c0:TW], AF.Exp, r=(), w=(ps_trk[pss], ptt), scale=0.125)
                        if j >= 0:
                            Sx.tt(DVE, pt[:, c0:c0 + 128], pt[:, c0:c0 + 128], mask_bf[:], ALU.mult, r=(ptt, const_trk), w=(ptt,))
                        pn, pd = c * 2, c * 2 + 1
                        Sx.mm(ps[pn][:, c0:TW], vsl[:, kc, :], pt[:, c0:TW], kc == 0, kc == nk - 1,
                              r=(slot_trk[v_s[h_]], ptt), w=(ps_trk[pn],))
                        Sx.mm(ps[pd][:, c0:TW], ones_bf[:], pt[:, c0:TW], kc == 0, kc == nk - 1,
                              r=(const_trk, ptt), w=(ps_trk[pd],))
                        tick()
                        if kc == nk - 1:
                            deferred.append([2, lambda c=c, evac=evac: evac(c)])

                    for si, (c, kc) in enumerate(steps):
                        inflight.append(stageA(c, kc))
                        if len(inflight) > 2:
                            stageB(inflight.pop(0))
                    while inflight:
                        stageB(inflight.pop(0))
                while deferred:
                    deferred.pop(0)[1]()
                mem_attn(tt)
                out_proj(b, i, tt)
            release(kvs)

        for b in range(nb):
            reg_ffn(b, 0, 0)
            reg_slabs(("memkv", b, 0), w_mem_kv[0], 0, 512)
            reg_slabs(("a_in", b), a_w_in[0], 0, 1792)
            for tt in range(TT):
                reg_wout(("wout", b, 0, tt), w_out[0])
            reg_ffn(b, 0, 1)
            reg_ffn(b, 1, 0)
            reg_slabs(("memkv", b, 1), w_mem_kv[1], 0, 512)
            wreg(("kv", b), 12, lambda sids: None)
            reg_slabs(("b_k", b), b_w_in[0], 768, 768)
            reg_slabs(("b_v", b), b_w_in[0], 1536, 768)
            for tt in range(TT):
                reg_slabs(("b_q", b, tt), b_w_in[0], 0, 768)
                reg_slabs(("b_qm", b, tt), b_w_in[0], 2304, 256)
                reg_wout(("wout", b, 1, tt), w_out[1])
            reg_ffn(b, 1, 1)

        for b in range(nb):
            load_x(b)
            st = 0
            for i in range(2):
                for sub in range(3):
                    if st >= stages:
                        break
                    if sub == 0:
                        ffn(b, i, 0)
                    elif sub == 1:
                        (mixer_a if i == 0 else mixer_b)(b)
                    else:
                        ffn(b, i, 1)
                    st += 1
            store_x(b)
        Sx.wait_all_dma(SP)

        block = es.enter_context(nc.Block())

        @block.tensor
        def _(h):
            for f in PE.prog:
                f(h)

        @block.scalar
        def _(h):
            for f in ACT.prog:
                f(h)

        @block.vector
        def _(h):
            for f in DVE.prog:
                f(h)

        @block.gpsimd
        def _(h):
            for f in POOL.prog:
                f(h)

        @block.sync
        def _(h):
            for f in SP.prog:
                f(h)
    return nc


_NC_CACHE = {}


def kernel(**inputs):
    x = np.ascontiguousarray(inputs["x"], dtype=np.float32)
    mem = np.ascontiguousarray(inputs["mem"], dtype=np.float32)
    if "nc" not in _NC_CACHE:
        _NC_CACHE["nc"] = build_nc()
    nc = _NC_CACHE["nc"]
    shared = {k: np.ascontiguousarray(v, dtype=np.float32) for k, v in inputs.items() if k not in ("x", "mem")}
    in_maps = []
    for c in range(N_CORES):
        m = dict(shared)
        m["x"] = x[c * NB_CORE:(c + 1) * NB_CORE]
        m["mem"] = mem[c * NB_CORE:(c + 1) * NB_CORE]
        in_maps.append(m)
    res = run_bass_kernel_spmd(nc, in_maps, core_ids=list(range(N_CORES)))
    return np.concatenate([r["y"] for r in res.results], axis=0)
```
